# Optimizing a Trainium2 kernel written in Bass

```python
import jax, jax.numpy as jnp
from jax import lax
import numpy as np

D_MODEL = 1024
BATCH = 8
SEQ = 4096
DEPTH = 1

CHUNK = 64
MIX_WIDTH = D_MODEL
ATTN_WIDTH = MIX_WIDTH // 2
N_HEADS = 8
HEAD_DIM = ATTN_WIDTH // N_HEADS
POOL_WIDTH = MIX_WIDTH - ATTN_WIDTH
POOL_WINDOWS = (2, 4, 8, 16)
N_POOL_GROUPS = len(POOL_WINDOWS)
POOL_GROUP = POOL_WIDTH // N_POOL_GROUPS
IN_WIDTH = 3 * ATTN_WIDTH + N_HEADS + POOL_WIDTH
D_FF = -(-(8 * D_MODEL) // (3 * 256)) * 256
Q_BLOCK = 128
EPS = 1e-6

kernel_name = "fox_pool_hybrid_block"


def rmsnorm(x, g):
    xf = x.astype(jnp.float32)
    y = xf * lax.rsqrt(jnp.mean(xf * xf, axis=-1, keepdims=True) + EPS)
    return (y * g.astype(jnp.float32)).astype(x.dtype)


def forgetting_attention(q, k, v, log_f):
    S, Dh = q.shape[2], q.shape[3]
    c = jnp.cumsum(log_f, axis=-1)
    qf = q.astype(jnp.float32) * (Dh ** -0.5)
    kf = k.astype(jnp.float32)
    vf = v.astype(jnp.float32)
    outs = []
    for i in range(S // Q_BLOCK):
        q0, q1 = i * Q_BLOCK, (i + 1) * Q_BLOCK
        logits = jnp.einsum('bhqd,bhkd->bhqk', qf[:, :, q0:q1], kf[:, :, :q1])
        logits = logits + c[:, :, q0:q1, None] - c[:, :, None, :q1]
        t_pos = jnp.arange(q0, q1)[:, None]
        s_pos = jnp.arange(q1)[None, :]
        logits = jnp.where(s_pos <= t_pos, logits, -jnp.inf)
        p = jax.nn.softmax(logits, axis=-1)
        outs.append(jnp.einsum('bhqk,bhkd->bhqd', p, vf[:, :, :q1]))
    return jnp.concatenate(outs, axis=2).astype(q.dtype)


def multiscale_pool(u, w_pool, pool_scale):
    B, S, C = u.shape
    uf = u.astype(jnp.float32)
    cs = jnp.concatenate([jnp.zeros((B, 1, C), jnp.float32), jnp.cumsum(uf, axis=1)], axis=1)
    t = jnp.arange(S)
    groups = []
    for g, w in enumerate(POOL_WINDOWS):
        lo, hi = g * POOL_GROUP, (g + 1) * POOL_GROUP
        start = jnp.maximum(t + 1 - w, 0)
        csg = cs[:, :, lo:hi]
        window_sum = csg[:, 1:] - csg[:, start]
        count = (t + 1 - start).astype(jnp.float32)[None, :, None]
        groups.append(window_sum / count - uf[:, :, lo:hi])
    pooled = jnp.stack(groups, axis=2)
    mixed = jnp.einsum('bsgc,gcd->bsgd', pooled, w_pool.astype(jnp.float32)).reshape(B, S, C)
    return (mixed * pool_scale.astype(jnp.float32)).astype(u.dtype)


def setup_inputs(seed: int = 0) -> dict:
    key = jax.random.key(seed)
    ks = jax.random.split(key, 13)
    f32 = jnp.float32
    nrm = lambda k, shape, scale: jax.random.normal(k, shape, f32) * scale
    return {
        "x": jax.random.normal(ks[0], (BATCH, SEQ, D_MODEL), f32),
        "norm1_g": 1.0 + nrm(ks[1], (DEPTH, D_MODEL), 0.05),
        "w_in": nrm(ks[2], (DEPTH, D_MODEL, IN_WIDTH), D_MODEL ** -0.5),
        "b_forget": 2.0 + nrm(ks[3], (DEPTH, N_HEADS), 0.5),
        "w_pool": nrm(ks[4], (DEPTH, N_POOL_GROUPS, POOL_GROUP, POOL_GROUP), POOL_GROUP ** -0.5),
        "pool_scale": 1.0 + nrm(ks[5], (DEPTH, POOL_WIDTH), 0.1),
        "w_out": nrm(ks[6], (DEPTH, MIX_WIDTH, D_MODEL), MIX_WIDTH ** -0.5),
        "norm2_g": 1.0 + nrm(ks[7], (DEPTH, D_MODEL), 0.05),
        "w_gate": nrm(ks[8], (DEPTH, D_MODEL, D_FF), D_MODEL ** -0.5),
        "w_up": nrm(ks[9], (DEPTH, D_MODEL, D_FF), D_MODEL ** -0.5),
        "w_down": nrm(ks[10], (DEPTH, D_FF, D_MODEL), D_FF ** -0.5),
        "final_g": 1.0 + nrm(ks[11], (D_MODEL,), 0.05),
    }


def reference(x, norm1_g, w_in, b_forget, w_pool, pool_scale, w_out, norm2_g, w_gate, w_up, w_down, final_g):
    B, S, _ = x.shape
    for layer in range(DEPTH):
        h = rmsnorm(x, norm1_g[layer])
        proj = jnp.einsum('bsd,de->bse', h, w_in[layer])
        a0 = ATTN_WIDTH
        q = proj[..., 0:a0]
        k = proj[..., a0:2 * a0]
        v = proj[..., 2 * a0:3 * a0]
        f_logit = proj[..., 3 * a0:3 * a0 + N_HEADS]
        u = proj[..., 3 * a0 + N_HEADS:]
        to_heads = lambda t: t.reshape(B, S, N_HEADS, HEAD_DIM).transpose(0, 2, 1, 3)
        log_f = jax.nn.log_sigmoid(f_logit.astype(jnp.float32) + b_forget[layer].astype(jnp.float32))
        log_f = log_f.transpose(0, 2, 1)
        attn = forgetting_attention(to_heads(q), to_heads(k), to_heads(v), log_f)
        attn = attn.transpose(0, 2, 1, 3).reshape(B, S, ATTN_WIDTH)
        pool = multiscale_pool(u, w_pool[layer], pool_scale[layer])
        mixed = jnp.concatenate([attn, pool], axis=-1)
        x = x + jnp.einsum('bse,ed->bsd', mixed, w_out[layer])
        h2 = rmsnorm(x, norm2_g[layer])
        gate = jnp.einsum('bsd,df->bsf', h2, w_gate[layer])
        up = jnp.einsum('bsd,df->bsf', h2, w_up[layer])
        x = x + jnp.einsum('bsf,fd->bsd', jax.nn.silu(gate) * up, w_down[layer])
    return rmsnorm(x, final_g)
```

```python
from contextlib import ExitStack

import numpy as np
import concourse.bass as bass
import concourse.mybir as mybir
from concourse.bass_utils import run_bass_kernel_spmd

F32 = mybir.dt.float32
BF16 = mybir.dt.bfloat16
AF = mybir.ActivationFunctionType
ALU = mybir.AluOpType

S = 4096
D = 1024
TT = 512
NT = S // TT
KC = D // 128
H = 8
DFF = 2816
FC = DFF // 128
INW = 2056
EPS = 1e-6
NEG = -30000.0

CF_G1, CF_G2, CF_GF, CF_PS, CF_BF, CF_IC, CF_TRI = 0, 8, 16, 24, 28, 60, 124
CF_E64 = 124 + 128
NCF = 124 + 128 + 128
CB_ID, CB_MASK, CB_SEL = 0, 128, 256
NCB = 256 + 1024


class Buf:
    __slots__ = ("name", "w", "r")

    def __init__(self, name):
        self.name = name
        self.w = None
        self.r = []


class Sched:
    ENGS = ("pe", "act", "dve", "pool", "sp")

    def __init__(self):
        self.prog = {e: [] for e in self.ENGS}
        self.cnt = {}
        self.waited = {e: {} for e in self.ENGS}

    def _deps(self, eng, reads, writes):
        need = {}

        def add(t):
            if t is not None and need.get(t[0], 0) < t[1]:
                need[t[0]] = t[1]

        for b in reads:
            add(b.w)
        for b in writes:
            add(b.w)
            for t in b.r:
                add(t)
        out = []
        for k, v in need.items():
            if k == eng and eng == "pe":
                continue
            if self.waited[eng].get(k, 0) >= v:
                continue
            self.waited[eng][k] = v
            out.append((k, v))
        return out

    def _commit(self, tok, reads, writes):
        for b in reads:
            b.r.append(tok)
        for b in writes:
            b.w = tok
            b.r = []

    def op(self, eng, fn, reads=(), writes=()):
        waits = self._deps(eng, reads, writes)
        self.cnt[eng] = self.cnt.get(eng, 0) + 1
        tok = (eng, self.cnt[eng])
        self.prog[eng].append((waits, fn, (eng, 1)))
        self._commit(tok, reads, writes)
        return tok

    def dma(self, queue, semkey, fn, reads=(), writes=(), track=True):
        waits = self._deps(queue, reads, writes) if track else []
        self.cnt[semkey] = self.cnt.get(semkey, 0) + 16
        tok = (semkey, self.cnt[semkey])
        self.prog[queue].append((waits, fn, (semkey, 16)))
        if track:
            self._commit(tok, reads, writes)
        return tok

    def barrier(self):
        for e in self.ENGS:
            waits = []
            for k, v in self.cnt.items():
                if k == e and e == "pe":
                    continue
                if self.waited[e].get(k, 0) >= v:
                    continue
                self.waited[e][k] = v
                waits.append((k, v))
            if waits:
                self.prog[e].append((waits, None, None))

    def replay(self, eng, e, semh):
        for waits, fn, sig in self.prog[eng]:
            for k, v in waits:
                e.wait_ge(semh[k], v)
            if fn is None:
                continue
            ins = fn(e)
            ins.then_inc(semh[sig[0]], sig[1])


class Arena:
    def __init__(self, ap):
        self.ap = ap
        self.off = 0
        self.words = ap.shape[1]

    def alloc(self, n, dtype):
        nbytes = n * (2 if dtype == BF16 else 4)
        words = (nbytes + 31) // 32 * 8
        assert self.off + words <= self.words, ("arena overflow", self.off, words, self.words)
        v = self.ap[:, self.off:self.off + words]
        self.off += words
        if dtype == BF16:
            return v.bitcast(BF16)[:, 0:n]
        return v[:, 0:n]


def build_program(debug=False):
    nc = bass.Bass("TRN2", target_bir_lowering=False)
    dt_in = lambda name, shape: nc.dram_tensor(name, shape, F32, kind="ExternalInput").ap()
    xT = dt_in("xT", [D, S])
    w_in = dt_in("w_in", [D, INW])
    w_out = dt_in("w_out", [D, D])
    w_pool = dt_in("w_pool", [4, 128, 128])
    w_gate = dt_in("w_gate", [D, DFF])
    w_up = dt_in("w_up", [D, DFF])
    w_down = dt_in("w_down", [DFF, D])
    cf_d = dt_in("cf", [128, NCF])
    cb_d = dt_in("cb", [128, NCB])
    x1s = nc.dram_tensor("x1s", [D, S], F32, kind=("ExternalOutput" if debug else "Internal")).ap()
    yT = nc.dram_tensor("yT", [D, S], F32, kind="ExternalOutput").ap()

    xT_v = xT.rearrange("(c p) t -> p c t", p=128)
    x1_v = x1s.rearrange("(c p) t -> p c t", p=128)
    yT_v = yT.rearrange("(c p) t -> p c t", p=128)

    S_ = Sched()
    semnames = ["pe", "act", "dve", "pool", "cst", "cst2", "wi0", "wi1", "wi2", "wi3", "w1", "g0", "g1", "g2", "g3", "u0", "u1", "u2", "u3", "d0", "d1", "d2", "d3", "xl0", "xl1", "xs0", "xs1",
                "yl0", "yl1", "ys0", "ys1"]

    with ExitStack() as ctx:
        AW = 53100
        arena_t = ctx.enter_context(nc.sbuf_tensor("arena", [128, AW], F32))
        ps = [ctx.enter_context(nc.psum_tensor(f"ps{i}", [128, 512], F32)) for i in range(8)]
        semh = {n: ctx.enter_context(nc.semaphore(n)) for n in semnames}
        block = ctx.enter_context(nc.Block())

        ar = Arena(arena_t[:, :])
        cf = ar.alloc(NCF, F32)
        cb = ar.alloc(NCB, BF16)
        ones_bf = ar.alloc(128, BF16)
        ones_f = ar.alloc(128, F32)
        eps_t = ar.alloc(8, F32)
        base_off = ar.off
        B_cst = Buf("cst")

        g1 = cf[:, CF_G1:CF_G1 + 8]
        g2 = cf[:, CF_G2:CF_G2 + 8]
        gf = cf[:, CF_GF:CF_GF + 8]
        pscale = cf[:, CF_PS:CF_PS + 4]
        bfor = cf[:, CF_BF:CF_BF + 32]
        invcnt = cf[:, CF_IC:CF_IC + 64].rearrange("p (g t) -> p g t", t=16)
        tri_f = cf[:, CF_TRI:CF_TRI + 128]
        e64_f = cf[:, CF_E64:CF_E64 + 128]
        ident_bf = cb[:, CB_ID:CB_ID + 128]
        maskb = cb[:, CB_MASK:CB_MASK + 128]
        selneg = cb[:, CB_SEL:CB_SEL + 1024].rearrange("p (h m) -> p h m", m=128)

        S_.dma("sp", "cst", lambda e: e.dma_start(out=cf, in_=cf_d), writes=[B_cst])
        S_.dma("pool", "cst2", lambda e: e.dma_start(out=cb, in_=cb_d), writes=[B_cst])
        S_.op("pool", lambda e: e.memset(ones_bf, 1.0), writes=[B_cst])
        S_.op("pool", lambda e: e.memset(ones_f, 1.0), writes=[B_cst])
        S_.op("pool", lambda e: e.memset(eps_t, EPS), writes=[B_cst])

        win_bf = ar.alloc(KC * INW, BF16).rearrange("p (c n) -> p c n", n=INW)
        wout_bf = ar.alloc(KC * D, BF16).rearrange("p (c n) -> p c n", n=D)
        wpool_bf = ar.alloc(4 * 128, BF16).rearrange("p (g n) -> p g n", n=128)
        kT = ar.alloc(4 * S, BF16).rearrange("p (a t) -> p a t", t=S)
        NJ = S // 128
        vst = ar.alloc(NJ * H * 65, BF16).rearrange("p (j h d) -> p j h d", h=H, d=65)
        xt = [ar.alloc(KC * TT, F32).rearrange("p (c t) -> p c t", t=TT) for _ in range(2)]
        NXSQ = 3
        xsq = [ar.alloc(TT, BF16) for _ in range(NXSQ)]
        hb = ar.alloc(KC * TT, BF16).rearrange("p (c t) -> p c t", t=TT)
        rstd = ar.alloc(TT, F32)
        qz = ar.alloc(H * TT, BF16).rearrange("p (h t) -> p h t", t=TT)
        UW = TT + 16
        uT = ar.alloc(4 * UW, F32).rearrange("p (g t) -> p g t", t=UW)
        ptmp = [ar.alloc(UW, F32) for _ in range(2)]
        lnt = ptmp[0][:, 0:TT]
        pooledT = ar.alloc(4 * TT, BF16).rearrange("p (g t) -> p g t", t=TT)
        mixT = ar.alloc(4 * TT, BF16).rearrange("p (g t) -> p g t", t=TT)
        attnT = ar.alloc(4 * TT, BF16).rearrange("p (a t) -> p a t", t=TT)
        NPT = 3
        pT = [ar.alloc(TT, BF16) for _ in range(NPT)]
        nr = [ar.alloc(TT, F32) for _ in range(2)]
        CT_bf = ar.alloc(TT, BF16)
        fz = ar.alloc(32, F32)
        lst = ar.alloc(NJ * H, F32).rearrange("p (n h) -> p n h", h=H)
        accs = ar.alloc((NJ + 1) * H, F32).rearrange("p (n h) -> p n h", h=H)
        Cst = ar.alloc(NJ * H, F32).rearrange("p (n h) -> p n h", h=H)
        tmp16 = ar.alloc(16, F32)
        p1_end = ar.off

        ar.off = base_off
        FBLK = [(0, 6), (6, 12), (12, 17), (17, 22)]
        wg_blk, wu_blk = [None] * 4, [None] * 4
        for bi in (0,):
            f0, f1 = FBLK[bi]
            wg_blk[bi] = ar.alloc(KC * (f1 - f0) * 128, BF16).rearrange("p (c n) -> p c n", c=KC)
            wu_blk[bi] = ar.alloc(KC * (f1 - f0) * 128, BF16).rearrange("p (c n) -> p c n", c=KC)
        assert ar.off <= base_off + (KC * INW * 2 + 3) // 4, "prefetch blocks must fit in the w_in region"
        for bi in (1, 2, 3):
            f0, f1 = FBLK[bi]
            wg_blk[bi] = ar.alloc(KC * (f1 - f0) * 128, BF16).rearrange("p (c n) -> p c n", c=KC)
            wu_blk[bi] = ar.alloc(KC * (f1 - f0) * 128, BF16).rearrange("p (c n) -> p c n", c=KC)
        wd_bf = ar.alloc(FC * D, BF16).rearrange("p (c n) -> p c n", n=D)
        yt = [ar.alloc(KC * TT, F32).rearrange("p (c t) -> p c t", t=TT) for _ in range(2)]
        hb2 = ar.alloc(KC * TT, BF16).rearrange("p (c t) -> p c t", t=TT)
        lnt2 = ar.alloc(TT, F32)
        xsq2 = [lnt2.bitcast(BF16)[:, 0:TT], lnt2.bitcast(BF16)[:, TT:2 * TT]]
        NXSQ2 = 2
        rstd2 = ar.alloc(TT, F32)
        actT = ar.alloc(FC * TT, BF16).rearrange("p (c t) -> p c t", t=TT)
        sil = [ar.alloc(TT, F32) for _ in range(2)]
        p2_end = ar.off
        wg_v = w_gate.rearrange("(c p) n -> p c n", p=128)
        wu_v = w_up.rearrange("(c p) n -> p c n", p=128)
        wd_v = w_down.rearrange("(c p) n -> p c n", p=128)
        B_wg = [Buf(f"wg{i}") for i in range(4)]
        B_wu = [Buf(f"wu{i}") for i in range(4)]
        B_wd = [Buf(f"wd{i}") for i in range(4)]

        def emit_w2_block(bi, extra_writes=()):
            f0, f1 = FBLK[bi]
            S_.dma("pool", f"g{bi}", lambda e: e.dma_start(out=wg_blk[bi][:, :, :], in_=wg_v[:, :, f0 * 128:f1 * 128]),
                   writes=[B_wg[bi]] + list(extra_writes))
            S_.dma("pool", f"u{bi}", lambda e: e.dma_start(out=wu_blk[bi][:, :, :], in_=wu_v[:, :, f0 * 128:f1 * 128]),
                   writes=[B_wu[bi]] + list(extra_writes))

        B_w1 = Buf("w1")
        win_v = w_in.rearrange("(c p) n -> p c n", p=128)
        B_win = [Buf(f"win{i}") for i in range(4)]
        for i, (c0, c1) in enumerate([(1024, 1544), (0, 512), (512, 1024), (1544, 2056)]):
            S_.dma("pool", f"wi{i}", lambda e, c0=c0, c1=c1: e.dma_start(out=win_bf[:, :, c0:c1], in_=win_v[:, :, c0:c1]),
                   writes=[B_win[i]])
        for c in range(KC):
            S_.dma("pool", "w1", lambda e, c=c: e.dma_start(out=wout_bf[:, c, :], in_=w_out[c * 128:(c + 1) * 128, :]),
                   track=False)
        for g in range(4):
            S_.dma("pool", "w1", lambda e, g=g: e.dma_start(out=wpool_bf[:, g, :], in_=w_pool[g]), track=False)
        B_w1.w = ("w1", S_.cnt["w1"])
        B_vones = Buf("vones")
        B_uT = [Buf(f"uT{g}") for g in range(4)]
        B_uhalo = Buf("uhalo")
        B_acc = [Buf(f"acc{n}") for n in range(NJ + 1)]
        S_.op("pool", lambda e: e.memset(vst[:, :, :, 64:65], 1.0), writes=[B_vones])
        S_.op("pool", lambda e: e.memset(uT[:, :, 0:16], 0.0), writes=[B_uhalo])
        S_.op("pool", lambda e: e.memset(accs[:, 0, :], 0.0), writes=[B_acc[0]])
        B_init = Buf("init")
        S_.op("pool", lambda e: e.memset(qz[:, :, :], 0.0), writes=[B_init])
        S_.op("pool", lambda e: e.memset(CT_bf, 0.0), writes=[B_init])
        S_.op("pool", lambda e: e.memset(nr[0], 0.0), writes=[B_init])
        S_.op("pool", lambda e: e.memset(nr[1], 0.0), writes=[B_init])

        B_xt = [[Buf(f"xt{i}_{c}") for c in range(KC)] for i in range(2)]
        B_xsq = [Buf(f"xsq{i}") for i in range(NXSQ)]
        B_hb = [Buf(f"hb{c}") for c in range(KC)]
        B_rstd = Buf("rstd")
        B_ps = [Buf(f"ps{i}") for i in range(8)]
        B_qz = [Buf(f"qz{h}") for h in range(H)]
        B_kT = [[Buf(f"kT{a}_{t}") for t in range(NT)] for a in range(4)]
        B_v = [Buf(f"v{j}") for j in range(NJ)]
        B_ptmp = [Buf("ptmp0"), Buf("ptmp1")]
        B_lnt = B_ptmp[0]
        B_pooled = [Buf(f"pooled{g}") for g in range(4)]
        B_mix = [Buf(f"mix{g}") for g in range(4)]
        B_attn = [Buf(f"attn{h}") for h in range(H)]
        B_pT = [Buf(f"pT{i}") for i in range(NPT)]
        B_nr, B_CT = [Buf("nr0"), Buf("nr1")], Buf("CTbf")
        B_fz = Buf("fz")
        B_lst = [Buf(f"lst{n}") for n in range(NJ)]
        B_Cst = [Buf(f"Cst{t}") for t in range(NT)]
        B_tmp16 = Buf("tmp16")

        ST_BANKS = [0, 1, 2]
        ACC_BANKS = [3, 4]
        PB_BANKS = [5, 6]
        MISC = 7
        pb_ctr = [0]

        def next_pb():
            b = PB_BANKS[pb_ctr[0] % 2]
            pb_ctr[0] += 1
            return b

        xsq_ctr = [0]

        def emit_load(tt):
            i = tt % 2
            S_.dma("sp", f"xl{i}", lambda e: e.dma_start(out=xt[i][:], in_=xT_v[:, :, tt * TT:(tt + 1) * TT]),
                   writes=B_xt[i])

        def norm_steps(xtile, B_x, hbt, B_h, gcol, misc_bank=MISC):
            steps = []
            ks = []
            for c in range(KC):
                k = xsq_ctr[0] % NXSQ
                xsq_ctr[0] += 1
                ks.append(k)

            def sq(c):
                k = ks[c]
                S_.op("pool", lambda e: e.tensor_tensor(out=xsq[k], in0=xtile[:, c, :], in1=xtile[:, c, :],
                                                        op=ALU.mult),
                      reads=[B_x[c]], writes=[B_xsq[k]])

            def mm(c):
                k = ks[c]
                S_.op("pe", lambda e: e.matmul(ps[misc_bank][:, :], ones_bf, xsq[k], start=(c == 0),
                                               stop=(c == KC - 1)),
                      reads=[B_xsq[k], B_cst], writes=[B_ps[misc_bank]])

            LAG = 2
            for c in range(KC + LAG):
                def st(c=c):
                    if c < KC:
                        sq(c)
                    if c - LAG >= 0:
                        mm(c - LAG)
                steps.append(st)

            def rs():
                S_.op("act", lambda e: e.activation(out=lnt, in_=ps[misc_bank][:, :], func=AF.Ln, bias=eps_t[:, 0:1],
                                                    scale=1.0 / D),
                      reads=[B_ps[misc_bank], B_cst], writes=[B_lnt])
                S_.op("act", lambda e: e.activation(out=rstd, in_=lnt, func=AF.Exp, scale=-0.5),
                      reads=[B_lnt], writes=[B_rstd])
            steps.append(rs)
            for c in range(KC):
                def hbs(c=c):
                    S_.op("dve", lambda e: e.scalar_tensor_tensor(out=hbt[:, c, :], in0=xtile[:, c, :],
                                                                  scalar=gcol[:, c:c + 1], in1=rstd,
                                                                  op0=ALU.mult, op1=ALU.mult),
                          reads=[B_x[c], B_rstd, B_cst], writes=[B_h[c]])
                steps.append(hbs)
            return steps

        evac_ctr = [0]

        def evac_copy(out_ap, in_ap, reads, writes, scale=None):
            k = evac_ctr[0]
            evac_ctr[0] += 1
            if k % 2 == 0:
                if scale is None:
                    S_.op("act", lambda e: e.activation(out=out_ap, in_=in_ap, func=AF.Copy), reads=reads, writes=writes)
                else:
                    S_.op("act", lambda e: e.activation(out=out_ap, in_=in_ap, func=AF.Copy, scale=scale),
                          reads=reads, writes=writes)
            else:
                if scale is None:
                    S_.op("dve", lambda e: e.tensor_copy(out=out_ap, in_=in_ap), reads=reads, writes=writes)
                else:
                    S_.op("dve", lambda e: e.tensor_scalar(out=out_ap, in0=in_ap, scalar1=scale, scalar2=None,
                                                           op0=ALU.mult), reads=reads, writes=writes)

        def emit_vf(tt):
            for blk in range(4):
                b = next_pb()
                j = tt * 4 + blk

                def mmv(e, blk=blk, b=b):
                    ins = None
                    for c in range(KC):
                        lhsT = hb[:, c, blk * 128:(blk + 1) * 128]
                        e.matmul(ps[b][:, :], lhsT, win_bf[:, c, 1024:1536], start=(c == 0), stop=(c == KC - 1))
                        ins = e.matmul(ps[MISC][:, blk * 8:(blk + 1) * 8], lhsT, win_bf[:, c, 1536:1544],
                                       start=(c == 0), stop=(c == KC - 1))
                    return ins

                S_.op("pe", mmv, reads=B_hb + [B_win[0]], writes=[B_ps[b], B_ps[MISC]])
                evac_copy(vst[:, j, :, 0:64], ps[b][:, :].rearrange("p (h d) -> p h d", d=64), [B_ps[b]], [B_v[j]])

        def emit_qku(tt):
            specs = []
            for a in range(4):
                specs.append(("q", a, a * 128))
            for a in range(4):
                specs.append(("k", a, 512 + a * 128))
            for g in range(4):
                specs.append(("u", g, 1544 + g * 128))
            for kind, idx, col in specs:
                b = next_pb()

                def mm(e, col=col, b=b):
                    ins = None
                    for c in range(KC):
                        ins = e.matmul(ps[b][:, :], win_bf[:, c, col:col + 128], hb[:, c, :], start=(c == 0),
                                       stop=(c == KC - 1))
                    return ins

                wb = {"q": B_win[1], "k": B_win[2], "u": B_win[3]}[kind]
                S_.op("pe", mm, reads=B_hb + [wb], writes=[B_ps[b]])
                if kind == "q":
                    evac_copy(qz[0:64, 2 * idx, :], ps[b][0:64, :], [B_ps[b], B_init], [B_qz[2 * idx]], scale=0.125)
                    evac_copy(qz[64:128, 2 * idx + 1, :], ps[b][64:128, :], [B_ps[b], B_init], [B_qz[2 * idx + 1]],
                              scale=0.125)
                elif kind == "k":
                    evac_copy(kT[:, idx, tt * TT:(tt + 1) * TT], ps[b][:, :], [B_ps[b]], [B_kT[idx][tt]])
                else:
                    evac_copy(uT[:, idx, 16:UW], ps[b][:, :], [B_ps[b]], [B_uT[idx]])

        def emit_fchain_a(tt):
            n0 = tt * 4
            S_.op("dve", lambda e: e.tensor_tensor(out=fz, in0=ps[MISC][:, 0:32], in1=bfor, op=ALU.add),
                  reads=[B_ps[MISC], B_cst], writes=[B_fz])
            S_.op("act", lambda e: e.activation(out=fz, in_=fz, func=AF.Exp, scale=-1.0), reads=[B_fz], writes=[B_fz])
            lview = lst[:, n0:n0 + 4, :].rearrange("p n h -> p (n h)")
            S_.op("act", lambda e: e.activation(out=lview, in_=fz, func=AF.Ln, bias=1.0, scale=1.0),
                  reads=[B_fz], writes=[B_lst[n0 + i] for i in range(4)])
            for i in range(4):
                n = n0 + i
                S_.op("dve", lambda e, n=n: e.tensor_tensor(out=accs[:, n + 1, :], in0=accs[:, n, :], in1=lst[:, n, :],
                                                            op=ALU.add),
                      reads=[B_acc[n], B_lst[n]], writes=[B_acc[n + 1]])

        def emit_fchain_b(tt):
            n0 = tt * 4
            bct = next_pb()

            def mmc(e):
                ins = None
                for i in range(4):
                    n = n0 + i
                    e.matmul(ps[MISC][:, 32 + i * 8:40 + i * 8], tri_f, lst[:, n, :], start=True, stop=False)
                    e.matmul(ps[MISC][:, 32 + i * 8:40 + i * 8], ones_f, accs[:, n, :], start=False, stop=True)
                for i in range(4):
                    n = n0 + i
                    e.matmul(ps[bct][0:8, i * 128:(i + 1) * 128], lst[:, n, :], tri_f, start=True, stop=False)
                    ins = e.matmul(ps[bct][0:8, i * 128:(i + 1) * 128], accs[:, n, :], ones_f, start=False, stop=True)
                return ins

            S_.op("pe", mmc, reads=[B_lst[n0 + i] for i in range(4)] + [B_acc[n0 + i] for i in range(4)] + [B_cst],
                  writes=[B_ps[MISC], B_ps[bct]])
            S_.op("dve", lambda e: e.tensor_copy(out=Cst[:, n0:n0 + 4, :].rearrange("p n h -> p (n h)"),
                                                 in_=ps[MISC][:, 32:64]),
                  reads=[B_ps[MISC]], writes=[B_Cst[tt]])
            S_.op("dve", lambda e: e.tensor_copy(out=CT_bf[0:8, :], in_=ps[bct][0:8, :]),
                  reads=[B_ps[bct], B_init], writes=[B_CT])

        def emit_pool(tt):
            for g in range(4):
                w = 2 << g
                src = uT[:, g, :]
                srcB = [B_uT[g], B_uhalo]
                shift = 1
                lo = 0
                step = 0
                while shift < w:
                    dst = ptmp[step % 2]
                    nlo = lo + shift
                    S_.op("pool", lambda e, dst=dst, src=src, nlo=nlo, shift=shift:
                          e.tensor_tensor(out=dst[:, nlo:UW], in0=src[:, nlo:UW], in1=src[:, nlo - shift:UW - shift],
                                          op=ALU.add),
                          reads=srcB, writes=[B_ptmp[step % 2]])
                    src = dst
                    srcB = [B_ptmp[step % 2]]
                    lo = nlo
                    shift *= 2
                    step += 1
                oth = ptmp[step % 2]
                Both = B_ptmp[step % 2]
                S_.op("pool", lambda e, src=src, oth=oth, w=w: e.tensor_scalar(out=oth[:, 16:UW], in0=src[:, 16:UW],
                                                                           scalar1=1.0 / w, scalar2=None,
                                                                           op0=ALU.mult),
                      reads=srcB, writes=[Both])
                S_.op("pool", lambda e, oth=oth, g=g: e.tensor_tensor(out=pooledT[:, g, :], in0=oth[:, 16:UW],
                                                                     in1=uT[:, g, 16:UW], op=ALU.subtract),
                      reads=[Both, B_uT[g]], writes=[B_pooled[g]])
                if tt == 0:
                    S_.op("pool", lambda e, src=src, g=g: e.tensor_tensor(out=tmp16, in0=src[:, 16:32],
                                                                          in1=invcnt[:, g, :], op=ALU.mult),
                          reads=srcB + [B_cst], writes=[B_tmp16])
                    S_.op("pool", lambda e, g=g: e.tensor_tensor(out=pooledT[:, g, 0:16], in0=tmp16,
                                                                 in1=uT[:, g, 16:32], op=ALU.subtract),
                          reads=[B_tmp16, B_uT[g]], writes=[B_pooled[g]])
            S_.op("pool", lambda e: e.tensor_copy(out=uT[:, :, 0:16], in_=uT[:, :, TT:UW]),
                  reads=B_uT, writes=[B_uhalo])

        def emit_poolmix(tt):
            for g in range(4):
                b = next_pb()
                S_.op("pe", lambda e, g=g, b=b: e.matmul(ps[b][:, :], wpool_bf[:, g, :], pooledT[:, g, :], start=True,
                                                         stop=True),
                      reads=[B_pooled[g], B_w1], writes=[B_ps[b]])
                S_.op("act", lambda e, g=g, b=b: e.activation(out=mixT[:, g, :], in_=ps[b][:, :], func=AF.Copy,
                                                              scale=pscale[:, g:g + 1]),
                      reads=[B_ps[b], B_cst], writes=[B_mix[g]])

        deferred = []

        def flush_deferred(now=None):
            keep = []
            for due, fn in deferred:
                if now is None or due <= now:
                    fn()
                else:
                    keep.append((due, fn))
            deferred[:] = keep

        def emit_attention(tt, steps):
            nb = 4 * tt + 4
            blocks = [(h, j) for h in range(H) for j in range(nb)]
            LOOK = 2
            state = {}
            steps = list(steps)

            def qk(n):
                h, j = blocks[n]
                a = h // 2
                diag = j >= 4 * tt
                c0 = 128 * (j - 4 * tt) if diag else 0
                sb = ST_BANKS[n % 3]
                pi = n % NPT

                def mm(e):
                    e.matmul(ps[sb][:, c0:TT], kT[:, a, j * 128:(j + 1) * 128], qz[:, h, c0:TT],
                             start=True, stop=False)
                    ins = e.matmul(ps[sb][:, c0:TT], selneg[:, h, :], CT_bf[:, c0:TT], start=False,
                                   stop=(not diag))
                    if diag:
                        ins = e.matmul(ps[sb][:, c0:c0 + 128], ident_bf, maskb, start=False, stop=True)
                    return ins

                S_.op("pe", mm, reads=[B_kT[a][j // 4], B_qz[h], B_CT, B_cst], writes=[B_ps[sb]])
                S_.op("act", lambda e: e.activation(out=pT[pi][:, c0:TT], in_=ps[sb][:, c0:TT], func=AF.Exp,
                                                    bias=Cst[:, j, h:h + 1], scale=1.0),
                      reads=[B_ps[sb], B_Cst[j // 4]], writes=[B_pT[pi]])
                state[n] = (c0, pi)

            def pv(n, it):
                h, j = blocks[n]
                a, p0 = h // 2, 64 * (h % 2)
                c0, pi = state.pop(n)
                ab = ACC_BANKS[h % 2]
                S_.op("pe", lambda e: e.matmul(ps[ab][0:65, c0:TT], vst[:, j, h, :], pT[pi][:, c0:TT], start=(j == 0),
                                               stop=(j == nb - 1)),
                      reads=[B_pT[pi], B_v[j], B_vones], writes=[B_ps[ab]])
                if j == nb - 1:
                    nrb = nr[h % 2]
                    Bn = B_nr[h % 2]
                    S_.op("dve", lambda e: e.tensor_copy(out=nrb[0:64, :], in_=ps[ab][0:64, :]),
                          reads=[B_ps[ab], B_init], writes=[Bn])
                    S_.op("dve", lambda e: e.reciprocal(out=nrb[64:65, :], in_=ps[ab][64:65, :]),
                          reads=[B_ps[ab], B_init], writes=[Bn])

                    def fin():
                        b = next_pb()
                        S_.op("pe", lambda e: e.matmul(ps[b][:, :], e64_f, nrb, start=True, stop=True),
                              reads=[Bn, B_cst], writes=[B_ps[b]])
                        S_.op("dve", lambda e: e.tensor_tensor(out=attnT[p0:p0 + 64, a, :], in0=nrb[0:64, :],
                                                               in1=ps[b][0:64, :], op=ALU.mult),
                              reads=[Bn, B_ps[b]], writes=[B_attn[h]])
                    deferred.append((it + 7, fin))

            nblk = len(blocks)
            start_steps = max(2, nblk // 8)
            for n in range(nblk + LOOK):
                if n < nblk:
                    qk(n)
                flush_deferred(n)
                if n - LOOK >= 0:
                    pv(n - LOOK, n)
                if n >= start_steps and steps:
                    steps.pop(0)()
            for st in steps:
                st()

        def emit_wout(tt):
            i = tt % 2
            for m in range(KC):
                b = next_pb()

                def mm(e, m=m, b=b):
                    ins = None
                    for ec in range(8):
                        rhs = attnT[:, ec, :] if ec < 4 else mixT[:, ec - 4, :]
                        ins = e.matmul(ps[b][:, :], wout_bf[:, ec, m * 128:(m + 1) * 128], rhs, start=(ec == 0),
                                       stop=(ec == 7))
                    return ins

                S_.op("pe", mm, reads=B_attn + B_mix + [B_w1], writes=[B_ps[b]])
                S_.op("dve", lambda e, m=m, b=b: e.tensor_tensor(out=xt[i][:, m, :], in0=ps[b][:, :], in1=xt[i][:, m, :],
                                                                 op=ALU.add),
                      reads=[B_ps[b], B_xt[i][m]], writes=[B_xt[i][m]])
            S_.dma("sp", f"xs{i}", lambda e: e.dma_start(out=x1_v[:, :, tt * TT:(tt + 1) * TT], in_=xt[i][:]),
                   reads=B_xt[i])

        emit_load(0)
        for st in norm_steps(xt[0], B_xt[0], hb, B_hb, g1):
            st()
        emit_vf(0)
        emit_fchain_a(0)
        emit_qku(0)
        emit_fchain_b(0)
        emit_pool(0)
        for tt in range(NT):
            steps = []
            if tt + 1 < NT:
                emit_load(tt + 1)
                steps = norm_steps(xt[(tt + 1) % 2], B_xt[(tt + 1) % 2], hb, B_hb, g1)
            emit_attention(tt, steps)
            emit_poolmix(tt)
            if tt + 1 < NT:
                emit_vf(tt + 1)
                emit_fchain_a(tt + 1)
            flush_deferred()
            emit_wout(tt)
            if tt + 1 < NT:
                emit_qku(tt + 1)
                emit_fchain_b(tt + 1)
                emit_pool(tt + 1)
            if tt + 1 == NT - 1 or NT == 1:
                emit_w2_block(0, extra_writes=B_win)

        S_.barrier()

        B_yt = [[Buf(f"yt{i}_{c}") for c in range(KC)] for i in range(2)]
        B_hb2 = [Buf(f"hb2_{c}") for c in range(KC)]
        B_act = [Buf(f"act{c}") for c in range(FC)]
        B_sil = [Buf("sil0"), Buf("sil1")]
        B_xsq2 = [Buf(f"xsq2_{i}") for i in range(NXSQ2)]
        B_lnt2, B_rstd2 = Buf("lnt2"), Buf("rstd2")
        B_ps2 = [Buf(f"ps2_{i}") for i in range(8)]
        GB, UB, DB, NB = [0, 1], [2, 3], [4, 5], 6

        def emit_load2(tt):
            i = tt % 2
            S_.dma("sp", f"yl{i}", lambda e: e.dma_start(out=yt[i][:], in_=x1_v[:, :, tt * TT:(tt + 1) * TT]),
                   writes=B_yt[i])

        emit_load2(0)
        for bi in (1, 2, 3):
            emit_w2_block(bi)
        for bi, (f0, f1) in enumerate(FBLK):
            S_.dma("pool", f"d{bi}", lambda e, f0=f0, f1=f1: e.dma_start(out=wd_bf[:, f0:f1, :], in_=wd_v[:, f0:f1, :]),
                   writes=[B_wd[bi]])

        def norm2_steps(xtile, B_x, out_fn, bank):
            steps = []

            def sq(c):
                k = c % NXSQ2
                S_.op("pool", lambda e: e.tensor_tensor(out=xsq2[k], in0=xtile[:, c, :], in1=xtile[:, c, :],
                                                        op=ALU.mult),
                      reads=[B_x[c]], writes=[B_xsq2[k]])

            def mm(c):
                k = c % NXSQ2
                S_.op("pe", lambda e: e.matmul(ps[bank][:, :], ones_bf, xsq2[k], start=(c == 0), stop=(c == KC - 1)),
                      reads=[B_xsq2[k], B_cst], writes=[B_ps2[bank]])

            LAG = 1
            for c in range(KC + LAG):
                def st(c=c):
                    if c < KC:
                        sq(c)
                    if c - LAG >= 0:
                        mm(c - LAG)
                steps.append(st)

            def rs():
                S_.op("act", lambda e: e.activation(out=lnt2, in_=ps[bank][:, :], func=AF.Ln, bias=eps_t[:, 0:1],
                                                    scale=1.0 / D),
                      reads=[B_ps2[bank], B_cst], writes=[B_lnt2] + B_xsq2)
                S_.op("act", lambda e: e.activation(out=rstd2, in_=lnt2, func=AF.Exp, scale=-0.5),
                      reads=[B_lnt2] + B_xsq2, writes=[B_rstd2])
            steps.append(rs)
            for c in range(KC):
                steps.append(lambda c=c: out_fn(c))
            return steps

        def hb_steps(tt):
            i = tt % 2
            xtile = yt[i]

            def mk_hb(c):
                S_.op("dve", lambda e: e.scalar_tensor_tensor(out=hb2[:, c, :], in0=xtile[:, c, :],
                                                              scalar=g2[:, c:c + 1], in1=rstd2,
                                                              op0=ALU.mult, op1=ALU.mult),
                      reads=[B_yt[i][c], B_rstd2, B_cst], writes=[B_hb2[c]])
            return norm2_steps(xtile, B_yt[i], mk_hb, NB)

        def final_steps(tt):
            i = tt % 2
            xtile = yt[i]

            def mk_y(c):
                S_.op("dve", lambda e: e.scalar_tensor_tensor(out=xtile[:, c, :], in0=xtile[:, c, :],
                                                              scalar=gf[:, c:c + 1], in1=rstd2,
                                                              op0=ALU.mult, op1=ALU.mult),
                      reads=[B_yt[i][c], B_rstd2, B_cst], writes=[B_yt[i][c]])
            steps = norm2_steps(xtile, B_yt[i], mk_y, 7)

            def store():
                S_.dma("sp", f"ys{i}", lambda e: e.dma_start(out=yT_v[:, :, tt * TT:(tt + 1) * TT], in_=xtile[:]),
                       reads=B_yt[i])
            steps.append(store)
            return steps

        def fblk_of(fc):
            for bi, (f0, f1) in enumerate(FBLK):
                if f0 <= fc < f1:
                    return bi, fc - f0
            raise AssertionError

        def emit_gateup(tt, pending):
            for fc in range(FC):
                gb, ub = GB[fc % 2], UB[fc % 2]
                bi, fo = fblk_of(fc)

                def mmg(e, bi=bi, fo=fo, gb=gb):
                    ins = None
                    for c in range(KC):
                        ins = e.matmul(ps[gb][:, :], wg_blk[bi][:, c, fo * 128:(fo + 1) * 128], hb2[:, c, :],
                                       start=(c == 0), stop=(c == KC - 1))
                    return ins

                def mmu(e, bi=bi, fo=fo, ub=ub):
                    ins = None
                    for c in range(KC):
                        ins = e.matmul(ps[ub][:, :], wu_blk[bi][:, c, fo * 128:(fo + 1) * 128], hb2[:, c, :],
                                       start=(c == 0), stop=(c == KC - 1))
                    return ins

                S_.op("pe", mmg, reads=B_hb2 + [B_wg[bi]], writes=[B_ps2[gb]])
                S_.op("pe", mmu, reads=B_hb2 + [B_wu[bi]], writes=[B_ps2[ub]])
                sl = fc % 2
                S_.op("act", lambda e, gb=gb, sl=sl: e.activation(out=sil[sl], in_=ps[gb][:, :], func=AF.Silu),
                      reads=[B_ps2[gb]], writes=[B_sil[sl]])
                S_.op("dve", lambda e, fc=fc, ub=ub, sl=sl: e.tensor_tensor(out=actT[:, fc, :], in0=ps[ub][:, :],
                                                                            in1=sil[sl], op=ALU.mult),
                      reads=[B_ps2[ub], B_sil[sl]], writes=[B_act[fc]])
                if pending and fc >= 1:
                    pending.pop(0)()
            while pending:
                pending.pop(0)()

        def emit_down(tt, pending):
            i = tt % 2
            xtile = yt[i]
            for m in range(KC):
                db = DB[m % 2]

                def mmd(e, m=m, db=db):
                    ins = None
                    for fc in range(FC):
                        ins = e.matmul(ps[db][:, :], wd_bf[:, fc, m * 128:(m + 1) * 128], actT[:, fc, :],
                                       start=(fc == 0), stop=(fc == FC - 1))
                    return ins

                S_.op("pe", mmd, reads=B_act + B_wd, writes=[B_ps2[db]])
                S_.op("dve", lambda e, m=m, db=db: e.tensor_tensor(out=xtile[:, m, :], in0=ps[db][:, :],
                                                                   in1=xtile[:, m, :], op=ALU.add),
                      reads=[B_ps2[db], B_yt[i][m]], writes=[B_yt[i][m]])
                for _ in range(3):
                    if pending:
                        pending.pop(0)()
            while pending:
                pending.pop(0)()

        for st in hb_steps(0):
            st()
        pend_final = []
        for tt in range(NT):
            emit_gateup(tt, pend_final)
            if tt + 1 < NT:
                emit_load2(tt + 1)
            emit_down(tt, hb_steps(tt + 1) if tt + 1 < NT else [])
            pend_final = final_steps(tt)
        for st in pend_final:
            st()

        S_.barrier()

        print(f"[kernel] arena words: phase1 {p1_end} phase2 {p2_end} of {AW}; "
              f"sem counts: { {k: v for k, v in S_.cnt.items()} }")

        @block.tensor
        def _(e):
            S_.replay("pe", e, semh)

        @block.scalar
        def _(e):
            S_.replay("act", e, semh)

        @block.vector
        def _(e):
            S_.replay("dve", e, semh)

        @block.gpsimd
        def _(e):
            S_.replay("pool", e, semh)

        @block.sync
        def _(e):
            S_.replay("sp", e, semh)

    return nc


def _consts(norm1_g, norm2_g, final_g, pool_scale, b_forget):
    cf = np.zeros((128, NCF), np.float32)
    cf[:, CF_G1:CF_G1 + 8] = np.asarray(norm1_g, np.float32).reshape(8, 128).T
    cf[:, CF_G2:CF_G2 + 8] = np.asarray(norm2_g, np.float32).reshape(8, 128).T
    cf[:, CF_GF:CF_GF + 8] = np.asarray(final_g, np.float32).reshape(8, 128).T
    cf[:, CF_PS:CF_PS + 4] = np.asarray(pool_scale, np.float32).reshape(4, 128).T
    cf[:, CF_BF:CF_BF + 32] = np.tile(np.asarray(b_forget, np.float32).reshape(1, 8), (128, 4))
    t = np.arange(16)
    ic = np.stack([1.0 / np.minimum(t + 1, 2 << g) for g in range(4)]).astype(np.float32)
    cf[:, CF_IC:CF_IC + 64] = ic.reshape(1, 64)
    kk = np.arange(128)
    cf[:, CF_TRI:CF_TRI + 128] = (kk[:, None] <= kk[None, :]).astype(np.float32)
    cf[64, CF_E64:CF_E64 + 128] = 1.0
    cb = np.zeros((128, NCB), np.float32)
    cb[:, CB_ID:CB_ID + 128] = np.eye(128, dtype=np.float32)
    cb[:, CB_MASK:CB_MASK + 128] = np.where(kk[:, None] <= kk[None, :], 0.0, NEG).astype(np.float32)
    for h in range(8):
        cb[h, CB_SEL + h * 128:CB_SEL + (h + 1) * 128] = -1.0
    return cf, cb


_NC_CACHE = {}


def kernel(x, norm1_g, w_in, b_forget, w_pool, pool_scale, w_out, norm2_g, w_gate, w_up, w_down, final_g):
    x = np.asarray(x, np.float32)
    B = x.shape[0]
    cf, cb = _consts(np.asarray(norm1_g)[0], np.asarray(norm2_g)[0], np.asarray(final_g), np.asarray(pool_scale)[0],
                     np.asarray(b_forget)[0])
    shared = {
        "w_in": np.ascontiguousarray(np.asarray(w_in, np.float32)[0]),
        "w_out": np.ascontiguousarray(np.asarray(w_out, np.float32)[0]),
        "w_pool": np.ascontiguousarray(np.asarray(w_pool, np.float32)[0]),
        "w_gate": np.ascontiguousarray(np.asarray(w_gate, np.float32)[0]),
        "w_up": np.ascontiguousarray(np.asarray(w_up, np.float32)[0]),
        "w_down": np.ascontiguousarray(np.asarray(w_down, np.float32)[0]),
        "cf": cf,
        "cb": cb,
    }
    in_maps = [dict(shared, xT=np.ascontiguousarray(x[b].T)) for b in range(B)]
    if "nc" not in _NC_CACHE:
        _NC_CACHE["nc"] = build_program()
    nc = _NC_CACHE["nc"]
    res = run_bass_kernel_spmd(nc, in_maps, core_ids=list(range(B)))
    out = np.stack([np.asarray(res.results[b]["yT"]).T for b in range(B)])
    return np.ascontiguousarray(out.astype(np.float32))
```

```python
from contextlib import ExitStack

import numpy as np
import concourse.bass as bass
import concourse.mybir as mybir
from concourse.bass_utils import run_bass_kernel_spmd

F32 = mybir.dt.float32
BF16 = mybir.dt.bfloat16
AF = mybir.ActivationFunctionType
ALU = mybir.AluOpType

S = 4096
D = 1024
TT = 512
NT = S // TT
KC = D // 128
H = 8
DFF = 2816
FC = DFF // 128
INW = 2056
EPS = 1e-6
NEG = -30000.0

CF_G1, CF_G2, CF_GF, CF_PS, CF_BF, CF_IC, CF_TRI = 0, 8, 16, 24, 28, 60, 124
CF_E64 = 124 + 128
NCF = 124 + 128 + 128
CB_ID, CB_MASK, CB_SEL = 0, 128, 256
NCB = 256 + 1024


class Buf:
    __slots__ = ("name", "w", "r")

    def __init__(self, name):
        self.name = name
        self.w = None
        self.r = []


class Sched:
    ENGS = ("pe", "act", "dve", "pool", "sp")

    def __init__(self):
        self.prog = {e: [] for e in self.ENGS}
        self.cnt = {}
        self.waited = {e: {} for e in self.ENGS}

    def _deps(self, eng, reads, writes):
        need = {}

        def add(t):
            if t is not None and need.get(t[0], 0) < t[1]:
                need[t[0]] = t[1]

        for b in reads:
            add(b.w)
        for b in writes:
            add(b.w)
            for t in b.r:
                add(t)
        out = []
        for k, v in need.items():
            if k == eng and eng == "pe":
                continue
            if self.waited[eng].get(k, 0) >= v:
                continue
            self.waited[eng][k] = v
            out.append((k, v))
        return out

    def _commit(self, tok, reads, writes):
        for b in reads:
            b.r.append(tok)
        for b in writes:
            b.w = tok
            b.r = []

    def op(self, eng, fn, reads=(), writes=()):
        waits = self._deps(eng, reads, writes)
        self.cnt[eng] = self.cnt.get(eng, 0) + 1
        tok = (eng, self.cnt[eng])
        self.prog[eng].append((waits, fn, (eng, 1)))
        self._commit(tok, reads, writes)
        return tok

    def dma(self, queue, semkey, fn, reads=(), writes=(), track=True):
        waits = self._deps(queue, reads, writes) if track else []
        self.cnt[semkey] = self.cnt.get(semkey, 0) + 16
        tok = (semkey, self.cnt[semkey])
        self.prog[queue].append((waits, fn, (semkey, 16)))
        if track:
            self._commit(tok, reads, writes)
        return tok

    def barrier(self):
        for e in self.ENGS:
            waits = []
            for k, v in self.cnt.items():
                if k == e and e == "pe":
                    continue
                if self.waited[e].get(k, 0) >= v:
                    continue
                self.waited[e][k] = v
                waits.append((k, v))
            if waits:
                self.prog[e].append((waits, None, None))

    def replay(self, eng, e, semh):
        for waits, fn, sig in self.prog[eng]:
            for k, v in waits:
                e.wait_ge(semh[k], v)
            if fn is None:
                continue
            ins = fn(e)
            ins.then_inc(semh[sig[0]], sig[1])


class Arena:
    def __init__(self, ap):
        self.ap = ap
        self.off = 0
        self.words = ap.shape[1]

    def alloc(self, n, dtype):
        nbytes = n * (2 if dtype == BF16 else 4)
        words = (nbytes + 31) // 32 * 8
        assert self.off + words <= self.words, ("arena overflow", self.off, words, self.words)
        v = self.ap[:, self.off:self.off + words]
        self.off += words
        if dtype == BF16:
            return v.bitcast(BF16)[:, 0:n]
        return v[:, 0:n]


def build_program(debug=False):
    nc = bass.Bass("TRN2", target_bir_lowering=False)
    dt_in = lambda name, shape: nc.dram_tensor(name, shape, F32, kind="ExternalInput").ap()
    xT = dt_in("xT", [D, S])
    w_in = dt_in("w_in", [D, INW])
    w_out = dt_in("w_out", [D, D])
    w_pool = dt_in("w_pool", [4, 128, 128])
    w_gate = dt_in("w_gate", [D, DFF])
    w_up = dt_in("w_up", [D, DFF])
    w_down = dt_in("w_down", [DFF, D])
    cf_d = dt_in("cf", [128, NCF])
    cb_d = dt_in("cb", [128, NCB])
    x1s = nc.dram_tensor("x1s", [D, S], F32, kind=("ExternalOutput" if debug else "Internal")).ap()
    yT = nc.dram_tensor("yT", [D, S], F32, kind="ExternalOutput").ap()

    xT_v = xT.rearrange("(c p) t -> p c t", p=128)
    x1_v = x1s.rearrange("(c p) t -> p c t", p=128)
    yT_v = yT.rearrange("(c p) t -> p c t", p=128)

    S_ = Sched()
    semnames = ["pe", "act", "dve", "pool", "cst", "cst2", "wi0", "wi1", "wi2", "wi3", "w1", "g0", "g1", "g2", "g3", "u0", "u1", "u2", "u3", "d0", "d1", "d2", "d3", "xl0", "xl1", "xs0", "xs1",
                "yl0", "yl1", "ys0", "ys1"]

    with ExitStack() as ctx:
        AW = 53100
        arena_t = ctx.enter_context(nc.sbuf_tensor("arena", [128, AW], F32))
        ps = [ctx.enter_context(nc.psum_tensor(f"ps{i}", [128, 512], F32)) for i in range(8)]
        semh = {n: ctx.enter_context(nc.semaphore(n)) for n in semnames}
        block = ctx.enter_context(nc.Block())

        ar = Arena(arena_t[:, :])
        cf = ar.alloc(NCF, F32)
        cb = ar.alloc(NCB, BF16)
        ones_bf = ar.alloc(128, BF16)
        ones_f = ar.alloc(128, F32)
        eps_t = ar.alloc(8, F32)
        base_off = ar.off
        B_cst = Buf("cst")

        g1 = cf[:, CF_G1:CF_G1 + 8]
        g2 = cf[:, CF_G2:CF_G2 + 8]
        gf = cf[:, CF_GF:CF_GF + 8]
        pscale = cf[:, CF_PS:CF_PS + 4]
        bfor = cf[:, CF_BF:CF_BF + 32]
        invcnt = cf[:, CF_IC:CF_IC + 64].rearrange("p (g t) -> p g t", t=16)
        tri_f = cf[:, CF_TRI:CF_TRI + 128]
        e64_f = cf[:, CF_E64:CF_E64 + 128]
        ident_bf = cb[:, CB_ID:CB_ID + 128]
        maskb = cb[:, CB_MASK:CB_MASK + 128]
        selneg = cb[:, CB_SEL:CB_SEL + 1024].rearrange("p (h m) -> p h m", m=128)

        S_.dma("sp", "cst", lambda e: e.dma_start(out=cf, in_=cf_d), writes=[B_cst])
        S_.dma("pool", "cst2", lambda e: e.dma_start(out=cb, in_=cb_d), writes=[B_cst])
        S_.op("pool", lambda e: e.memset(ones_bf, 1.0), writes=[B_cst])
        S_.op("pool", lambda e: e.memset(ones_f, 1.0), writes=[B_cst])
        S_.op("pool", lambda e: e.memset(eps_t, EPS), writes=[B_cst])

        win_bf = ar.alloc(KC * INW, BF16).rearrange("p (c n) -> p c n", n=INW)
        wout_bf = ar.alloc(KC * D, BF16).rearrange("p (c n) -> p c n", n=D)
        wpool_bf = ar.alloc(4 * 128, BF16).rearrange("p (g n) -> p g n", n=128)
        kT = ar.alloc(4 * S, BF16).rearrange("p (a t) -> p a t", t=S)
        NJ = S // 128
        vst = ar.alloc(NJ * H * 65, BF16).rearrange("p (j h d) -> p j h d", h=H, d=65)
        xt = [ar.alloc(KC * TT, F32).rearrange("p (c t) -> p c t", t=TT) for _ in range(2)]
        NXSQ = 3
        xsq = [ar.alloc(TT, BF16) for _ in range(NXSQ)]
        hb2d = ar.alloc(KC * TT, BF16)
        hb = hb2d.rearrange("p (c t) -> p c t", t=TT)
        lnt = hb2d[:, 6 * TT:8 * TT].bitcast(F32)
        rstd = ar.alloc(TT, F32)
        qz = ar.alloc(H * TT, BF16).rearrange("p (h t) -> p h t", t=TT)
        UW = TT + 16
        uT = ar.alloc(4 * UW, F32).rearrange("p (g t) -> p g t", t=UW)
        ptmp = [ar.alloc(UW, F32) for _ in range(2)]
        pooledT = ar.alloc(4 * TT, BF16).rearrange("p (g t) -> p g t", t=TT)
        mixT = ar.alloc(4 * TT, BF16).rearrange("p (g t) -> p g t", t=TT)
        attnT = ar.alloc(4 * TT, BF16).rearrange("p (a t) -> p a t", t=TT)
        NPT = 3
        pT = [ar.alloc(TT, BF16) for _ in range(NPT)]
        nr = [ar.alloc(TT, F32) for _ in range(2)]
        CT_bf = ar.alloc(TT, BF16)
        fz = ar.alloc(32, F32)
        lst = ar.alloc(NJ * H, F32).rearrange("p (n h) -> p n h", h=H)
        accs = ar.alloc((NJ + 1) * H, F32).rearrange("p (n h) -> p n h", h=H)
        Cst = ar.alloc(NJ * H, F32).rearrange("p (n h) -> p n h", h=H)
        tmp16 = ar.alloc(16, F32)
        p1_end = ar.off

        ar.off = base_off
        FBLK = [(0, 6), (6, 12), (12, 17), (17, 22)]
        wg_blk, wu_blk = [None] * 4, [None] * 4
        for bi in (0,):
            f0, f1 = FBLK[bi]
            wg_blk[bi] = ar.alloc(KC * (f1 - f0) * 128, BF16).rearrange("p (c n) -> p c n", c=KC)
            wu_blk[bi] = ar.alloc(KC * (f1 - f0) * 128, BF16).rearrange("p (c n) -> p c n", c=KC)
        assert ar.off <= base_off + (KC * INW * 2 + 3) // 4, "prefetch blocks must fit in the w_in region"
        for bi in (1, 2, 3):
            f0, f1 = FBLK[bi]
            wg_blk[bi] = ar.alloc(KC * (f1 - f0) * 128, BF16).rearrange("p (c n) -> p c n", c=KC)
            wu_blk[bi] = ar.alloc(KC * (f1 - f0) * 128, BF16).rearrange("p (c n) -> p c n", c=KC)
        wd_bf = ar.alloc(FC * D, BF16).rearrange("p (c n) -> p c n", n=D)
        yt = [ar.alloc(KC * TT, F32).rearrange("p (c t) -> p c t", t=TT) for _ in range(2)]
        hb2 = ar.alloc(KC * TT, BF16).rearrange("p (c t) -> p c t", t=TT)
        lnt2 = ar.alloc(TT, F32)
        xsq2 = [lnt2.bitcast(BF16)[:, 0:TT], lnt2.bitcast(BF16)[:, TT:2 * TT]]
        NXSQ2 = 2
        rstd2 = ar.alloc(TT, F32)
        actT = ar.alloc(FC * TT, BF16).rearrange("p (c t) -> p c t", t=TT)
        sil = [ar.alloc(TT, F32) for _ in range(2)]
        p2_end = ar.off
        wg_v = w_gate.rearrange("(c p) n -> p c n", p=128)
        wu_v = w_up.rearrange("(c p) n -> p c n", p=128)
        wd_v = w_down.rearrange("(c p) n -> p c n", p=128)
        B_wg = [Buf(f"wg{i}") for i in range(4)]
        B_wu = [Buf(f"wu{i}") for i in range(4)]
        B_wd = [Buf(f"wd{i}") for i in range(4)]

        def emit_w2_block(bi, extra_writes=()):
            f0, f1 = FBLK[bi]
            S_.dma("pool", f"g{bi}", lambda e: e.dma_start(out=wg_blk[bi][:, :, :], in_=wg_v[:, :, f0 * 128:f1 * 128]),
                   writes=[B_wg[bi]] + list(extra_writes))
            S_.dma("pool", f"u{bi}", lambda e: e.dma_start(out=wu_blk[bi][:, :, :], in_=wu_v[:, :, f0 * 128:f1 * 128]),
                   writes=[B_wu[bi]] + list(extra_writes))

        B_w1 = Buf("w1")
        win_v = w_in.rearrange("(c p) n -> p c n", p=128)
        B_win = [Buf(f"win{i}") for i in range(4)]
        for i, (c0, c1) in enumerate([(1024, 1544), (0, 512), (512, 1024), (1544, 2056)]):
            S_.dma("pool", f"wi{i}", lambda e, c0=c0, c1=c1: e.dma_start(out=win_bf[:, :, c0:c1], in_=win_v[:, :, c0:c1]),
                   writes=[B_win[i]])
        for c in range(KC):
            S_.dma("pool", "w1", lambda e, c=c: e.dma_start(out=wout_bf[:, c, :], in_=w_out[c * 128:(c + 1) * 128, :]),
                   track=False)
        for g in range(4):
            S_.dma("pool", "w1", lambda e, g=g: e.dma_start(out=wpool_bf[:, g, :], in_=w_pool[g]), track=False)
        B_w1.w = ("w1", S_.cnt["w1"])
        B_vones = Buf("vones")
        B_uT = [Buf(f"uT{g}") for g in range(4)]
        B_uhalo = Buf("uhalo")
        B_acc = [Buf(f"acc{n}") for n in range(NJ + 1)]
        S_.op("pool", lambda e: e.memset(vst[:, :, :, 64:65], 1.0), writes=[B_vones])
        S_.op("pool", lambda e: e.memset(uT[:, :, 0:16], 0.0), writes=[B_uhalo])
        S_.op("pool", lambda e: e.memset(accs[:, 0, :], 0.0), writes=[B_acc[0]])
        B_init = Buf("init")
        S_.op("pool", lambda e: e.memset(qz[:, :, :], 0.0), writes=[B_init])
        S_.op("pool", lambda e: e.memset(CT_bf, 0.0), writes=[B_init])
        S_.op("pool", lambda e: e.memset(nr[0], 0.0), writes=[B_init])
        S_.op("pool", lambda e: e.memset(nr[1], 0.0), writes=[B_init])

        B_xt = [[Buf(f"xt{i}_{c}") for c in range(KC)] for i in range(2)]
        B_xsq = [Buf(f"xsq{i}") for i in range(NXSQ)]
        B_hb = [Buf(f"hb{c}") for c in range(KC)]
        B_rstd = Buf("rstd")
        B_ps = [Buf(f"ps{i}") for i in range(8)]
        B_qz = [Buf(f"qz{h}") for h in range(H)]
        B_kT = [[Buf(f"kT{a}_{t}") for t in range(NT)] for a in range(4)]
        B_v = [Buf(f"v{j}") for j in range(NJ)]
        B_ptmp = [Buf("ptmp0"), Buf("ptmp1")]
        B_pooled = [Buf(f"pooled{g}") for g in range(4)]
        B_mix = [Buf(f"mix{g}") for g in range(4)]
        B_attn = [Buf(f"attn{h}") for h in range(H)]
        B_pT = [Buf(f"pT{i}") for i in range(NPT)]
        B_nr, B_CT = [Buf("nr0"), Buf("nr1")], Buf("CTbf")
        B_fz = Buf("fz")
        B_lst = [Buf(f"lst{n}") for n in range(NJ)]
        B_Cst = [Buf(f"Cst{t}") for t in range(NT)]
        B_tmp16 = Buf("tmp16")

        ST_BANKS = [0, 1, 2]
        ACC_BANKS = [3, 4]
        PB_BANKS = [5, 6]
        MISC = 7
        pb_ctr = [0]

        def next_pb():
            b = PB_BANKS[pb_ctr[0] % 2]
            pb_ctr[0] += 1
            return b

        xsq_ctr = [0]

        def emit_load(tt):
            i = tt % 2
            S_.dma("sp", f"xl{i}", lambda e: e.dma_start(out=xt[i][:], in_=xT_v[:, :, tt * TT:(tt + 1) * TT]),
                   writes=B_xt[i])

        def norm_steps(xtile, B_x, hbt, B_h, gcol, misc_bank=MISC):
            steps = []
            ks = []
            for c in range(KC):
                k = xsq_ctr[0] % NXSQ
                xsq_ctr[0] += 1
                ks.append(k)

            def sq(c):
                k = ks[c]
                S_.op("pool", lambda e: e.tensor_tensor(out=xsq[k], in0=xtile[:, c, :], in1=xtile[:, c, :],
                                                        op=ALU.mult),
                      reads=[B_x[c]], writes=[B_xsq[k]])

            def mm(c):
                k = ks[c]
                S_.op("pe", lambda e: e.matmul(ps[misc_bank][:, :], ones_bf, xsq[k], start=(c == 0),
                                               stop=(c == KC - 1)),
                      reads=[B_xsq[k], B_cst], writes=[B_ps[misc_bank]])

            LAG = 2
            for c in range(KC + LAG):
                def st(c=c):
                    if c < KC:
                        sq(c)
                    if c - LAG >= 0:
                        mm(c - LAG)
                steps.append(st)

            def rs():
                S_.op("act", lambda e: e.activation(out=lnt, in_=ps[misc_bank][:, :], func=AF.Ln, bias=eps_t[:, 0:1],
                                                    scale=1.0 / D),
                      reads=[B_ps[misc_bank], B_cst], writes=[B_h[6], B_h[7]])
                S_.op("act", lambda e: e.activation(out=rstd, in_=lnt, func=AF.Exp, scale=-0.5),
                      reads=[B_h[6], B_h[7]], writes=[B_rstd])
            steps.append(rs)
            for c in range(KC):
                def hbs(c=c):
                    S_.op("dve", lambda e: e.scalar_tensor_tensor(out=hbt[:, c, :], in0=xtile[:, c, :],
                                                                  scalar=gcol[:, c:c + 1], in1=rstd,
                                                                  op0=ALU.mult, op1=ALU.mult),
                          reads=[B_x[c], B_rstd, B_cst], writes=[B_h[c]])
                steps.append(hbs)
            return steps

        evac_ctr = [0]

        def evac_copy(out_ap, in_ap, reads, writes, scale=None):
            k = evac_ctr[0]
            evac_ctr[0] += 1
            if k % 2 == 0:
                if scale is None:
                    S_.op("act", lambda e: e.activation(out=out_ap, in_=in_ap, func=AF.Copy), reads=reads, writes=writes)
                else:
                    S_.op("act", lambda e: e.activation(out=out_ap, in_=in_ap, func=AF.Copy, scale=scale),
                          reads=reads, writes=writes)
            else:
                if scale is None:
                    S_.op("dve", lambda e: e.tensor_copy(out=out_ap, in_=in_ap), reads=reads, writes=writes)
                else:
                    S_.op("dve", lambda e: e.tensor_scalar(out=out_ap, in0=in_ap, scalar1=scale, scalar2=None,
                                                           op0=ALU.mult), reads=reads, writes=writes)

        def emit_vf(tt):
            for blk in range(4):
                b = next_pb()
                j = tt * 4 + blk

                def mmv(e, blk=blk, b=b):
                    ins = None
                    for c in range(KC):
                        lhsT = hb[:, c, blk * 128:(blk + 1) * 128]
                        e.matmul(ps[b][:, :], lhsT, win_bf[:, c, 1024:1536], start=(c == 0), stop=(c == KC - 1))
                        ins = e.matmul(ps[MISC][:, blk * 8:(blk + 1) * 8], lhsT, win_bf[:, c, 1536:1544],
                                       start=(c == 0), stop=(c == KC - 1))
                    return ins

                S_.op("pe", mmv, reads=B_hb + [B_win[0]], writes=[B_ps[b], B_ps[MISC]])
                evac_copy(vst[:, j, :, 0:64], ps[b][:, :].rearrange("p (h d) -> p h d", d=64), [B_ps[b]], [B_v[j]])

        def emit_qku(tt):
            specs = []
            for a in range(4):
                specs.append(("q", a, a * 128))
            for a in range(4):
                specs.append(("k", a, 512 + a * 128))
            for g in range(4):
                specs.append(("u", g, 1544 + g * 128))
            for kind, idx, col in specs:
                b = next_pb()

                def mm(e, col=col, b=b):
                    ins = None
                    for c in range(KC):
                        ins = e.matmul(ps[b][:, :], win_bf[:, c, col:col + 128], hb[:, c, :], start=(c == 0),
                                       stop=(c == KC - 1))
                    return ins

                wb = {"q": B_win[1], "k": B_win[2], "u": B_win[3]}[kind]
                S_.op("pe", mm, reads=B_hb + [wb], writes=[B_ps[b]])
                if kind == "q":
                    evac_copy(qz[0:64, 2 * idx, :], ps[b][0:64, :], [B_ps[b], B_init], [B_qz[2 * idx]], scale=0.125)
                    evac_copy(qz[64:128, 2 * idx + 1, :], ps[b][64:128, :], [B_ps[b], B_init], [B_qz[2 * idx + 1]],
                              scale=0.125)
                elif kind == "k":
                    evac_copy(kT[:, idx, tt * TT:(tt + 1) * TT], ps[b][:, :], [B_ps[b]], [B_kT[idx][tt]])
                else:
                    evac_copy(uT[:, idx, 16:UW], ps[b][:, :], [B_ps[b]], [B_uT[idx]])

        def emit_fchain_a(tt):
            n0 = tt * 4
            S_.op("dve", lambda e: e.tensor_tensor(out=fz, in0=ps[MISC][:, 0:32], in1=bfor, op=ALU.add),
                  reads=[B_ps[MISC], B_cst], writes=[B_fz])
            S_.op("act", lambda e: e.activation(out=fz, in_=fz, func=AF.Exp, scale=-1.0), reads=[B_fz], writes=[B_fz])
            lview = lst[:, n0:n0 + 4, :].rearrange("p n h -> p (n h)")
            S_.op("act", lambda e: e.activation(out=lview, in_=fz, func=AF.Ln, bias=1.0, scale=1.0),
                  reads=[B_fz], writes=[B_lst[n0 + i] for i in range(4)])
            for i in range(4):
                n = n0 + i
                S_.op("dve", lambda e, n=n: e.tensor_tensor(out=accs[:, n + 1, :], in0=accs[:, n, :], in1=lst[:, n, :],
                                                            op=ALU.add),
                      reads=[B_acc[n], B_lst[n]], writes=[B_acc[n + 1]])

        def emit_fchain_b(tt):
            n0 = tt * 4
            bct = next_pb()

            def mmc(e):
                ins = None
                for i in range(4):
                    n = n0 + i
                    e.matmul(ps[MISC][:, 32 + i * 8:40 + i * 8], tri_f, lst[:, n, :], start=True, stop=False)
                    e.matmul(ps[MISC][:, 32 + i * 8:40 + i * 8], ones_f, accs[:, n, :], start=False, stop=True)
                for i in range(4):
                    n = n0 + i
                    e.matmul(ps[bct][0:8, i * 128:(i + 1) * 128], lst[:, n, :], tri_f, start=True, stop=False)
                    ins = e.matmul(ps[bct][0:8, i * 128:(i + 1) * 128], accs[:, n, :], ones_f, start=False, stop=True)
                return ins

            S_.op("pe", mmc, reads=[B_lst[n0 + i] for i in range(4)] + [B_acc[n0 + i] for i in range(4)] + [B_cst],
                  writes=[B_ps[MISC], B_ps[bct]])
            S_.op("dve", lambda e: e.tensor_copy(out=Cst[:, n0:n0 + 4, :].rearrange("p n h -> p (n h)"),
                                                 in_=ps[MISC][:, 32:64]),
                  reads=[B_ps[MISC]], writes=[B_Cst[tt]])
            S_.op("dve", lambda e: e.tensor_copy(out=CT_bf[0:8, :], in_=ps[bct][0:8, :]),
                  reads=[B_ps[bct], B_init], writes=[B_CT])

        def pool_sched(tt):
            sched = {}
            offs = [0, 1, 6, 12, 19]

            def chain(g):
                w = 2 << g
                src = uT[:, g, :]
                srcB = [B_uT[g], B_uhalo]
                shift, lo, step = 1, 0, 0
                while shift < w:
                    dst = ptmp[step % 2]
                    nlo = lo + shift
                    S_.op("pool", lambda e, dst=dst, src=src, nlo=nlo, shift=shift:
                          e.tensor_tensor(out=dst[:, nlo:UW], in0=src[:, nlo:UW], in1=src[:, nlo - shift:UW - shift],
                                          op=ALU.add),
                          reads=srcB, writes=[B_ptmp[step % 2]])
                    src = dst
                    srcB = [B_ptmp[step % 2]]
                    lo = nlo
                    shift *= 2
                    step += 1
                return src, srcB

            res = {}

            def do_chain(g):
                res[g] = chain(g)

            def do_final(g):
                w = 2 << g
                src, srcB = res[g]
                S_.op("dve", lambda e: e.scalar_tensor_tensor(
                    out=pooledT[:, g, :], in0=src[:, 16:UW], scalar=1.0 / w, in1=uT[:, g, 16:UW],
                    op0=ALU.mult, op1=ALU.subtract),
                    reads=srcB + [B_uT[g]], writes=[B_pooled[g]])
                if tt == 0:
                    S_.op("dve", lambda e: e.tensor_tensor(out=tmp16, in0=src[:, 16:32], in1=invcnt[:, g, :],
                                                           op=ALU.mult),
                          reads=srcB + [B_cst], writes=[B_tmp16])
                    S_.op("dve", lambda e: e.tensor_tensor(out=pooledT[:, g, 0:16], in0=tmp16, in1=uT[:, g, 16:32],
                                                           op=ALU.subtract),
                          reads=[B_tmp16, B_uT[g]], writes=[B_pooled[g]])

            def halo():
                S_.op("pool", lambda e: e.tensor_copy(out=uT[:, :, 0:16], in_=uT[:, :, TT:UW]),
                      reads=B_uT, writes=[B_uhalo])

            sched.setdefault(offs[0], []).append(lambda: do_chain(0))
            for g in range(4):
                sched.setdefault(offs[g + 1], []).append(lambda g=g: do_final(g))
                if g + 1 < 4:
                    sched.setdefault(offs[g + 1], []).append(lambda g=g: do_chain(g + 1))
            sched.setdefault(offs[4], []).append(halo)
            return sched

        def emit_poolmix(tt):
            for g in range(4):
                b = next_pb()
                S_.op("pe", lambda e, g=g, b=b: e.matmul(ps[b][:, :], wpool_bf[:, g, :], pooledT[:, g, :], start=True,
                                                         stop=True),
                      reads=[B_pooled[g], B_w1], writes=[B_ps[b]])
                S_.op("act", lambda e, g=g, b=b: e.activation(out=mixT[:, g, :], in_=ps[b][:, :], func=AF.Copy,
                                                              scale=pscale[:, g:g + 1]),
                      reads=[B_ps[b], B_cst], writes=[B_mix[g]])

        deferred = []

        def flush_deferred(now=None):
            keep = []
            for due, fn in deferred:
                if now is None or due <= now:
                    fn()
                else:
                    keep.append((due, fn))
            deferred[:] = keep

        def emit_attention(tt, steps, psched):
            nb = 4 * tt + 4
            blocks = [(h, j) for h in range(H) for j in range(nb)]
            LOOK = 2
            state = {}
            steps = list(steps)

            def qk(n):
                h, j = blocks[n]
                a = h // 2
                diag = j >= 4 * tt
                c0 = 128 * (j - 4 * tt) if diag else 0
                sb = ST_BANKS[n % 3]
                pi = n % NPT

                def mm(e):
                    e.matmul(ps[sb][:, c0:TT], kT[:, a, j * 128:(j + 1) * 128], qz[:, h, c0:TT],
                             start=True, stop=False)
                    ins = e.matmul(ps[sb][:, c0:TT], selneg[:, h, :], CT_bf[:, c0:TT], start=False,
                                   stop=(not diag))
                    if diag:
                        ins = e.matmul(ps[sb][:, c0:c0 + 128], ident_bf, maskb, start=False, stop=True)
                    return ins

                S_.op("pe", mm, reads=[B_kT[a][j // 4], B_qz[h], B_CT, B_cst], writes=[B_ps[sb]])
                S_.op("act", lambda e: e.activation(out=pT[pi][:, c0:TT], in_=ps[sb][:, c0:TT], func=AF.Exp,
                                                    bias=Cst[:, j, h:h + 1], scale=1.0),
                      reads=[B_ps[sb], B_Cst[j // 4]], writes=[B_pT[pi]])
                state[n] = (c0, pi)

            def pv(n, it):
                h, j = blocks[n]
                a, p0 = h // 2, 64 * (h % 2)
                c0, pi = state.pop(n)
                ab = ACC_BANKS[h % 2]
                S_.op("pe", lambda e: e.matmul(ps[ab][0:65, c0:TT], vst[:, j, h, :], pT[pi][:, c0:TT], start=(j == 0),
                                               stop=(j == nb - 1)),
                      reads=[B_pT[pi], B_v[j], B_vones], writes=[B_ps[ab]])
                if j == nb - 1:
                    nrb = nr[h % 2]
                    Bn = B_nr[h % 2]
                    S_.op("dve", lambda e: e.tensor_copy(out=nrb[0:64, :], in_=ps[ab][0:64, :]),
                          reads=[B_ps[ab], B_init], writes=[Bn])
                    S_.op("dve", lambda e: e.reciprocal(out=nrb[64:65, :], in_=ps[ab][64:65, :]),
                          reads=[B_ps[ab], B_init], writes=[Bn])

                    def fin():
                        b = next_pb()
                        S_.op("pe", lambda e: e.matmul(ps[b][:, :], e64_f, nrb, start=True, stop=True),
                              reads=[Bn, B_cst], writes=[B_ps[b]])
                        S_.op("dve", lambda e: e.tensor_tensor(out=attnT[p0:p0 + 64, a, :], in0=nrb[0:64, :],
                                                               in1=ps[b][0:64, :], op=ALU.mult),
                              reads=[Bn, B_ps[b]], writes=[B_attn[h]])
                    deferred.append((it + 7, fin))

            nblk = len(blocks)
            start_steps = max(2, nblk // 8)
            for n in range(nblk + LOOK):
                if n < nblk:
                    qk(n)
                flush_deferred(n)
                if n - LOOK >= 0:
                    pv(n - LOOK, n)
                if n >= start_steps and steps:
                    steps.pop(0)()
                for fn in psched.pop(n - 3, []):
                    fn()
            for st in steps:
                st()
            for k in sorted(psched):
                for fn in psched[k]:
                    fn()

        def emit_wout(tt):
            i = tt % 2
            for m in range(KC):
                b = next_pb()

                def mm(e, m=m, b=b):
                    ins = None
                    for ec in range(8):
                        rhs = attnT[:, ec, :] if ec < 4 else mixT[:, ec - 4, :]
                        ins = e.matmul(ps[b][:, :], wout_bf[:, ec, m * 128:(m + 1) * 128], rhs, start=(ec == 0),
                                       stop=(ec == 7))
                    return ins

                S_.op("pe", mm, reads=B_attn + B_mix + [B_w1], writes=[B_ps[b]])
                S_.op("dve", lambda e, m=m, b=b: e.tensor_tensor(out=xt[i][:, m, :], in0=ps[b][:, :], in1=xt[i][:, m, :],
                                                                 op=ALU.add),
                      reads=[B_ps[b], B_xt[i][m]], writes=[B_xt[i][m]])
            S_.dma("sp", f"xs{i}", lambda e: e.dma_start(out=x1_v[:, :, tt * TT:(tt + 1) * TT], in_=xt[i][:]),
                   reads=B_xt[i])

        emit_load(0)
        for st in norm_steps(xt[0], B_xt[0], hb, B_hb, g1):
            st()
        emit_vf(0)
        emit_fchain_a(0)
        emit_qku(0)
        emit_fchain_b(0)
        for tt in range(NT):
            steps = []
            if tt + 1 < NT:
                emit_load(tt + 1)
                steps = norm_steps(xt[(tt + 1) % 2], B_xt[(tt + 1) % 2], hb, B_hb, g1)
            emit_attention(tt, steps, pool_sched(tt))
            emit_poolmix(tt)
            if tt + 1 < NT:
                emit_vf(tt + 1)
                emit_fchain_a(tt + 1)
            flush_deferred()
            emit_wout(tt)
            if tt + 1 < NT:
                emit_qku(tt + 1)
                emit_fchain_b(tt + 1)
            if tt + 1 == NT - 1 or NT == 1:
                emit_w2_block(0, extra_writes=B_win)

        S_.barrier()

        B_yt = [[Buf(f"yt{i}_{c}") for c in range(KC)] for i in range(2)]
        B_hb2 = [Buf(f"hb2_{c}") for c in range(KC)]
        B_act = [Buf(f"act{c}") for c in range(FC)]
        B_sil = [Buf("sil0"), Buf("sil1")]
        B_xsq2 = [Buf(f"xsq2_{i}") for i in range(NXSQ2)]
        B_lnt2, B_rstd2 = Buf("lnt2"), Buf("rstd2")
        B_ps2 = [Buf(f"ps2_{i}") for i in range(8)]
        GB, UB, DB, NB = [0, 1], [2, 3], [4, 5], 6

        def emit_load2(tt):
            i = tt % 2
            S_.dma("sp", f"yl{i}", lambda e: e.dma_start(out=yt[i][:], in_=x1_v[:, :, tt * TT:(tt + 1) * TT]),
                   writes=B_yt[i])

        emit_load2(0)
        for bi in (1, 2, 3):
            emit_w2_block(bi)
        for bi, (f0, f1) in enumerate(FBLK):
            S_.dma("pool", f"d{bi}", lambda e, f0=f0, f1=f1: e.dma_start(out=wd_bf[:, f0:f1, :], in_=wd_v[:, f0:f1, :]),
                   writes=[B_wd[bi]])

        def norm2_steps(xtile, B_x, out_fn, bank):
            steps = []

            def sq(c):
                k = c % NXSQ2
                S_.op("pool", lambda e: e.tensor_tensor(out=xsq2[k], in0=xtile[:, c, :], in1=xtile[:, c, :],
                                                        op=ALU.mult),
                      reads=[B_x[c]], writes=[B_xsq2[k]])

            def mm(c):
                k = c % NXSQ2
                S_.op("pe", lambda e: e.matmul(ps[bank][:, :], ones_bf, xsq2[k], start=(c == 0), stop=(c == KC - 1)),
                      reads=[B_xsq2[k], B_cst], writes=[B_ps2[bank]])

            LAG = 1
            for c in range(KC + LAG):
                def st(c=c):
                    if c < KC:
                        sq(c)
                    if c - LAG >= 0:
                        mm(c - LAG)
                steps.append(st)

            def rs():
                S_.op("act", lambda e: e.activation(out=lnt2, in_=ps[bank][:, :], func=AF.Ln, bias=eps_t[:, 0:1],
                                                    scale=1.0 / D),
                      reads=[B_ps2[bank], B_cst], writes=[B_lnt2] + B_xsq2)
                S_.op("act", lambda e: e.activation(out=rstd2, in_=lnt2, func=AF.Exp, scale=-0.5),
                      reads=[B_lnt2] + B_xsq2, writes=[B_rstd2])
            steps.append(rs)
            for c in range(KC):
                steps.append(lambda c=c: out_fn(c))
            return steps

        def hb_steps(tt):
            i = tt % 2
            xtile = yt[i]

            def mk_hb(c):
                S_.op("dve", lambda e: e.scalar_tensor_tensor(out=hb2[:, c, :], in0=xtile[:, c, :],
                                                              scalar=g2[:, c:c + 1], in1=rstd2,
                                                              op0=ALU.mult, op1=ALU.mult),
                      reads=[B_yt[i][c], B_rstd2, B_cst], writes=[B_hb2[c]])
            return norm2_steps(xtile, B_yt[i], mk_hb, NB)

        def final_steps(tt):
            i = tt % 2
            xtile = yt[i]

            def mk_y(c):
                S_.op("dve", lambda e: e.scalar_tensor_tensor(out=xtile[:, c, :], in0=xtile[:, c, :],
                                                              scalar=gf[:, c:c + 1], in1=rstd2,
                                                              op0=ALU.mult, op1=ALU.mult),
                      reads=[B_yt[i][c], B_rstd2, B_cst], writes=[B_yt[i][c]])
            steps = norm2_steps(xtile, B_yt[i], mk_y, 7)

            def store():
                S_.dma("sp", f"ys{i}", lambda e: e.dma_start(out=yT_v[:, :, tt * TT:(tt + 1) * TT], in_=xtile[:]),
                       reads=B_yt[i])
            steps.append(store)
            return steps

        def fblk_of(fc):
            for bi, (f0, f1) in enumerate(FBLK):
                if f0 <= fc < f1:
                    return bi, fc - f0
            raise AssertionError

        def emit_gateup(tt, pending):
            for fc in range(FC):
                gb, ub = GB[fc % 2], UB[fc % 2]
                bi, fo = fblk_of(fc)

                def mmg(e, bi=bi, fo=fo, gb=gb):
                    ins = None
                    for c in range(KC):
                        ins = e.matmul(ps[gb][:, :], wg_blk[bi][:, c, fo * 128:(fo + 1) * 128], hb2[:, c, :],
                                       start=(c == 0), stop=(c == KC - 1))
                    return ins

                def mmu(e, bi=bi, fo=fo, ub=ub):
                    ins = None
                    for c in range(KC):
                        ins = e.matmul(ps[ub][:, :], wu_blk[bi][:, c, fo * 128:(fo + 1) * 128], hb2[:, c, :],
                                       start=(c == 0), stop=(c == KC - 1))
                    return ins

                S_.op("pe", mmg, reads=B_hb2 + [B_wg[bi]], writes=[B_ps2[gb]])
                S_.op("pe", mmu, reads=B_hb2 + [B_wu[bi]], writes=[B_ps2[ub]])
                sl = fc % 2
                S_.op("act", lambda e, gb=gb, sl=sl: e.activation(out=sil[sl], in_=ps[gb][:, :], func=AF.Silu),
                      reads=[B_ps2[gb]], writes=[B_sil[sl]])
                S_.op("dve", lambda e, fc=fc, ub=ub, sl=sl: e.tensor_tensor(out=actT[:, fc, :], in0=ps[ub][:, :],
                                                                            in1=sil[sl], op=ALU.mult),
                      reads=[B_ps2[ub], B_sil[sl]], writes=[B_act[fc]])
                if pending and fc >= 1:
                    pending.pop(0)()
            while pending:
                pending.pop(0)()

        def emit_down(tt, pending):
            i = tt % 2
            xtile = yt[i]
            for m in range(KC):
                db = DB[m % 2]

                def mmd(e, m=m, db=db):
                    ins = None
                    for fc in range(FC):
                        ins = e.matmul(ps[db][:, :], wd_bf[:, fc, m * 128:(m + 1) * 128], actT[:, fc, :],
                                       start=(fc == 0), stop=(fc == FC - 1))
                    return ins

                S_.op("pe", mmd, reads=B_act + B_wd, writes=[B_ps2[db]])
                S_.op("dve", lambda e, m=m, db=db: e.tensor_tensor(out=xtile[:, m, :], in0=ps[db][:, :],
                                                                   in1=xtile[:, m, :], op=ALU.add),
                      reads=[B_ps2[db], B_yt[i][m]], writes=[B_yt[i][m]])
                for _ in range(3):
                    if pending:
                        pending.pop(0)()
            while pending:
                pending.pop(0)()

        for st in hb_steps(0):
            st()
        pend_final = []
        for tt in range(NT):
            emit_gateup(tt, pend_final)
            if tt + 1 < NT:
                emit_load2(tt + 1)
            emit_down(tt, hb_steps(tt + 1) if tt + 1 < NT else [])
            pend_final = final_steps(tt)
        for st in pend_final:
            st()

        S_.barrier()

        print(f"[kernel] arena words: phase1 {p1_end} phase2 {p2_end} of {AW}; "
              f"sem counts: { {k: v for k, v in S_.cnt.items()} }")

        @block.tensor
        def _(e):
            S_.replay("pe", e, semh)

        @block.scalar
        def _(e):
            S_.replay("act", e, semh)

        @block.vector
        def _(e):
            S_.replay("dve", e, semh)

        @block.gpsimd
        def _(e):
            S_.replay("pool", e, semh)

        @block.sync
        def _(e):
            S_.replay("sp", e, semh)

    return nc


def _consts(norm1_g, norm2_g, final_g, pool_scale, b_forget):
    cf = np.zeros((128, NCF), np.float32)
    cf[:, CF_G1:CF_G1 + 8] = np.asarray(norm1_g, np.float32).reshape(8, 128).T
    cf[:, CF_G2:CF_G2 + 8] = np.asarray(norm2_g, np.float32).reshape(8, 128).T
    cf[:, CF_GF:CF_GF + 8] = np.asarray(final_g, np.float32).reshape(8, 128).T
    cf[:, CF_PS:CF_PS + 4] = np.asarray(pool_scale, np.float32).reshape(4, 128).T
    cf[:, CF_BF:CF_BF + 32] = np.tile(np.asarray(b_forget, np.float32).reshape(1, 8), (128, 4))
    t = np.arange(16)
    ic = np.stack([1.0 / np.minimum(t + 1, 2 << g) for g in range(4)]).astype(np.float32)
    cf[:, CF_IC:CF_IC + 64] = ic.reshape(1, 64)
    kk = np.arange(128)
    cf[:, CF_TRI:CF_TRI + 128] = (kk[:, None] <= kk[None, :]).astype(np.float32)
    cf[64, CF_E64:CF_E64 + 128] = 1.0
    cb = np.zeros((128, NCB), np.float32)
    cb[:, CB_ID:CB_ID + 128] = np.eye(128, dtype=np.float32)
    cb[:, CB_MASK:CB_MASK + 128] = np.where(kk[:, None] <= kk[None, :], 0.0, NEG).astype(np.float32)
    for h in range(8):
        cb[h, CB_SEL + h * 128:CB_SEL + (h + 1) * 128] = -1.0
    return cf, cb


_NC_CACHE = {}


def kernel(x, norm1_g, w_in, b_forget, w_pool, pool_scale, w_out, norm2_g, w_gate, w_up, w_down, final_g):
    x = np.asarray(x, np.float32)
    B = x.shape[0]
    cf, cb = _consts(np.asarray(norm1_g)[0], np.asarray(norm2_g)[0], np.asarray(final_g), np.asarray(pool_scale)[0],
                     np.asarray(b_forget)[0])
    shared = {
        "w_in": np.ascontiguousarray(np.asarray(w_in, np.float32)[0]),
        "w_out": np.ascontiguousarray(np.asarray(w_out, np.float32)[0]),
        "w_pool": np.ascontiguousarray(np.asarray(w_pool, np.float32)[0]),
        "w_gate": np.ascontiguousarray(np.asarray(w_gate, np.float32)[0]),
        "w_up": np.ascontiguousarray(np.asarray(w_up, np.float32)[0]),
        "w_down": np.ascontiguousarray(np.asarray(w_down, np.float32)[0]),
        "cf": cf,
        "cb": cb,
    }
    in_maps = [dict(shared, xT=np.ascontiguousarray(x[b].T)) for b in range(B)]
    if "nc" not in _NC_CACHE:
        _NC_CACHE["nc"] = build_program()
    nc = _NC_CACHE["nc"]
    res = run_bass_kernel_spmd(nc, in_maps, core_ids=list(range(B)))
    out = np.stack([np.asarray(res.results[b]["yT"]).T for b in range(B)])
    return np.ascontiguousarray(out.astype(np.float32))
```

```python
from contextlib import ExitStack

import numpy as np
import concourse.bass as bass
import concourse.mybir as mybir
from concourse.bass_utils import run_bass_kernel_spmd

F32 = mybir.dt.float32
BF16 = mybir.dt.bfloat16
AF = mybir.ActivationFunctionType
ALU = mybir.AluOpType

S = 4096
D = 1024
TT = 512
NT = S // TT
KC = D // 128
H = 8
DFF = 2816
FC = DFF // 128
INW = 2056
EPS = 1e-6
NEG = -30000.0

CF_G1, CF_G2, CF_GF, CF_PS, CF_BF, CF_IC, CF_TRI = 0, 8, 16, 24, 28, 60, 124
CF_E64 = 124 + 128
NCF = 124 + 128 + 128
CB_ID, CB_MASK, CB_SEL = 0, 128, 256
NCB = 256 + 1024


class Buf:
    __slots__ = ("name", "w", "r")

    def __init__(self, name):
        self.name = name
        self.w = None
        self.r = []


class Sched:
    ENGS = ("pe", "act", "dve", "pool", "sp")

    def __init__(self):
        self.prog = {e: [] for e in self.ENGS}
        self.cnt = {}
        self.waited = {e: {} for e in self.ENGS}

    @staticmethod
    def _flat(bufs):
        out = []
        for b in bufs:
            if isinstance(b, (list, tuple)):
                out.extend(Sched._flat(b))
            else:
                out.append(b)
        return out

    def _deps(self, eng, reads, writes):
        reads, writes = self._flat(reads), self._flat(writes)
        need = {}

        def add(t):
            if t is not None and need.get(t[0], 0) < t[1]:
                need[t[0]] = t[1]

        for b in reads:
            add(b.w)
        for b in writes:
            add(b.w)
            for t in b.r:
                add(t)
        out = []
        for k, v in need.items():
            if k == eng and eng == "pe":
                continue
            if self.waited[eng].get(k, 0) >= v:
                continue
            self.waited[eng][k] = v
            out.append((k, v))
        return out

    def _commit(self, tok, reads, writes):
        reads, writes = self._flat(reads), self._flat(writes)
        for b in reads:
            b.r.append(tok)
        for b in writes:
            b.w = tok
            b.r = []

    def op(self, eng, fn, reads=(), writes=()):
        waits = self._deps(eng, reads, writes)
        self.cnt[eng] = self.cnt.get(eng, 0) + 1
        tok = (eng, self.cnt[eng])
        self.prog[eng].append((waits, fn, (eng, 1)))
        self._commit(tok, reads, writes)
        return tok

    def dma(self, queue, semkey, fn, reads=(), writes=(), track=True):
        waits = self._deps(queue, reads, writes) if track else []
        self.cnt[semkey] = self.cnt.get(semkey, 0) + 16
        tok = (semkey, self.cnt[semkey])
        self.prog[queue].append((waits, fn, (semkey, 16)))
        if track:
            self._commit(tok, reads, writes)
        return tok

    def barrier(self):
        for e in self.ENGS:
            waits = []
            for k, v in self.cnt.items():
                if k == e and e == "pe":
                    continue
                if self.waited[e].get(k, 0) >= v:
                    continue
                self.waited[e][k] = v
                waits.append((k, v))
            if waits:
                self.prog[e].append((waits, None, None))

    def replay(self, eng, e, semh):
        for waits, fn, sig in self.prog[eng]:
            for k, v in waits:
                e.wait_ge(semh[k], v)
            if fn is None:
                continue
            ins = fn(e)
            ins.then_inc(semh[sig[0]], sig[1])


class Arena:
    def __init__(self, ap):
        self.ap = ap
        self.off = 0
        self.words = ap.shape[1]

    def alloc(self, n, dtype):
        nbytes = n * (2 if dtype == BF16 else 4)
        words = (nbytes + 31) // 32 * 8
        assert self.off + words <= self.words, ("arena overflow", self.off, words, self.words)
        v = self.ap[:, self.off:self.off + words]
        self.off += words
        if dtype == BF16:
            return v.bitcast(BF16)[:, 0:n]
        return v[:, 0:n]


def build_program(debug=False):
    nc = bass.Bass("TRN2", target_bir_lowering=False)
    dt_in = lambda name, shape: nc.dram_tensor(name, shape, F32, kind="ExternalInput").ap()
    xT = dt_in("xT", [D, S])
    w_in = dt_in("w_in", [D, INW])
    w_out = dt_in("w_out", [D, D])
    w_pool = dt_in("w_pool", [4, 128, 128])
    w_gate = dt_in("w_gate", [D, DFF])
    w_up = dt_in("w_up", [D, DFF])
    w_down = dt_in("w_down", [DFF, D])
    cf_d = dt_in("cf", [128, NCF])
    cb_d = dt_in("cb", [128, NCB])
    x1s = nc.dram_tensor("x1s", [D, S], F32, kind=("ExternalOutput" if debug else "Internal")).ap()
    yT = nc.dram_tensor("yT", [D, S], F32, kind="ExternalOutput").ap()

    xT_v = xT.rearrange("(c p) t -> p c t", p=128)
    x1_v = x1s.rearrange("(c p) t -> p c t", p=128)
    yT_v = yT.rearrange("(c p) t -> p c t", p=128)

    S_ = Sched()
    semnames = ["pe", "act", "dve", "pool", "cst", "cst2", "wi0", "wi1", "wi2", "wi3", "w1", "g0", "g1", "g2", "g3", "u0", "u1", "u2", "u3", "d0", "d1", "d2", "d3", "xl0", "xl1", "xs0", "xs1",
                "yl0", "yl1", "ys0", "ys1"]

    with ExitStack() as ctx:
        AW = 53100
        arena_t = ctx.enter_context(nc.sbuf_tensor("arena", [128, AW], F32))
        ps = [ctx.enter_context(nc.psum_tensor(f"ps{i}", [128, 512], F32)) for i in range(8)]
        semh = {n: ctx.enter_context(nc.semaphore(n)) for n in semnames}
        block = ctx.enter_context(nc.Block())

        ar = Arena(arena_t[:, :])
        cf = ar.alloc(NCF, F32)
        cb = ar.alloc(NCB, BF16)
        ones_bf = ar.alloc(128, BF16)
        ones_f = ar.alloc(128, F32)
        eps_t = ar.alloc(8, F32)
        xt = [ar.alloc(KC * TT, F32).rearrange("p (c t) -> p c t", t=TT) for _ in range(2)]
        base_off = ar.off
        B_cf, B_cb, B_ones = Buf("cf"), Buf("cb"), Buf("ones")
        B_cst = [B_cf, B_cb, B_ones]

        g1 = cf[:, CF_G1:CF_G1 + 8]
        g2 = cf[:, CF_G2:CF_G2 + 8]
        gf = cf[:, CF_GF:CF_GF + 8]
        pscale = cf[:, CF_PS:CF_PS + 4]
        bfor = cf[:, CF_BF:CF_BF + 32]
        invcnt = cf[:, CF_IC:CF_IC + 64].rearrange("p (g t) -> p g t", t=16)
        tri_f = cf[:, CF_TRI:CF_TRI + 128]
        e64_f = cf[:, CF_E64:CF_E64 + 128]
        ident_bf = cb[:, CB_ID:CB_ID + 128]
        maskb = cb[:, CB_MASK:CB_MASK + 128]
        selneg = cb[:, CB_SEL:CB_SEL + 1024].rearrange("p (h m) -> p h m", m=128)

        S_.dma("sp", "cst", lambda e: e.dma_start(out=cf, in_=cf_d), writes=[B_cf])
        S_.dma("pool", "cst2", lambda e: e.dma_start(out=cb, in_=cb_d), writes=[B_cb])
        B_xt = [[Buf(f"xt{i}_{c}") for c in range(KC)] for i in range(2)]
        S_.dma("sp", "xl0", lambda e: e.dma_start(out=xt[0][:], in_=xT_v[:, :, 0:TT]), writes=B_xt[0])
        S_.op("pool", lambda e: e.memset(ones_bf, 1.0), writes=[B_ones])
        S_.op("pool", lambda e: e.memset(ones_f, 1.0), writes=[B_ones])
        S_.op("pool", lambda e: e.memset(eps_t, EPS), writes=[B_ones])

        win_bf = ar.alloc(KC * INW, BF16).rearrange("p (c n) -> p c n", n=INW)
        wout_bf = ar.alloc(KC * D, BF16).rearrange("p (c n) -> p c n", n=D)
        wpool_bf = ar.alloc(4 * 128, BF16).rearrange("p (g n) -> p g n", n=128)
        kT = ar.alloc(4 * S, BF16).rearrange("p (a t) -> p a t", t=S)
        NJ = S // 128
        vst = ar.alloc(NJ * H * 65, BF16).rearrange("p (j h d) -> p j h d", h=H, d=65)
        NXSQ = 3
        xsq = [ar.alloc(TT, BF16) for _ in range(NXSQ)]
        hb2d = ar.alloc(KC * TT, BF16)
        hb = hb2d.rearrange("p (c t) -> p c t", t=TT)
        lnt = hb2d[:, 6 * TT:8 * TT].bitcast(F32)
        rstd = ar.alloc(TT, F32)
        qz = ar.alloc(H * TT, BF16).rearrange("p (h t) -> p h t", t=TT)
        UW = TT + 16
        uT = ar.alloc(4 * UW, F32).rearrange("p (g t) -> p g t", t=UW)
        ptmp = [ar.alloc(UW, F32) for _ in range(2)]
        pooledT = ar.alloc(4 * TT, BF16).rearrange("p (g t) -> p g t", t=TT)
        mixT = ar.alloc(4 * TT, BF16).rearrange("p (g t) -> p g t", t=TT)
        attnT = ar.alloc(4 * TT, BF16).rearrange("p (a t) -> p a t", t=TT)
        NPT = 3
        pT = [ar.alloc(TT, BF16) for _ in range(NPT)]
        nr = [ar.alloc(TT, F32) for _ in range(2)]
        CT_bf = ar.alloc(TT, BF16)
        fz = ar.alloc(32, F32)
        lst = ar.alloc(NJ * H, F32).rearrange("p (n h) -> p n h", h=H)
        accs = ar.alloc((NJ + 1) * H, F32).rearrange("p (n h) -> p n h", h=H)
        Cst = ar.alloc(NJ * H, F32).rearrange("p (n h) -> p n h", h=H)
        tmp16 = ar.alloc(16, F32)
        p1_end = ar.off

        ar.off = base_off
        FBLK = [(0, 6), (6, 12), (12, 17), (17, 22)]
        wg_blk, wu_blk = [None] * 4, [None] * 4
        for bi in (0,):
            f0, f1 = FBLK[bi]
            wg_blk[bi] = ar.alloc(KC * (f1 - f0) * 128, BF16).rearrange("p (c n) -> p c n", c=KC)
            wu_blk[bi] = ar.alloc(KC * (f1 - f0) * 128, BF16).rearrange("p (c n) -> p c n", c=KC)
        assert ar.off <= base_off + (KC * INW * 2 + 3) // 4, "prefetch blocks must fit in the w_in region"
        for bi in (1, 2, 3):
            f0, f1 = FBLK[bi]
            wg_blk[bi] = ar.alloc(KC * (f1 - f0) * 128, BF16).rearrange("p (c n) -> p c n", c=KC)
            wu_blk[bi] = ar.alloc(KC * (f1 - f0) * 128, BF16).rearrange("p (c n) -> p c n", c=KC)
        wd_bf = ar.alloc(FC * D, BF16).rearrange("p (c n) -> p c n", n=D)
        yt = xt
        hb2 = ar.alloc(KC * TT, BF16).rearrange("p (c t) -> p c t", t=TT)
        lnt2 = ar.alloc(TT, F32)
        xsq2 = [lnt2.bitcast(BF16)[:, 0:TT], lnt2.bitcast(BF16)[:, TT:2 * TT]]
        NXSQ2 = 2
        rstd2 = ar.alloc(TT, F32)
        actT = ar.alloc(FC * TT, BF16).rearrange("p (c t) -> p c t", t=TT)
        sil = [ar.alloc(TT, F32) for _ in range(2)]
        p2_end = ar.off
        wg_v = w_gate.rearrange("(c p) n -> p c n", p=128)
        wu_v = w_up.rearrange("(c p) n -> p c n", p=128)
        wd_v = w_down.rearrange("(c p) n -> p c n", p=128)
        B_wg = [Buf(f"wg{i}") for i in range(4)]
        B_wu = [Buf(f"wu{i}") for i in range(4)]
        B_wd = [Buf(f"wd{i}") for i in range(4)]

        def emit_w2_block(bi, extra_writes=()):
            f0, f1 = FBLK[bi]
            S_.dma("pool", f"g{bi}", lambda e: e.dma_start(out=wg_blk[bi][:, :, :], in_=wg_v[:, :, f0 * 128:f1 * 128]),
                   writes=[B_wg[bi]] + list(extra_writes))
            S_.dma("pool", f"u{bi}", lambda e: e.dma_start(out=wu_blk[bi][:, :, :], in_=wu_v[:, :, f0 * 128:f1 * 128]),
                   writes=[B_wu[bi]] + list(extra_writes))

        B_w1 = Buf("w1")
        win_v = w_in.rearrange("(c p) n -> p c n", p=128)
        B_win = [Buf(f"win{i}") for i in range(4)]
        for i, (c0, c1) in enumerate([(1024, 1544), (0, 512), (512, 1024), (1544, 2056)]):
            S_.dma("pool", f"wi{i}", lambda e, c0=c0, c1=c1: e.dma_start(out=win_bf[:, :, c0:c1], in_=win_v[:, :, c0:c1]),
                   writes=[B_win[i]])
        for c in range(KC):
            S_.dma("pool", "w1", lambda e, c=c: e.dma_start(out=wout_bf[:, c, :], in_=w_out[c * 128:(c + 1) * 128, :]),
                   track=False)
        for g in range(4):
            S_.dma("pool", "w1", lambda e, g=g: e.dma_start(out=wpool_bf[:, g, :], in_=w_pool[g]), track=False)
        B_w1.w = ("w1", S_.cnt["w1"])
        B_xsq = [Buf(f"xsq{i}") for i in range(NXSQ)]
        B_hb = [Buf(f"hb{c}") for c in range(KC)]
        B_rstd = Buf("rstd")
        B_ps = [Buf(f"ps{i}") for i in range(8)]
        B_qz = [Buf(f"qz{h}") for h in range(H)]
        B_kT = [[Buf(f"kT{a}_{t}") for t in range(NT)] for a in range(4)]
        B_v = [Buf(f"v{j}") for j in range(NJ)]
        B_ptmp = [Buf("ptmp0"), Buf("ptmp1")]
        B_pooled = [Buf(f"pooled{g}") for g in range(4)]
        B_mix = [Buf(f"mix{g}") for g in range(4)]
        B_attn = [Buf(f"attn{h}") for h in range(H)]
        B_pT = [Buf(f"pT{i}") for i in range(NPT)]
        B_nr, B_CT = [Buf("nr0"), Buf("nr1")], Buf("CTbf")
        B_fz = Buf("fz")
        B_lst = [Buf(f"lst{n}") for n in range(NJ)]
        B_Cst = [Buf(f"Cst{t}") for t in range(NT)]
        B_tmp16 = Buf("tmp16")

        ST_BANKS = [0, 1, 2]
        ACC_BANKS = [3, 4]
        PB_BANKS = [5, 6]
        MISC = 7
        pb_ctr = [0]

        def next_pb():
            b = PB_BANKS[pb_ctr[0] % 2]
            pb_ctr[0] += 1
            return b

        xsq_ctr = [0]

        B_vones = Buf("vones")
        B_uT = [Buf(f"uT{g}") for g in range(4)]
        B_uhalo = Buf("uhalo")
        B_acc = [Buf(f"acc{n}") for n in range(NJ + 1)]
        S_.op("pool", lambda e: e.memset(vst[:, :, :, 64:65], 1.0), writes=[B_vones])
        S_.op("pool", lambda e: e.memset(uT[:, :, 0:16], 0.0), writes=[B_uhalo])
        S_.op("pool", lambda e: e.memset(accs[:, 0, :], 0.0), writes=[B_acc[0]])
        B_init = Buf("init")
        S_.op("pool", lambda e: e.memset(qz[:, :, :], 0.0), writes=[B_init])
        S_.op("pool", lambda e: e.memset(CT_bf, 0.0), writes=[B_init])
        S_.op("pool", lambda e: e.memset(nr[0], 0.0), writes=[B_init])
        S_.op("pool", lambda e: e.memset(nr[1], 0.0), writes=[B_init])

        def emit_load(tt):
            i = tt % 2
            S_.dma("sp", f"xl{i}", lambda e: e.dma_start(out=xt[i][:], in_=xT_v[:, :, tt * TT:(tt + 1) * TT]),
                   writes=B_xt[i])

        def norm_steps(xtile, B_x, hbt, B_h, gcol, misc_bank=MISC, sq_eng="pool"):
            steps = []
            ks = []
            for c in range(KC):
                k = xsq_ctr[0] % NXSQ
                xsq_ctr[0] += 1
                ks.append(k)

            def sq(c):
                k = ks[c]
                if sq_eng == "act":
                    S_.op("act", lambda e: e.activation(out=xsq[k], in_=xtile[:, c, :], func=AF.Square),
                          reads=[B_x[c]], writes=[B_xsq[k]])
                else:
                    S_.op("pool", lambda e: e.tensor_tensor(out=xsq[k], in0=xtile[:, c, :], in1=xtile[:, c, :],
                                                            op=ALU.mult),
                          reads=[B_x[c]], writes=[B_xsq[k]])

            def mm(c):
                k = ks[c]
                S_.op("pe", lambda e: e.matmul(ps[misc_bank][:, :], ones_bf, xsq[k], start=(c == 0),
                                               stop=(c == KC - 1)),
                      reads=[B_xsq[k], B_cst], writes=[B_ps[misc_bank]])

            LAG = 2
            for c in range(KC + LAG):
                def st(c=c):
                    if c < KC:
                        sq(c)
                    if c - LAG >= 0:
                        mm(c - LAG)
                steps.append(st)

            def rs():
                S_.op("act", lambda e: e.activation(out=lnt, in_=ps[misc_bank][:, :], func=AF.Ln, bias=eps_t[:, 0:1],
                                                    scale=1.0 / D),
                      reads=[B_ps[misc_bank], B_cst], writes=[B_h[6], B_h[7]])
                S_.op("act", lambda e: e.activation(out=rstd, in_=lnt, func=AF.Exp, scale=-0.5),
                      reads=[B_h[6], B_h[7]], writes=[B_rstd])
            steps.append(rs)
            for c in range(KC):
                def hbs(c=c):
                    S_.op("dve", lambda e: e.scalar_tensor_tensor(out=hbt[:, c, :], in0=xtile[:, c, :],
                                                                  scalar=gcol[:, c:c + 1], in1=rstd,
                                                                  op0=ALU.mult, op1=ALU.mult),
                          reads=[B_x[c], B_rstd, B_cst], writes=[B_h[c]])
                steps.append(hbs)
            return steps

        evac_ctr = [0]

        def evac_copy(out_ap, in_ap, reads, writes, scale=None):
            k = evac_ctr[0]
            evac_ctr[0] += 1
            if k % 2 == 0:
                if scale is None:
                    S_.op("act", lambda e: e.activation(out=out_ap, in_=in_ap, func=AF.Copy), reads=reads, writes=writes)
                else:
                    S_.op("act", lambda e: e.activation(out=out_ap, in_=in_ap, func=AF.Copy, scale=scale),
                          reads=reads, writes=writes)
            else:
                if scale is None:
                    S_.op("dve", lambda e: e.tensor_copy(out=out_ap, in_=in_ap), reads=reads, writes=writes)
                else:
                    S_.op("dve", lambda e: e.tensor_scalar(out=out_ap, in0=in_ap, scalar1=scale, scalar2=None,
                                                           op0=ALU.mult), reads=reads, writes=writes)

        def emit_vf(tt):
            for blk in range(4):
                b = next_pb()
                j = tt * 4 + blk

                def mmv(e, blk=blk, b=b):
                    ins = None
                    for c in range(KC):
                        lhsT = hb[:, c, blk * 128:(blk + 1) * 128]
                        e.matmul(ps[b][:, :], lhsT, win_bf[:, c, 1024:1536], start=(c == 0), stop=(c == KC - 1))
                        ins = e.matmul(ps[MISC][:, blk * 8:(blk + 1) * 8], lhsT, win_bf[:, c, 1536:1544],
                                       start=(c == 0), stop=(c == KC - 1))
                    return ins

                S_.op("pe", mmv, reads=B_hb + [B_win[0]], writes=[B_ps[b], B_ps[MISC]])
                evac_copy(vst[:, j, :, 0:64], ps[b][:, :].rearrange("p (h d) -> p h d", d=64), [B_ps[b]], [B_v[j]])

        def emit_qku(tt):
            specs = []
            for a in range(4):
                specs.append(("q", a, a * 128))
            for a in range(4):
                specs.append(("k", a, 512 + a * 128))
            for g in range(4):
                specs.append(("u", g, 1544 + g * 128))
            for kind, idx, col in specs:
                b = next_pb()

                def mm(e, col=col, b=b):
                    ins = None
                    for c in range(KC):
                        ins = e.matmul(ps[b][:, :], win_bf[:, c, col:col + 128], hb[:, c, :], start=(c == 0),
                                       stop=(c == KC - 1))
                    return ins

                wb = {"q": B_win[1], "k": B_win[2], "u": B_win[3]}[kind]
                S_.op("pe", mm, reads=B_hb + [wb], writes=[B_ps[b]])
                if kind == "q":
                    evac_copy(qz[0:64, 2 * idx, :], ps[b][0:64, :], [B_ps[b], B_init], [B_qz[2 * idx]], scale=0.125)
                    evac_copy(qz[64:128, 2 * idx + 1, :], ps[b][64:128, :], [B_ps[b], B_init], [B_qz[2 * idx + 1]],
                              scale=0.125)
                elif kind == "k":
                    evac_copy(kT[:, idx, tt * TT:(tt + 1) * TT], ps[b][:, :], [B_ps[b]], [B_kT[idx][tt]])
                else:
                    evac_copy(uT[:, idx, 16:UW], ps[b][:, :], [B_ps[b]], [B_uT[idx]])

        def emit_fchain_a(tt):
            n0 = tt * 4
            S_.op("dve", lambda e: e.tensor_tensor(out=fz, in0=ps[MISC][:, 0:32], in1=bfor, op=ALU.add),
                  reads=[B_ps[MISC], B_cst], writes=[B_fz])
            S_.op("act", lambda e: e.activation(out=fz, in_=fz, func=AF.Exp, scale=-1.0), reads=[B_fz], writes=[B_fz])
            lview = lst[:, n0:n0 + 4, :].rearrange("p n h -> p (n h)")
            S_.op("act", lambda e: e.activation(out=lview, in_=fz, func=AF.Ln, bias=1.0, scale=1.0),
                  reads=[B_fz], writes=[B_lst[n0 + i] for i in range(4)])
            for i in range(4):
                n = n0 + i
                S_.op("dve", lambda e, n=n: e.tensor_tensor(out=accs[:, n + 1, :], in0=accs[:, n, :], in1=lst[:, n, :],
                                                            op=ALU.add),
                      reads=[B_acc[n], B_lst[n]], writes=[B_acc[n + 1]])

        def emit_fchain_b(tt):
            n0 = tt * 4
            bct = next_pb()

            def mmc(e):
                ins = None
                for i in range(4):
                    n = n0 + i
                    e.matmul(ps[MISC][:, 32 + i * 8:40 + i * 8], tri_f, lst[:, n, :], start=True, stop=False)
                    e.matmul(ps[MISC][:, 32 + i * 8:40 + i * 8], ones_f, accs[:, n, :], start=False, stop=True)
                for i in range(4):
                    n = n0 + i
                    e.matmul(ps[bct][0:8, i * 128:(i + 1) * 128], lst[:, n, :], tri_f, start=True, stop=False)
                    ins = e.matmul(ps[bct][0:8, i * 128:(i + 1) * 128], accs[:, n, :], ones_f, start=False, stop=True)
                return ins

            S_.op("pe", mmc, reads=[B_lst[n0 + i] for i in range(4)] + [B_acc[n0 + i] for i in range(4)] + [B_cst],
                  writes=[B_ps[MISC], B_ps[bct]])
            S_.op("dve", lambda e: e.tensor_copy(out=Cst[:, n0:n0 + 4, :].rearrange("p n h -> p (n h)"),
                                                 in_=ps[MISC][:, 32:64]),
                  reads=[B_ps[MISC]], writes=[B_Cst[tt]])
            S_.op("dve", lambda e: e.tensor_copy(out=CT_bf[0:8, :], in_=ps[bct][0:8, :]),
                  reads=[B_ps[bct], B_init], writes=[B_CT])

        def pool_sched(tt):
            sched = {}
            offs = [0, 1, 6, 12, 19]

            def chain(g):
                w = 2 << g
                src = uT[:, g, :]
                srcB = [B_uT[g], B_uhalo]
                shift, lo, step = 1, 0, 0
                while shift < w:
                    dst = ptmp[step % 2]
                    nlo = lo + shift
                    S_.op("pool", lambda e, dst=dst, src=src, nlo=nlo, shift=shift:
                          e.tensor_tensor(out=dst[:, nlo:UW], in0=src[:, nlo:UW], in1=src[:, nlo - shift:UW - shift],
                                          op=ALU.add),
                          reads=srcB, writes=[B_ptmp[step % 2]])
                    src = dst
                    srcB = [B_ptmp[step % 2]]
                    lo = nlo
                    shift *= 2
                    step += 1
                return src, srcB

            res = {}

            def do_chain(g):
                res[g] = chain(g)

            def do_final(g):
                w = 2 << g
                src, srcB = res[g]
                S_.op("dve", lambda e: e.scalar_tensor_tensor(
                    out=pooledT[:, g, :], in0=src[:, 16:UW], scalar=1.0 / w, in1=uT[:, g, 16:UW],
                    op0=ALU.mult, op1=ALU.subtract),
                    reads=srcB + [B_uT[g]], writes=[B_pooled[g]])
                if tt == 0:
                    S_.op("dve", lambda e: e.tensor_tensor(out=tmp16, in0=src[:, 16:32], in1=invcnt[:, g, :],
                                                           op=ALU.mult),
                          reads=srcB + [B_cst], writes=[B_tmp16])
                    S_.op("dve", lambda e: e.tensor_tensor(out=pooledT[:, g, 0:16], in0=tmp16, in1=uT[:, g, 16:32],
                                                           op=ALU.subtract),
                          reads=[B_tmp16, B_uT[g]], writes=[B_pooled[g]])

            def halo():
                S_.op("pool", lambda e: e.tensor_copy(out=uT[:, :, 0:16], in_=uT[:, :, TT:UW]),
                      reads=B_uT, writes=[B_uhalo])

            sched.setdefault(offs[0], []).append(lambda: do_chain(0))
            for g in range(4):
                sched.setdefault(offs[g + 1], []).append(lambda g=g: do_final(g))
                if g + 1 < 4:
                    sched.setdefault(offs[g + 1], []).append(lambda g=g: do_chain(g + 1))
            sched.setdefault(offs[4], []).append(halo)
            return sched

        def emit_poolmix(tt):
            for g in range(4):
                b = next_pb()
                S_.op("pe", lambda e, g=g, b=b: e.matmul(ps[b][:, :], wpool_bf[:, g, :], pooledT[:, g, :], start=True,
                                                         stop=True),
                      reads=[B_pooled[g], B_w1], writes=[B_ps[b]])
                S_.op("act", lambda e, g=g, b=b: e.activation(out=mixT[:, g, :], in_=ps[b][:, :], func=AF.Copy,
                                                              scale=pscale[:, g:g + 1]),
                      reads=[B_ps[b], B_cst], writes=[B_mix[g]])

        deferred = []

        def flush_deferred(now=None):
            keep = []
            for due, fn in deferred:
                if now is None or due <= now:
                    fn()
                else:
                    keep.append((due, fn))
            deferred[:] = keep

        def emit_attention(tt, steps, psched):
            nb = 4 * tt + 4
            blocks = [(h, j) for h in range(H) for j in range(nb)]
            LOOK = 2
            state = {}
            steps = list(steps)

            def qk(n):
                h, j = blocks[n]
                a = h // 2
                diag = j >= 4 * tt
                c0 = 128 * (j - 4 * tt) if diag else 0
                sb = ST_BANKS[n % 3]
                pi = n % NPT

                def mm(e):
                    e.matmul(ps[sb][:, c0:TT], kT[:, a, j * 128:(j + 1) * 128], qz[:, h, c0:TT],
                             start=True, stop=False)
                    ins = e.matmul(ps[sb][:, c0:TT], selneg[:, h, :], CT_bf[:, c0:TT], start=False,
                                   stop=(not diag))
                    if diag:
                        ins = e.matmul(ps[sb][:, c0:c0 + 128], ident_bf, maskb, start=False, stop=True)
                    return ins

                S_.op("pe", mm, reads=[B_kT[a][j // 4], B_qz[h], B_CT, B_cst], writes=[B_ps[sb]])
                S_.op("act", lambda e: e.activation(out=pT[pi][:, c0:TT], in_=ps[sb][:, c0:TT], func=AF.Exp,
                                                    bias=Cst[:, j, h:h + 1], scale=1.0),
                      reads=[B_ps[sb], B_Cst[j // 4]], writes=[B_pT[pi]])
                state[n] = (c0, pi)

            def pv(n, it):
                h, j = blocks[n]
                a, p0 = h // 2, 64 * (h % 2)
                c0, pi = state.pop(n)
                ab = ACC_BANKS[h % 2]
                S_.op("pe", lambda e: e.matmul(ps[ab][0:65, c0:TT], vst[:, j, h, :], pT[pi][:, c0:TT], start=(j == 0),
                                               stop=(j == nb - 1)),
                      reads=[B_pT[pi], B_v[j], B_vones], writes=[B_ps[ab]])
                if j == nb - 1:
                    nrb = nr[h % 2]
                    Bn = B_nr[h % 2]
                    S_.op("dve", lambda e: e.tensor_copy(out=nrb[0:64, :], in_=ps[ab][0:64, :]),
                          reads=[B_ps[ab], B_init], writes=[Bn])
                    S_.op("dve", lambda e: e.reciprocal(out=nrb[64:65, :], in_=ps[ab][64:65, :]),
                          reads=[B_ps[ab], B_init], writes=[Bn])

                    def fin():
                        b = next_pb()
                        S_.op("pe", lambda e: e.matmul(ps[b][:, :], e64_f, nrb, start=True, stop=True),
                              reads=[Bn, B_cst], writes=[B_ps[b]])
                        S_.op("dve", lambda e: e.tensor_tensor(out=attnT[p0:p0 + 64, a, :], in0=nrb[0:64, :],
                                                               in1=ps[b][0:64, :], op=ALU.mult),
                              reads=[Bn, B_ps[b]], writes=[B_attn[h]])
                    deferred.append((it + 7, fin))

            nblk = len(blocks)
            start_steps = max(2, nblk // 8)
            for n in range(nblk + LOOK):
                if n < nblk:
                    qk(n)
                flush_deferred(n)
                if n - LOOK >= 0:
                    pv(n - LOOK, n)
                if n >= start_steps and steps:
                    steps.pop(0)()
                for fn in psched.pop(n - 3, []):
                    fn()
            for st in steps:
                st()
            for k in sorted(psched):
                for fn in psched[k]:
                    fn()

        def emit_wout(tt):
            i = tt % 2
            for m in range(KC):
                b = next_pb()

                def mm(e, m=m, b=b):
                    ins = None
                    for ec in range(8):
                        rhs = attnT[:, ec, :] if ec < 4 else mixT[:, ec - 4, :]
                        ins = e.matmul(ps[b][:, :], wout_bf[:, ec, m * 128:(m + 1) * 128], rhs, start=(ec == 0),
                                       stop=(ec == 7))
                    return ins

                S_.op("pe", mm, reads=B_attn + B_mix + [B_w1], writes=[B_ps[b]])
                S_.op("dve", lambda e, m=m, b=b: e.tensor_tensor(out=xt[i][:, m, :], in0=ps[b][:, :], in1=xt[i][:, m, :],
                                                                 op=ALU.add),
                      reads=[B_ps[b], B_xt[i][m]], writes=[B_xt[i][m]])
            if tt + 1 < NT or debug:
                S_.dma("sp", f"xs{i}", lambda e: e.dma_start(out=x1_v[:, :, tt * TT:(tt + 1) * TT], in_=xt[i][:]),
                       reads=B_xt[i])

        for st in norm_steps(xt[0], B_xt[0], hb, B_hb, g1, sq_eng="act"):
            st()
        emit_vf(0)
        emit_fchain_a(0)
        emit_qku(0)
        emit_fchain_b(0)
        for tt in range(NT):
            steps = []
            if tt + 1 < NT:
                emit_load(tt + 1)
                steps = norm_steps(xt[(tt + 1) % 2], B_xt[(tt + 1) % 2], hb, B_hb, g1)
            emit_attention(tt, steps, pool_sched(tt))
            emit_poolmix(tt)
            if tt + 1 < NT:
                emit_vf(tt + 1)
                emit_fchain_a(tt + 1)
            flush_deferred()
            emit_wout(tt)
            if tt + 1 < NT:
                emit_qku(tt + 1)
                emit_fchain_b(tt + 1)
            if tt + 1 == NT - 1 or NT == 1:
                emit_w2_block(0, extra_writes=B_win)

        S_.barrier()

        B_yt = B_xt
        ORDER = [NT - 1] + list(range(NT - 1))
        BUFI = lambda k: (NT - 1 + k) % 2
        B_hb2 = [Buf(f"hb2_{c}") for c in range(KC)]
        B_act = [Buf(f"act{c}") for c in range(FC)]
        B_sil = [Buf("sil0"), Buf("sil1")]
        B_xsq2 = [Buf(f"xsq2_{i}") for i in range(NXSQ2)]
        B_lnt2, B_rstd2 = Buf("lnt2"), Buf("rstd2")
        B_ps2 = [Buf(f"ps2_{i}") for i in range(8)]
        GB, UB, DB, NB = [0, 1], [2, 3], [4, 5], 6

        def emit_load2(k):
            i, tt = BUFI(k), ORDER[k]
            S_.dma("sp", f"yl{i}", lambda e: e.dma_start(out=yt[i][:], in_=x1_v[:, :, tt * TT:(tt + 1) * TT]),
                   writes=B_yt[i])

        for bi in (1, 2, 3):
            emit_w2_block(bi)
        for bi, (f0, f1) in enumerate(FBLK):
            S_.dma("pool", f"d{bi}", lambda e, f0=f0, f1=f1: e.dma_start(out=wd_bf[:, f0:f1, :], in_=wd_v[:, f0:f1, :]),
                   writes=[B_wd[bi]])

        def norm2_steps(xtile, B_x, out_fn, bank, sq_eng="pool"):
            steps = []

            def sq(c):
                k = c % NXSQ2
                if sq_eng == "act":
                    S_.op("act", lambda e: e.activation(out=xsq2[k], in_=xtile[:, c, :], func=AF.Square),
                          reads=[B_x[c]], writes=[B_xsq2[k]])
                else:
                    S_.op("pool", lambda e: e.tensor_tensor(out=xsq2[k], in0=xtile[:, c, :], in1=xtile[:, c, :],
                                                            op=ALU.mult),
                          reads=[B_x[c]], writes=[B_xsq2[k]])

            def mm(c):
                k = c % NXSQ2
                S_.op("pe", lambda e: e.matmul(ps[bank][:, :], ones_bf, xsq2[k], start=(c == 0), stop=(c == KC - 1)),
                      reads=[B_xsq2[k], B_cst], writes=[B_ps2[bank]])

            LAG = 1
            for c in range(KC + LAG):
                def st(c=c):
                    if c < KC:
                        sq(c)
                    if c - LAG >= 0:
                        mm(c - LAG)
                steps.append(st)

            def rs():
                S_.op("act", lambda e: e.activation(out=lnt2, in_=ps[bank][:, :], func=AF.Ln, bias=eps_t[:, 0:1],
                                                    scale=1.0 / D),
                      reads=[B_ps2[bank], B_cst], writes=[B_lnt2] + B_xsq2)
                S_.op("act", lambda e: e.activation(out=rstd2, in_=lnt2, func=AF.Exp, scale=-0.5),
                      reads=[B_lnt2] + B_xsq2, writes=[B_rstd2])
            steps.append(rs)
            for c in range(KC):
                steps.append(lambda c=c: out_fn(c))
            return steps

        def hb_steps(k, sq_eng="pool"):
            i = BUFI(k)
            xtile = yt[i]

            def mk_hb(c):
                S_.op("dve", lambda e: e.scalar_tensor_tensor(out=hb2[:, c, :], in0=xtile[:, c, :],
                                                              scalar=g2[:, c:c + 1], in1=rstd2,
                                                              op0=ALU.mult, op1=ALU.mult),
                      reads=[B_yt[i][c], B_rstd2, B_cst], writes=[B_hb2[c]])
            return norm2_steps(xtile, B_yt[i], mk_hb, NB, sq_eng)

        def final_steps(k):
            i, tt = BUFI(k), ORDER[k]
            xtile = yt[i]

            def mk_y(c):
                S_.op("dve", lambda e: e.scalar_tensor_tensor(out=xtile[:, c, :], in0=xtile[:, c, :],
                                                              scalar=gf[:, c:c + 1], in1=rstd2,
                                                              op0=ALU.mult, op1=ALU.mult),
                      reads=[B_yt[i][c], B_rstd2, B_cst], writes=[B_yt[i][c]])
                S_.dma("sp", f"ys{i}", lambda e: e.dma_start(out=yT_v[:, c, tt * TT:(tt + 1) * TT], in_=xtile[:, c, :]),
                       reads=[B_yt[i][c]])
            return norm2_steps(xtile, B_yt[i], mk_y, 7)

        def fblk_of(fc):
            for bi, (f0, f1) in enumerate(FBLK):
                if f0 <= fc < f1:
                    return bi, fc - f0
            raise AssertionError

        def emit_gateup(tt, pending):
            for fc in range(FC):
                gb, ub = GB[fc % 2], UB[fc % 2]
                bi, fo = fblk_of(fc)

                def mmg(e, bi=bi, fo=fo, gb=gb):
                    ins = None
                    for c in range(KC):
                        ins = e.matmul(ps[gb][:, :], wg_blk[bi][:, c, fo * 128:(fo + 1) * 128], hb2[:, c, :],
                                       start=(c == 0), stop=(c == KC - 1))
                    return ins

                def mmu(e, bi=bi, fo=fo, ub=ub):
                    ins = None
                    for c in range(KC):
                        ins = e.matmul(ps[ub][:, :], wu_blk[bi][:, c, fo * 128:(fo + 1) * 128], hb2[:, c, :],
                                       start=(c == 0), stop=(c == KC - 1))
                    return ins

                S_.op("pe", mmg, reads=B_hb2 + [B_wg[bi]], writes=[B_ps2[gb]])
                S_.op("pe", mmu, reads=B_hb2 + [B_wu[bi]], writes=[B_ps2[ub]])
                sl = fc % 2
                S_.op("act", lambda e, gb=gb, sl=sl: e.activation(out=sil[sl], in_=ps[gb][:, :], func=AF.Silu),
                      reads=[B_ps2[gb]], writes=[B_sil[sl]])
                S_.op("dve", lambda e, fc=fc, ub=ub, sl=sl: e.tensor_tensor(out=actT[:, fc, :], in0=ps[ub][:, :],
                                                                            in1=sil[sl], op=ALU.mult),
                      reads=[B_ps2[ub], B_sil[sl]], writes=[B_act[fc]])
                if pending and fc >= 1:
                    pending.pop(0)()
            while pending:
                pending.pop(0)()

        def emit_down(k, pending):
            i = BUFI(k)
            xtile = yt[i]
            for m in range(KC):
                db = DB[m % 2]

                def mmd(e, m=m, db=db):
                    ins = None
                    for fc in range(FC):
                        ins = e.matmul(ps[db][:, :], wd_bf[:, fc, m * 128:(m + 1) * 128], actT[:, fc, :],
                                       start=(fc == 0), stop=(fc == FC - 1))
                    return ins

                S_.op("pe", mmd, reads=B_act + B_wd, writes=[B_ps2[db]])
                S_.op("dve", lambda e, m=m, db=db: e.tensor_tensor(out=xtile[:, m, :], in0=ps[db][:, :],
                                                                   in1=xtile[:, m, :], op=ALU.add),
                      reads=[B_ps2[db], B_yt[i][m]], writes=[B_yt[i][m]])
                for _ in range(3):
                    if pending:
                        pending.pop(0)()
            while pending:
                pending.pop(0)()

        for st in hb_steps(0, sq_eng="act"):
            st()
        pend_final = []
        for k in range(NT):
            emit_gateup(k, pend_final)
            if k + 1 < NT:
                emit_load2(k + 1)
            emit_down(k, hb_steps(k + 1) if k + 1 < NT else [])
            pend_final = final_steps(k)
        for st in pend_final:
            st()

        S_.barrier()

        print(f"[kernel] arena words: phase1 {p1_end} phase2 {p2_end} of {AW}; "
              f"sem counts: { {k: v for k, v in S_.cnt.items()} }")

        @block.tensor
        def _(e):
            S_.replay("pe", e, semh)

        @block.scalar
        def _(e):
            S_.replay("act", e, semh)

        @block.vector
        def _(e):
            S_.replay("dve", e, semh)

        @block.gpsimd
        def _(e):
            S_.replay("pool", e, semh)

        @block.sync
        def _(e):
            S_.replay("sp", e, semh)

    return nc


def _consts(norm1_g, norm2_g, final_g, pool_scale, b_forget):
    cf = np.zeros((128, NCF), np.float32)
    cf[:, CF_G1:CF_G1 + 8] = np.asarray(norm1_g, np.float32).reshape(8, 128).T
    cf[:, CF_G2:CF_G2 + 8] = np.asarray(norm2_g, np.float32).reshape(8, 128).T
    cf[:, CF_GF:CF_GF + 8] = np.asarray(final_g, np.float32).reshape(8, 128).T
    cf[:, CF_PS:CF_PS + 4] = np.asarray(pool_scale, np.float32).reshape(4, 128).T
    cf[:, CF_BF:CF_BF + 32] = np.tile(np.asarray(b_forget, np.float32).reshape(1, 8), (128, 4))
    t = np.arange(16)
    ic = np.stack([1.0 / np.minimum(t + 1, 2 << g) for g in range(4)]).astype(np.float32)
    cf[:, CF_IC:CF_IC + 64] = ic.reshape(1, 64)
    kk = np.arange(128)
    cf[:, CF_TRI:CF_TRI + 128] = (kk[:, None] <= kk[None, :]).astype(np.float32)
    cf[64, CF_E64:CF_E64 + 128] = 1.0
    cb = np.zeros((128, NCB), np.float32)
    cb[:, CB_ID:CB_ID + 128] = np.eye(128, dtype=np.float32)
    cb[:, CB_MASK:CB_MASK + 128] = np.where(kk[:, None] <= kk[None, :], 0.0, NEG).astype(np.float32)
    for h in range(8):
        cb[h, CB_SEL + h * 128:CB_SEL + (h + 1) * 128] = -1.0
    return cf, cb


_NC_CACHE = {}


def kernel(x, norm1_g, w_in, b_forget, w_pool, pool_scale, w_out, norm2_g, w_gate, w_up, w_down, final_g):
    x = np.asarray(x, np.float32)
    B = x.shape[0]
    cf, cb = _consts(np.asarray(norm1_g)[0], np.asarray(norm2_g)[0], np.asarray(final_g), np.asarray(pool_scale)[0],
                     np.asarray(b_forget)[0])
    shared = {
        "w_in": np.ascontiguousarray(np.asarray(w_in, np.float32)[0]),
        "w_out": np.ascontiguousarray(np.asarray(w_out, np.float32)[0]),
        "w_pool": np.ascontiguousarray(np.asarray(w_pool, np.float32)[0]),
        "w_gate": np.ascontiguousarray(np.asarray(w_gate, np.float32)[0]),
        "w_up": np.ascontiguousarray(np.asarray(w_up, np.float32)[0]),
        "w_down": np.ascontiguousarray(np.asarray(w_down, np.float32)[0]),
        "cf": cf,
        "cb": cb,
    }
    in_maps = [dict(shared, xT=np.ascontiguousarray(x[b].T)) for b in range(B)]
    if "nc" not in _NC_CACHE:
        _NC_CACHE["nc"] = build_program()
    nc = _NC_CACHE["nc"]
    res = run_bass_kernel_spmd(nc, in_maps, core_ids=list(range(B)))
    out = np.stack([np.asarray(res.results[b]["yT"]).T for b in range(B)])
    return np.ascontiguousarray(out.astype(np.float32))
```

```python
from contextlib import ExitStack

import numpy as np
import concourse.bass as bass
import concourse.mybir as mybir
from concourse.bass_utils import run_bass_kernel_spmd

F32 = mybir.dt.float32
BF16 = mybir.dt.bfloat16
AF = mybir.ActivationFunctionType
ALU = mybir.AluOpType

S = 4096
D = 1024
TT = 512
NT = S // TT
KC = D // 128
H = 8
DFF = 2816
FC = DFF // 128
INW = 2056
EPS = 1e-6
NEG = -30000.0

CF_G1, CF_G2, CF_GF, CF_PS, CF_BF, CF_IC, CF_TRI = 0, 8, 16, 24, 28, 60, 124
CF_E64 = 124 + 128
NCF = 124 + 128 + 128
CB_ID, CB_MASK, CB_SEL = 0, 128, 256
NCB = 256 + 1024


class Buf:
    __slots__ = ("name", "w", "r")

    def __init__(self, name):
        self.name = name
        self.w = None
        self.r = []


class Sched:
    ENGS = ("pe", "act", "dve", "pool", "sp")

    def __init__(self):
        self.prog = {e: [] for e in self.ENGS}
        self.cnt = {}
        self.waited = {e: {} for e in self.ENGS}

    @staticmethod
    def _flat(bufs):
        out = []
        for b in bufs:
            if isinstance(b, (list, tuple)):
                out.extend(Sched._flat(b))
            else:
                out.append(b)
        return out

    def _deps(self, eng, reads, writes):
        reads, writes = self._flat(reads), self._flat(writes)
        need = {}

        def add(t):
            if t is not None and need.get(t[0], 0) < t[1]:
                need[t[0]] = t[1]

        for b in reads:
            add(b.w)
        for b in writes:
            add(b.w)
            for t in b.r:
                add(t)
        out = []
        for k, v in need.items():
            if k == eng and eng == "pe":
                continue
            if self.waited[eng].get(k, 0) >= v:
                continue
            self.waited[eng][k] = v
            out.append((k, v))
        return out

    def _commit(self, tok, reads, writes):
        reads, writes = self._flat(reads), self._flat(writes)
        for b in reads:
            b.r.append(tok)
        for b in writes:
            b.w = tok
            b.r = []

    def op(self, eng, fn, reads=(), writes=()):
        waits = self._deps(eng, reads, writes)
        self.cnt[eng] = self.cnt.get(eng, 0) + 1
        tok = (eng, self.cnt[eng])
        self.prog[eng].append((waits, fn, (eng, 1)))
        self._commit(tok, reads, writes)
        return tok

    def dma(self, queue, semkey, fn, reads=(), writes=(), track=True):
        waits = self._deps(queue, reads, writes) if track else []
        self.cnt[semkey] = self.cnt.get(semkey, 0) + 16
        tok = (semkey, self.cnt[semkey])
        self.prog[queue].append((waits, fn, (semkey, 16)))
        if track:
            self._commit(tok, reads, writes)
        return tok

    def barrier(self):
        for e in self.ENGS:
            waits = []
            for k, v in self.cnt.items():
                if k == e and e == "pe":
                    continue
                if self.waited[e].get(k, 0) >= v:
                    continue
                self.waited[e][k] = v
                waits.append((k, v))
            if waits:
                self.prog[e].append((waits, None, None))

    def replay(self, eng, e, semh):
        for waits, fn, sig in self.prog[eng]:
            for k, v in waits:
                e.wait_ge(semh[k], v)
            if fn is None:
                continue
            ins = fn(e)
            ins.then_inc(semh[sig[0]], sig[1])


class Arena:
    def __init__(self, ap):
        self.ap = ap
        self.off = 0
        self.words = ap.shape[1]

    def alloc(self, n, dtype):
        nbytes = n * (2 if dtype == BF16 else 4)
        words = (nbytes + 31) // 32 * 8
        assert self.off + words <= self.words, ("arena overflow", self.off, words, self.words)
        v = self.ap[:, self.off:self.off + words]
        self.off += words
        if dtype == BF16:
            return v.bitcast(BF16)[:, 0:n]
        return v[:, 0:n]


def build_program(debug=False):
    nc = bass.Bass("TRN2", target_bir_lowering=False)
    dt_in = lambda name, shape: nc.dram_tensor(name, shape, F32, kind="ExternalInput").ap()
    xT = dt_in("xT", [D, S])
    w_in = dt_in("w_in", [D, INW])
    w_out = dt_in("w_out", [D, D])
    w_pool = dt_in("w_pool", [4, 128, 128])
    w_gate = dt_in("w_gate", [D, DFF])
    w_up = dt_in("w_up", [D, DFF])
    w_down = dt_in("w_down", [DFF, D])
    cf_d = dt_in("cf", [128, NCF])
    cb_d = dt_in("cb", [128, NCB])
    x1s = nc.dram_tensor("x1s", [D, S], F32, kind=("ExternalOutput" if debug else "Internal")).ap()
    yT = nc.dram_tensor("yT", [D, S], F32, kind="ExternalOutput").ap()

    xT_v = xT.rearrange("(c p) t -> p c t", p=128)
    x1_v = x1s.rearrange("(c p) t -> p c t", p=128)
    yT_v = yT.rearrange("(c p) t -> p c t", p=128)

    S_ = Sched()
    semnames = ["pe", "act", "dve", "pool", "cst", "cst2", "wi0", "wi1", "wi2", "wi3", "w1", "g0", "g1", "g2", "g3", "u0", "u1", "u2", "u3", "d0", "d1", "d2", "d3", "xl0", "xl1", "xs0", "xs1",
                "yl0", "yl1", "ys0", "ys1"]

    with ExitStack() as ctx:
        AW = 53100
        arena_t = ctx.enter_context(nc.sbuf_tensor("arena", [128, AW], F32))
        ps = [ctx.enter_context(nc.psum_tensor(f"ps{i}", [128, 512], F32)) for i in range(8)]
        semh = {n: ctx.enter_context(nc.semaphore(n)) for n in semnames}
        block = ctx.enter_context(nc.Block())

        ar = Arena(arena_t[:, :])
        cf = ar.alloc(NCF, F32)
        cb = ar.alloc(NCB, BF16)
        ones_bf = ar.alloc(128, BF16)
        ones_f = ar.alloc(128, F32)
        eps_t = ar.alloc(8, F32)
        xt = [ar.alloc(KC * TT, F32).rearrange("p (c t) -> p c t", t=TT) for _ in range(2)]
        base_off = ar.off
        B_cf, B_cb, B_ones = Buf("cf"), Buf("cb"), Buf("ones")
        B_cst = [B_cf, B_cb, B_ones]

        g1 = cf[:, CF_G1:CF_G1 + 8]
        g2 = cf[:, CF_G2:CF_G2 + 8]
        gf = cf[:, CF_GF:CF_GF + 8]
        pscale = cf[:, CF_PS:CF_PS + 4]
        bfor = cf[:, CF_BF:CF_BF + 32]
        invcnt = cf[:, CF_IC:CF_IC + 64].rearrange("p (g t) -> p g t", t=16)
        tri_f = cf[:, CF_TRI:CF_TRI + 128]
        e64_f = cf[:, CF_E64:CF_E64 + 128]
        ident_bf = cb[:, CB_ID:CB_ID + 128]
        maskb = cb[:, CB_MASK:CB_MASK + 128]
        selneg = cb[:, CB_SEL:CB_SEL + 1024].rearrange("p (h m) -> p h m", m=128)

        S_.dma("sp", "cst", lambda e: e.dma_start(out=cf, in_=cf_d), writes=[B_cf])
        S_.dma("pool", "cst2", lambda e: e.dma_start(out=cb, in_=cb_d), writes=[B_cb])
        B_xt = [[Buf(f"xt{i}_{c}") for c in range(KC)] for i in range(2)]
        S_.dma("sp", "xl0", lambda e: e.dma_start(out=xt[0][:], in_=xT_v[:, :, 0:TT]), writes=B_xt[0])
        S_.op("pool", lambda e: e.memset(ones_bf, 1.0), writes=[B_ones])
        S_.op("pool", lambda e: e.memset(ones_f, 1.0), writes=[B_ones])
        S_.op("pool", lambda e: e.memset(eps_t, EPS), writes=[B_ones])

        win_bf = ar.alloc(KC * INW, BF16).rearrange("p (c n) -> p c n", n=INW)
        wout_bf = ar.alloc(KC * D, BF16).rearrange("p (c n) -> p c n", n=D)
        wpool_bf = ar.alloc(4 * 128, BF16).rearrange("p (g n) -> p g n", n=128)
        kT = ar.alloc(4 * S, BF16).rearrange("p (a t) -> p a t", t=S)
        NJ = S // 128
        vst = ar.alloc(NJ * H * 65, BF16).rearrange("p (j h d) -> p j h d", h=H, d=65)
        NXSQ = 3
        xsq = [ar.alloc(TT, BF16) for _ in range(NXSQ)]
        hb2d = ar.alloc(KC * TT, BF16)
        hb = hb2d.rearrange("p (c t) -> p c t", t=TT)
        lnt = hb2d[:, 6 * TT:8 * TT].bitcast(F32)
        rstd = ar.alloc(TT, F32)
        qz = ar.alloc(H * TT, BF16).rearrange("p (h t) -> p h t", t=TT)
        UW = TT + 16
        uT = ar.alloc(4 * UW, F32).rearrange("p (g t) -> p g t", t=UW)
        ptmp = [ar.alloc(UW, F32) for _ in range(2)]
        pooledT = ar.alloc(4 * TT, BF16).rearrange("p (g t) -> p g t", t=TT)
        mixT = ar.alloc(4 * TT, BF16).rearrange("p (g t) -> p g t", t=TT)
        attnT = ar.alloc(4 * TT, BF16).rearrange("p (a t) -> p a t", t=TT)
        NPT = 3
        pT = [ar.alloc(TT, BF16) for _ in range(NPT)]
        nr = [ar.alloc(TT, F32) for _ in range(2)]
        CT_bf = ar.alloc(TT, BF16)
        fz = ar.alloc(32, F32)
        lst = ar.alloc(NJ * H, F32).rearrange("p (n h) -> p n h", h=H)
        accs = ar.alloc((NJ + 1) * H, F32).rearrange("p (n h) -> p n h", h=H)
        Cst = ar.alloc(NJ * H, F32).rearrange("p (n h) -> p n h", h=H)
        tmp16 = ar.alloc(16, F32)
        p1_end = ar.off

        ar.off = base_off
        FBLK = [(0, 6), (6, 12), (12, 17), (17, 22)]
        wg_blk, wu_blk = [None] * 4, [None] * 4
        for bi in (0,):
            f0, f1 = FBLK[bi]
            wg_blk[bi] = ar.alloc(KC * (f1 - f0) * 128, BF16).rearrange("p (c n) -> p c n", c=KC)
            wu_blk[bi] = ar.alloc(KC * (f1 - f0) * 128, BF16).rearrange("p (c n) -> p c n", c=KC)
        assert ar.off <= base_off + (KC * INW * 2 + 3) // 4, "prefetch blocks must fit in the w_in region"
        for bi in (1, 2, 3):
            f0, f1 = FBLK[bi]
            wg_blk[bi] = ar.alloc(KC * (f1 - f0) * 128, BF16).rearrange("p (c n) -> p c n", c=KC)
            wu_blk[bi] = ar.alloc(KC * (f1 - f0) * 128, BF16).rearrange("p (c n) -> p c n", c=KC)
        wd_bf = ar.alloc(FC * D, BF16).rearrange("p (c n) -> p c n", n=D)
        yt = xt
        hb2 = ar.alloc(KC * TT, BF16).rearrange("p (c t) -> p c t", t=TT)
        lnt2 = ar.alloc(TT, F32)
        xsq2 = [lnt2.bitcast(BF16)[:, 0:TT], lnt2.bitcast(BF16)[:, TT:2 * TT]]
        NXSQ2 = 2
        rstd2 = ar.alloc(TT, F32)
        actT = ar.alloc(FC * TT, BF16).rearrange("p (c t) -> p c t", t=TT)
        sil = [ar.alloc(TT, F32) for _ in range(2)]
        p2_end = ar.off
        wg_v = w_gate.rearrange("(c p) n -> p c n", p=128)
        wu_v = w_up.rearrange("(c p) n -> p c n", p=128)
        wd_v = w_down.rearrange("(c p) n -> p c n", p=128)
        B_wg = [Buf(f"wg{i}") for i in range(4)]
        B_wu = [Buf(f"wu{i}") for i in range(4)]
        B_wd = [Buf(f"wd{i}") for i in range(4)]

        def emit_w2_block(bi, extra_writes=()):
            f0, f1 = FBLK[bi]
            S_.dma("pool", f"g{bi}", lambda e: e.dma_start(out=wg_blk[bi][:, :, :], in_=wg_v[:, :, f0 * 128:f1 * 128]),
                   writes=[B_wg[bi]] + list(extra_writes))
            S_.dma("pool", f"u{bi}", lambda e: e.dma_start(out=wu_blk[bi][:, :, :], in_=wu_v[:, :, f0 * 128:f1 * 128]),
                   writes=[B_wu[bi]] + list(extra_writes))

        B_w1 = Buf("w1")
        win_v = w_in.rearrange("(c p) n -> p c n", p=128)
        B_win = [Buf(f"win{i}") for i in range(4)]
        for i, (c0, c1) in enumerate([(1024, 1544), (0, 512), (512, 1024), (1544, 2056)]):
            S_.dma("pool", f"wi{i}", lambda e, c0=c0, c1=c1: e.dma_start(out=win_bf[:, :, c0:c1], in_=win_v[:, :, c0:c1]),
                   writes=[B_win[i]])
        for c in range(KC):
            S_.dma("pool", "w1", lambda e, c=c: e.dma_start(out=wout_bf[:, c, :], in_=w_out[c * 128:(c + 1) * 128, :]),
                   track=False)
        for g in range(4):
            S_.dma("pool", "w1", lambda e, g=g: e.dma_start(out=wpool_bf[:, g, :], in_=w_pool[g]), track=False)
        B_w1.w = ("w1", S_.cnt["w1"])
        B_xsq = [Buf(f"xsq{i}") for i in range(NXSQ)]
        B_hb = [Buf(f"hb{c}") for c in range(KC)]
        B_rstd = Buf("rstd")
        B_ps = [Buf(f"ps{i}") for i in range(8)]
        B_qz = [Buf(f"qz{h}") for h in range(H)]
        B_kT = [[Buf(f"kT{a}_{t}") for t in range(NT)] for a in range(4)]
        B_v = [Buf(f"v{j}") for j in range(NJ)]
        B_ptmp = [Buf("ptmp0"), Buf("ptmp1")]
        B_pooled = [Buf(f"pooled{g}") for g in range(4)]
        B_mix = [Buf(f"mix{g}") for g in range(4)]
        B_attn = [Buf(f"attn{h}") for h in range(H)]
        B_pT = [Buf(f"pT{i}") for i in range(NPT)]
        B_nr, B_CT = [Buf("nr0"), Buf("nr1")], Buf("CTbf")
        B_fz = Buf("fz")
        B_lst = [Buf(f"lst{n}") for n in range(NJ)]
        B_Cst = [Buf(f"Cst{t}") for t in range(NT)]
        B_tmp16 = Buf("tmp16")

        ST_BANKS = [0, 1, 2]
        ACC_BANKS = [3, 4]
        PB_BANKS = [5, 6]
        MISC = 7
        pb_ctr = [0]

        def next_pb():
            b = PB_BANKS[pb_ctr[0] % 2]
            pb_ctr[0] += 1
            return b

        xsq_ctr = [0]

        B_vones = Buf("vones")
        B_uT = [Buf(f"uT{g}") for g in range(4)]
        B_uhalo = Buf("uhalo")
        B_acc = [Buf(f"acc{n}") for n in range(NJ + 1)]
        S_.op("pool", lambda e: e.memset(vst[:, :, :, 64:65], 1.0), writes=[B_vones])
        S_.op("pool", lambda e: e.memset(uT[:, :, 0:16], 0.0), writes=[B_uhalo])
        S_.op("pool", lambda e: e.memset(accs[:, 0, :], 0.0), writes=[B_acc[0]])
        B_init = Buf("init")
        S_.op("pool", lambda e: e.memset(qz[:, :, :], 0.0), writes=[B_init])
        S_.op("pool", lambda e: e.memset(CT_bf, 0.0), writes=[B_init])
        S_.op("pool", lambda e: e.memset(nr[0], 0.0), writes=[B_init])
        S_.op("pool", lambda e: e.memset(nr[1], 0.0), writes=[B_init])

        def emit_load(tt):
            i = tt % 2
            S_.dma("sp", f"xl{i}", lambda e: e.dma_start(out=xt[i][:], in_=xT_v[:, :, tt * TT:(tt + 1) * TT]),
                   writes=B_xt[i])

        def norm_steps(xtile, B_x, hbt, B_h, gcol, misc_bank=MISC, sq_eng="pool"):
            steps = []
            ks = []
            for c in range(KC):
                k = xsq_ctr[0] % NXSQ
                xsq_ctr[0] += 1
                ks.append(k)

            def sq(c):
                k = ks[c]
                if sq_eng == "act":
                    S_.op("act", lambda e: e.activation(out=xsq[k], in_=xtile[:, c, :], func=AF.Square),
                          reads=[B_x[c]], writes=[B_xsq[k]])
                else:
                    S_.op("pool", lambda e: e.tensor_tensor(out=xsq[k], in0=xtile[:, c, :], in1=xtile[:, c, :],
                                                            op=ALU.mult),
                          reads=[B_x[c]], writes=[B_xsq[k]])

            def mm(c):
                k = ks[c]
                S_.op("pe", lambda e: e.matmul(ps[misc_bank][:, :], ones_bf, xsq[k], start=(c == 0),
                                               stop=(c == KC - 1)),
                      reads=[B_xsq[k], B_cst], writes=[B_ps[misc_bank]])

            LAG = 2
            for c in range(KC + LAG):
                def st(c=c):
                    if c < KC:
                        sq(c)
                    if c - LAG >= 0:
                        mm(c - LAG)
                steps.append(st)

            def rs():
                S_.op("act", lambda e: e.activation(out=lnt, in_=ps[misc_bank][:, :], func=AF.Ln, bias=eps_t[:, 0:1],
                                                    scale=1.0 / D),
                      reads=[B_ps[misc_bank], B_cst], writes=[B_h[6], B_h[7]])
                S_.op("act", lambda e: e.activation(out=rstd, in_=lnt, func=AF.Exp, scale=-0.5),
                      reads=[B_h[6], B_h[7]], writes=[B_rstd])
            steps.append(rs)
            for c in range(KC):
                def hbs(c=c):
                    S_.op("dve", lambda e: e.scalar_tensor_tensor(out=hbt[:, c, :], in0=xtile[:, c, :],
                                                                  scalar=gcol[:, c:c + 1], in1=rstd,
                                                                  op0=ALU.mult, op1=ALU.mult),
                          reads=[B_x[c], B_rstd, B_cst], writes=[B_h[c]])
                steps.append(hbs)
            return steps

        evac_ctr = [0]

        def evac_copy(out_ap, in_ap, reads, writes, scale=None):
            k = evac_ctr[0]
            evac_ctr[0] += 1
            if k % 2 == 0:
                if scale is None:
                    S_.op("act", lambda e: e.activation(out=out_ap, in_=in_ap, func=AF.Copy), reads=reads, writes=writes)
                else:
                    S_.op("act", lambda e: e.activation(out=out_ap, in_=in_ap, func=AF.Copy, scale=scale),
                          reads=reads, writes=writes)
            else:
                if scale is None:
                    S_.op("dve", lambda e: e.tensor_copy(out=out_ap, in_=in_ap), reads=reads, writes=writes)
                else:
                    S_.op("dve", lambda e: e.tensor_scalar(out=out_ap, in0=in_ap, scalar1=scale, scalar2=None,
                                                           op0=ALU.mult), reads=reads, writes=writes)

        def emit_vf(tt):
            for blk in range(4):
                b = next_pb()
                j = tt * 4 + blk

                def mmv(e, blk=blk, b=b):
                    ins = None
                    for c in range(KC):
                        lhsT = hb[:, c, blk * 128:(blk + 1) * 128]
                        e.matmul(ps[b][:, :], lhsT, win_bf[:, c, 1024:1536], start=(c == 0), stop=(c == KC - 1))
                        ins = e.matmul(ps[MISC][:, blk * 8:(blk + 1) * 8], lhsT, win_bf[:, c, 1536:1544],
                                       start=(c == 0), stop=(c == KC - 1))
                    return ins

                S_.op("pe", mmv, reads=B_hb + [B_win[0]], writes=[B_ps[b], B_ps[MISC]])
                evac_copy(vst[:, j, :, 0:64], ps[b][:, :].rearrange("p (h d) -> p h d", d=64), [B_ps[b]], [B_v[j]])

        def emit_qku(tt):
            specs = []
            for a in range(4):
                specs.append(("q", a, a * 128))
            for a in range(4):
                specs.append(("k", a, 512 + a * 128))
            for g in range(4):
                specs.append(("u", g, 1544 + g * 128))
            for kind, idx, col in specs:
                b = next_pb()

                def mm(e, col=col, b=b):
                    ins = None
                    for c in range(KC):
                        ins = e.matmul(ps[b][:, :], win_bf[:, c, col:col + 128], hb[:, c, :], start=(c == 0),
                                       stop=(c == KC - 1))
                    return ins

                wb = {"q": B_win[1], "k": B_win[2], "u": B_win[3]}[kind]
                S_.op("pe", mm, reads=B_hb + [wb], writes=[B_ps[b]])
                if kind == "q":
                    evac_copy(qz[0:64, 2 * idx, :], ps[b][0:64, :], [B_ps[b], B_init], [B_qz[2 * idx]], scale=0.125)
                    evac_copy(qz[64:128, 2 * idx + 1, :], ps[b][64:128, :], [B_ps[b], B_init], [B_qz[2 * idx + 1]],
                              scale=0.125)
                elif kind == "k":
                    evac_copy(kT[:, idx, tt * TT:(tt + 1) * TT], ps[b][:, :], [B_ps[b]], [B_kT[idx][tt]])
                else:
                    evac_copy(uT[:, idx, 16:UW], ps[b][:, :], [B_ps[b]], [B_uT[idx]])

        def emit_fchain_a(tt):
            n0 = tt * 4
            S_.op("dve", lambda e: e.tensor_tensor(out=fz, in0=ps[MISC][:, 0:32], in1=bfor, op=ALU.add),
                  reads=[B_ps[MISC], B_cst], writes=[B_fz])
            S_.op("act", lambda e: e.activation(out=fz, in_=fz, func=AF.Exp, scale=-1.0), reads=[B_fz], writes=[B_fz])
            lview = lst[:, n0:n0 + 4, :].rearrange("p n h -> p (n h)")
            S_.op("act", lambda e: e.activation(out=lview, in_=fz, func=AF.Ln, bias=1.0, scale=1.0),
                  reads=[B_fz], writes=[B_lst[n0 + i] for i in range(4)])
            for i in range(4):
                n = n0 + i
                S_.op("dve", lambda e, n=n: e.tensor_tensor(out=accs[:, n + 1, :], in0=accs[:, n, :], in1=lst[:, n, :],
                                                            op=ALU.add),
                      reads=[B_acc[n], B_lst[n]], writes=[B_acc[n + 1]])

        def emit_fchain_b(tt):
            n0 = tt * 4
            bct = next_pb()

            def mmc(e):
                ins = None
                for i in range(4):
                    n = n0 + i
                    e.matmul(ps[MISC][:, 32 + i * 8:40 + i * 8], tri_f, lst[:, n, :], start=True, stop=False)
                    e.matmul(ps[MISC][:, 32 + i * 8:40 + i * 8], ones_f, accs[:, n, :], start=False, stop=True)
                for i in range(4):
                    n = n0 + i
                    e.matmul(ps[bct][0:8, i * 128:(i + 1) * 128], lst[:, n, :], tri_f, start=True, stop=False)
                    ins = e.matmul(ps[bct][0:8, i * 128:(i + 1) * 128], accs[:, n, :], ones_f, start=False, stop=True)
                return ins

            S_.op("pe", mmc, reads=[B_lst[n0 + i] for i in range(4)] + [B_acc[n0 + i] for i in range(4)] + [B_cst],
                  writes=[B_ps[MISC], B_ps[bct]])
            S_.op("dve", lambda e: e.tensor_copy(out=Cst[:, n0:n0 + 4, :].rearrange("p n h -> p (n h)"),
                                                 in_=ps[MISC][:, 32:64]),
                  reads=[B_ps[MISC]], writes=[B_Cst[tt]])
            S_.op("dve", lambda e: e.tensor_copy(out=CT_bf[0:8, :], in_=ps[bct][0:8, :]),
                  reads=[B_ps[bct], B_init], writes=[B_CT])

        def pool_sched(tt):
            sched = {}
            offs = [0, 1, 6, 12, 19]

            def chain(g):
                w = 2 << g
                src = uT[:, g, :]
                srcB = [B_uT[g], B_uhalo]
                shift, lo, step = 1, 0, 0
                while shift < w:
                    dst = ptmp[step % 2]
                    nlo = lo + shift
                    S_.op("pool", lambda e, dst=dst, src=src, nlo=nlo, shift=shift:
                          e.tensor_tensor(out=dst[:, nlo:UW], in0=src[:, nlo:UW], in1=src[:, nlo - shift:UW - shift],
                                          op=ALU.add),
                          reads=srcB, writes=[B_ptmp[step % 2]])
                    src = dst
                    srcB = [B_ptmp[step % 2]]
                    lo = nlo
                    shift *= 2
                    step += 1
                return src, srcB

            res = {}

            def do_chain(g):
                res[g] = chain(g)

            def do_final(g):
                w = 2 << g
                src, srcB = res[g]
                S_.op("dve", lambda e: e.scalar_tensor_tensor(
                    out=pooledT[:, g, :], in0=src[:, 16:UW], scalar=1.0 / w, in1=uT[:, g, 16:UW],
                    op0=ALU.mult, op1=ALU.subtract),
                    reads=srcB + [B_uT[g]], writes=[B_pooled[g]])
                if tt == 0:
                    S_.op("dve", lambda e: e.tensor_tensor(out=tmp16, in0=src[:, 16:32], in1=invcnt[:, g, :],
                                                           op=ALU.mult),
                          reads=srcB + [B_cst], writes=[B_tmp16])
                    S_.op("dve", lambda e: e.tensor_tensor(out=pooledT[:, g, 0:16], in0=tmp16, in1=uT[:, g, 16:32],
                                                           op=ALU.subtract),
                          reads=[B_tmp16, B_uT[g]], writes=[B_pooled[g]])

            def halo():
                S_.op("pool", lambda e: e.tensor_copy(out=uT[:, :, 0:16], in_=uT[:, :, TT:UW]),
                      reads=B_uT, writes=[B_uhalo])

            sched.setdefault(offs[0], []).append(lambda: do_chain(0))
            for g in range(4):
                sched.setdefault(offs[g + 1], []).append(lambda g=g: do_final(g))
                if g + 1 < 4:
                    sched.setdefault(offs[g + 1], []).append(lambda g=g: do_chain(g + 1))
            sched.setdefault(offs[4], []).append(halo)
            return sched

        def emit_poolmix(tt):
            for g in range(4):
                b = next_pb()
                S_.op("pe", lambda e, g=g, b=b: e.matmul(ps[b][:, :], wpool_bf[:, g, :], pooledT[:, g, :], start=True,
                                                         stop=True),
                      reads=[B_pooled[g], B_w1], writes=[B_ps[b]])
                S_.op("act", lambda e, g=g, b=b: e.activation(out=mixT[:, g, :], in_=ps[b][:, :], func=AF.Copy,
                                                              scale=pscale[:, g:g + 1]),
                      reads=[B_ps[b], B_cst], writes=[B_mix[g]])

        deferred = []

        def flush_deferred(now=None):
            keep = []
            for due, fn in deferred:
                if now is None or due <= now:
                    fn()
                else:
                    keep.append((due, fn))
            deferred[:] = keep

        def emit_attention(tt, steps, psched):
            nb = 4 * tt + 4
            blocks = [(h, j) for h in range(H) for j in range(nb)]
            LOOK = 2
            state = {}
            steps = list(steps)

            def qk(n):
                h, j = blocks[n]
                a = h // 2
                diag = j >= 4 * tt
                c0 = 128 * (j - 4 * tt) if diag else 0
                sb = ST_BANKS[n % 3]
                pi = n % NPT

                def mm(e):
                    e.matmul(ps[sb][:, c0:TT], kT[:, a, j * 128:(j + 1) * 128], qz[:, h, c0:TT],
                             start=True, stop=False)
                    ins = e.matmul(ps[sb][:, c0:TT], selneg[:, h, :], CT_bf[:, c0:TT], start=False,
                                   stop=(not diag))
                    if diag:
                        ins = e.matmul(ps[sb][:, c0:c0 + 128], ident_bf, maskb, start=False, stop=True)
                    return ins

                S_.op("pe", mm, reads=[B_kT[a][j // 4], B_qz[h], B_CT, B_cst], writes=[B_ps[sb]])
                S_.op("act", lambda e: e.activation(out=pT[pi][:, c0:TT], in_=ps[sb][:, c0:TT], func=AF.Exp,
                                                    bias=Cst[:, j, h:h + 1], scale=1.0),
                      reads=[B_ps[sb], B_Cst[j // 4]], writes=[B_pT[pi]])
                state[n] = (c0, pi)

            def pv(n, it):
                h, j = blocks[n]
                a, p0 = h // 2, 64 * (h % 2)
                c0, pi = state.pop(n)
                ab = ACC_BANKS[h % 2]
                S_.op("pe", lambda e: e.matmul(ps[ab][0:65, c0:TT], vst[:, j, h, :], pT[pi][:, c0:TT], start=(j == 0),
                                               stop=(j == nb - 1)),
                      reads=[B_pT[pi], B_v[j], B_vones], writes=[B_ps[ab]])
                if j == nb - 1:
                    nrb = nr[h % 2]
                    Bn = B_nr[h % 2]
                    S_.op("dve", lambda e: e.tensor_copy(out=nrb[0:64, :], in_=ps[ab][0:64, :]),
                          reads=[B_ps[ab], B_init], writes=[Bn])
                    S_.op("dve", lambda e: e.reciprocal(out=nrb[64:65, :], in_=ps[ab][64:65, :]),
                          reads=[B_ps[ab], B_init], writes=[Bn])

                    def fin():
                        b = next_pb()
                        S_.op("pe", lambda e: e.matmul(ps[b][:, :], e64_f, nrb, start=True, stop=True),
                              reads=[Bn, B_cst], writes=[B_ps[b]])
                        S_.op("dve", lambda e: e.tensor_tensor(out=attnT[p0:p0 + 64, a, :], in0=nrb[0:64, :],
                                                               in1=ps[b][0:64, :], op=ALU.mult),
                              reads=[Bn, B_ps[b]], writes=[B_attn[h]])
                    deferred.append((it + 7, fin))

            nblk = len(blocks)
            start_steps = max(2, nblk // 8)
            for n in range(nblk + LOOK):
                if n < nblk:
                    qk(n)
                flush_deferred(n)
                if n - LOOK >= 0:
                    pv(n - LOOK, n)
                if n >= start_steps and steps:
                    steps.pop(0)()
                for fn in psched.pop(n - 3, []):
                    fn()
            for st in steps:
                st()
            for k in sorted(psched):
                for fn in psched[k]:
                    fn()

        def emit_wout(tt):
            i = tt % 2
            for m in range(KC):
                b = next_pb()

                def mm(e, m=m, b=b):
                    ins = None
                    for ec in range(8):
                        rhs = attnT[:, ec, :] if ec < 4 else mixT[:, ec - 4, :]
                        ins = e.matmul(ps[b][:, :], wout_bf[:, ec, m * 128:(m + 1) * 128], rhs, start=(ec == 0),
                                       stop=(ec == 7))
                    return ins

                S_.op("pe", mm, reads=B_attn + B_mix + [B_w1], writes=[B_ps[b]])
                S_.op("dve", lambda e, m=m, b=b: e.tensor_tensor(out=xt[i][:, m, :], in0=ps[b][:, :], in1=xt[i][:, m, :],
                                                                 op=ALU.add),
                      reads=[B_ps[b], B_xt[i][m]], writes=[B_xt[i][m]])
            if tt + 1 < NT or debug:
                S_.dma("sp", f"xs{i}", lambda e: e.dma_start(out=x1_v[:, :, tt * TT:(tt + 1) * TT], in_=xt[i][:]),
                       reads=B_xt[i])

        for st in norm_steps(xt[0], B_xt[0], hb, B_hb, g1, sq_eng="act"):
            st()
        emit_vf(0)
        emit_fchain_a(0)
        emit_qku(0)
        emit_fchain_b(0)
        for tt in range(NT):
            steps = []
            if tt + 1 < NT:
                emit_load(tt + 1)
                steps = norm_steps(xt[(tt + 1) % 2], B_xt[(tt + 1) % 2], hb, B_hb, g1)
            emit_attention(tt, steps, pool_sched(tt))
            emit_poolmix(tt)
            if tt + 1 < NT:
                emit_vf(tt + 1)
                emit_fchain_a(tt + 1)
            flush_deferred()
            emit_wout(tt)
            if tt + 1 < NT:
                emit_qku(tt + 1)
                emit_fchain_b(tt + 1)
            if tt + 1 == NT - 1 or NT == 1:
                emit_w2_block(0, extra_writes=B_win)

        S_.barrier()

        B_yt = B_xt
        ORDER = [NT - 1] + list(range(NT - 1))
        BUFI = lambda k: (NT - 1 + k) % 2
        B_hb2 = [Buf(f"hb2_{c}") for c in range(KC)]
        B_act = [Buf(f"act{c}") for c in range(FC)]
        B_sil = [Buf("sil0"), Buf("sil1")]
        B_xsq2 = [Buf(f"xsq2_{i}") for i in range(NXSQ2)]
        B_lnt2, B_rstd2 = Buf("lnt2"), Buf("rstd2")
        B_ps2 = [Buf(f"ps2_{i}") for i in range(8)]
        GB, UB, DB, NB = [0, 1], [2, 3], [4, 5], 6

        def emit_load2(k):
            i, tt = BUFI(k), ORDER[k]
            S_.dma("sp", f"yl{i}", lambda e: e.dma_start(out=yt[i][:], in_=x1_v[:, :, tt * TT:(tt + 1) * TT]),
                   writes=B_yt[i])

        for bi in (1, 2, 3):
            emit_w2_block(bi)
        for bi, (f0, f1) in enumerate(FBLK):
            S_.dma("pool", f"d{bi}", lambda e, f0=f0, f1=f1: e.dma_start(out=wd_bf[:, f0:f1, :], in_=wd_v[:, f0:f1, :]),
                   writes=[B_wd[bi]])

        def norm2_steps(xtile, B_x, out_fn, bank, sq_eng="pool"):
            steps = []

            def sq(c):
                k = c % NXSQ2
                if sq_eng == "act":
                    S_.op("act", lambda e: e.activation(out=xsq2[k], in_=xtile[:, c, :], func=AF.Square),
                          reads=[B_x[c]], writes=[B_xsq2[k]])
                else:
                    S_.op("pool", lambda e: e.tensor_tensor(out=xsq2[k], in0=xtile[:, c, :], in1=xtile[:, c, :],
                                                            op=ALU.mult),
                          reads=[B_x[c]], writes=[B_xsq2[k]])

            def mm(c):
                k = c % NXSQ2
                S_.op("pe", lambda e: e.matmul(ps[bank][:, :], ones_bf, xsq2[k], start=(c == 0), stop=(c == KC - 1)),
                      reads=[B_xsq2[k], B_cst], writes=[B_ps2[bank]])

            LAG = 1
            for c in range(KC + LAG):
                def st(c=c):
                    if c < KC:
                        sq(c)
                    if c - LAG >= 0:
                        mm(c - LAG)
                steps.append(st)

            def rs():
                S_.op("act", lambda e: e.activation(out=lnt2, in_=ps[bank][:, :], func=AF.Ln, bias=eps_t[:, 0:1],
                                                    scale=1.0 / D),
                      reads=[B_ps2[bank], B_cst], writes=[B_lnt2] + B_xsq2)
                S_.op("act", lambda e: e.activation(out=rstd2, in_=lnt2, func=AF.Exp, scale=-0.5),
                      reads=[B_lnt2] + B_xsq2, writes=[B_rstd2])
            steps.append(rs)
            for c in range(KC):
                steps.append(lambda c=c: out_fn(c))
            return steps

        def hb_steps(k, sq_eng="pool"):
            i = BUFI(k)
            xtile = yt[i]

            def mk_hb(c):
                S_.op("dve", lambda e: e.scalar_tensor_tensor(out=hb2[:, c, :], in0=xtile[:, c, :],
                                                              scalar=g2[:, c:c + 1], in1=rstd2,
                                                              op0=ALU.mult, op1=ALU.mult),
                      reads=[B_yt[i][c], B_rstd2, B_cst], writes=[B_hb2[c]])
            return norm2_steps(xtile, B_yt[i], mk_hb, NB, sq_eng)

        def final_steps(k, sq_eng="pool"):
            i, tt = BUFI(k), ORDER[k]
            xtile = yt[i]

            def mk_y(c):
                S_.op("dve", lambda e: e.scalar_tensor_tensor(out=xtile[:, c, :], in0=xtile[:, c, :],
                                                              scalar=gf[:, c:c + 1], in1=rstd2,
                                                              op0=ALU.mult, op1=ALU.mult),
                      reads=[B_yt[i][c], B_rstd2, B_cst], writes=[B_yt[i][c]])
                S_.dma("sp", f"ys{i}", lambda e: e.dma_start(out=yT_v[:, c, tt * TT:(tt + 1) * TT], in_=xtile[:, c, :]),
                       reads=[B_yt[i][c]])
            return norm2_steps(xtile, B_yt[i], mk_y, 7, sq_eng)

        def fblk_of(fc):
            for bi, (f0, f1) in enumerate(FBLK):
                if f0 <= fc < f1:
                    return bi, fc - f0
            raise AssertionError

        def emit_gateup(tt, pending):
            for fc in range(FC):
                gb, ub = GB[fc % 2], UB[fc % 2]
                bi, fo = fblk_of(fc)

                def mmg(e, bi=bi, fo=fo, gb=gb):
                    ins = None
                    for c in range(KC):
                        ins = e.matmul(ps[gb][:, :], wg_blk[bi][:, c, fo * 128:(fo + 1) * 128], hb2[:, c, :],
                                       start=(c == 0), stop=(c == KC - 1))
                    return ins

                def mmu(e, bi=bi, fo=fo, ub=ub):
                    ins = None
                    for c in range(KC):
                        ins = e.matmul(ps[ub][:, :], wu_blk[bi][:, c, fo * 128:(fo + 1) * 128], hb2[:, c, :],
                                       start=(c == 0), stop=(c == KC - 1))
                    return ins

                S_.op("pe", mmg, reads=B_hb2 + [B_wg[bi]], writes=[B_ps2[gb]])
                S_.op("pe", mmu, reads=B_hb2 + [B_wu[bi]], writes=[B_ps2[ub]])
                sl = fc % 2
                S_.op("act", lambda e, gb=gb, sl=sl: e.activation(out=sil[sl], in_=ps[gb][:, :], func=AF.Silu),
                      reads=[B_ps2[gb]], writes=[B_sil[sl]])
                S_.op("dve", lambda e, fc=fc, ub=ub, sl=sl: e.tensor_tensor(out=actT[:, fc, :], in0=ps[ub][:, :],
                                                                            in1=sil[sl], op=ALU.mult),
                      reads=[B_ps2[ub], B_sil[sl]], writes=[B_act[fc]])
                if pending and fc >= 1:
                    pending.pop(0)()
            while pending:
                pending.pop(0)()

        def emit_down(k, pending, pops=3):
            i = BUFI(k)
            xtile = yt[i]
            for m in range(KC):
                db = DB[m % 2]

                def mmd(e, m=m, db=db):
                    ins = None
                    for fc in range(FC):
                        ins = e.matmul(ps[db][:, :], wd_bf[:, fc, m * 128:(m + 1) * 128], actT[:, fc, :],
                                       start=(fc == 0), stop=(fc == FC - 1))
                    return ins

                S_.op("pe", mmd, reads=B_act + B_wd, writes=[B_ps2[db]])
                S_.op("dve", lambda e, m=m, db=db: e.tensor_tensor(out=xtile[:, m, :], in0=ps[db][:, :],
                                                                   in1=xtile[:, m, :], op=ALU.add),
                      reads=[B_ps2[db], B_yt[i][m]], writes=[B_yt[i][m]])
                for _ in range(pops):
                    if pending:
                        pending.pop(0)()
            while pending:
                pending.pop(0)()

        for st in hb_steps(0, sq_eng="act"):
            st()
        pend_final = []
        for k in range(NT):
            emit_gateup(k, pend_final)
            if k + 1 < NT:
                emit_load2(k + 1)
            if k + 1 < NT:
                emit_down(k, hb_steps(k + 1))
                pend_final = final_steps(k)
            else:
                emit_down(k, final_steps(k, sq_eng="act"), pops=1)
                pend_final = []
        for st in pend_final:
            st()

        S_.barrier()

        print(f"[kernel] arena words: phase1 {p1_end} phase2 {p2_end} of {AW}; "
              f"sem counts: { {k: v for k, v in S_.cnt.items()} }")

        @block.tensor
        def _(e):
            S_.replay("pe", e, semh)

        @block.scalar
        def _(e):
            S_.replay("act", e, semh)

        @block.vector
        def _(e):
            S_.replay("dve", e, semh)

        @block.gpsimd
        def _(e):
            S_.replay("pool", e, semh)

        @block.sync
        def _(e):
            S_.replay("sp", e, semh)

    return nc


def _consts(norm1_g, norm2_g, final_g, pool_scale, b_forget):
    cf = np.zeros((128, NCF), np.float32)
    cf[:, CF_G1:CF_G1 + 8] = np.asarray(norm1_g, np.float32).reshape(8, 128).T
    cf[:, CF_G2:CF_G2 + 8] = np.asarray(norm2_g, np.float32).reshape(8, 128).T
    cf[:, CF_GF:CF_GF + 8] = np.asarray(final_g, np.float32).reshape(8, 128).T
    cf[:, CF_PS:CF_PS + 4] = np.asarray(pool_scale, np.float32).reshape(4, 128).T
    cf[:, CF_BF:CF_BF + 32] = np.tile(np.asarray(b_forget, np.float32).reshape(1, 8), (128, 4))
    t = np.arange(16)
    ic = np.stack([1.0 / np.minimum(t + 1, 2 << g) for g in range(4)]).astype(np.float32)
    cf[:, CF_IC:CF_IC + 64] = ic.reshape(1, 64)
    kk = np.arange(128)
    cf[:, CF_TRI:CF_TRI + 128] = (kk[:, None] <= kk[None, :]).astype(np.float32)
    cf[64, CF_E64:CF_E64 + 128] = 1.0
    cb = np.zeros((128, NCB), np.float32)
    cb[:, CB_ID:CB_ID + 128] = np.eye(128, dtype=np.float32)
    cb[:, CB_MASK:CB_MASK + 128] = np.where(kk[:, None] <= kk[None, :], 0.0, NEG).astype(np.float32)
    for h in range(8):
        cb[h, CB_SEL + h * 128:CB_SEL + (h + 1) * 128] = -1.0
    return cf, cb


_NC_CACHE = {}


def kernel(x, norm1_g, w_in, b_forget, w_pool, pool_scale, w_out, norm2_g, w_gate, w_up, w_down, final_g):
    x = np.asarray(x, np.float32)
    B = x.shape[0]
    cf, cb = _consts(np.asarray(norm1_g)[0], np.asarray(norm2_g)[0], np.asarray(final_g), np.asarray(pool_scale)[0],
                     np.asarray(b_forget)[0])
    shared = {
        "w_in": np.ascontiguousarray(np.asarray(w_in, np.float32)[0]),
        "w_out": np.ascontiguousarray(np.asarray(w_out, np.float32)[0]),
        "w_pool": np.ascontiguousarray(np.asarray(w_pool, np.float32)[0]),
        "w_gate": np.ascontiguousarray(np.asarray(w_gate, np.float32)[0]),
        "w_up": np.ascontiguousarray(np.asarray(w_up, np.float32)[0]),
        "w_down": np.ascontiguousarray(np.asarray(w_down, np.float32)[0]),
        "cf": cf,
        "cb": cb,
    }
    in_maps = [dict(shared, xT=np.ascontiguousarray(x[b].T)) for b in range(B)]
    if "nc" not in _NC_CACHE:
        _NC_CACHE["nc"] = build_program()
    nc = _NC_CACHE["nc"]
    res = run_bass_kernel_spmd(nc, in_maps, core_ids=list(range(B)))
    out = np.stack([np.asarray(res.results[b]["yT"]).T for b in range(B)])
    return np.ascontiguousarray(out.astype(np.float32))
```

```python
from contextlib import ExitStack

import numpy as np
import concourse.bass as bass
import concourse.mybir as mybir
from concourse.bass_utils import run_bass_kernel_spmd

F32 = mybir.dt.float32
BF16 = mybir.dt.bfloat16
AF = mybir.ActivationFunctionType
ALU = mybir.AluOpType

S = 4096
D = 1024
TT = 512
NT = S // TT
KC = D // 128
H = 8
DFF = 2816
FC = DFF // 128
INW = 2056
EPS = 1e-6
NEG = -30000.0

CF_G1, CF_G2, CF_GF, CF_PS, CF_BF, CF_IC, CF_TRI = 0, 8, 16, 24, 28, 60, 124
CF_E64 = 124 + 128
NCF = 124 + 128 + 128
CB_ID, CB_MASK, CB_SEL = 0, 128, 256
NCB = 256 + 1024


class Buf:
    __slots__ = ("name", "w", "r")

    def __init__(self, name):
        self.name = name
        self.w = None
        self.r = []


class Sched:
    ENGS = ("pe", "act", "dve", "pool", "sp")

    def __init__(self):
        self.prog = {e: [] for e in self.ENGS}
        self.cnt = {}
        self.waited = {e: {} for e in self.ENGS}

    @staticmethod
    def _flat(bufs):
        out = []
        for b in bufs:
            if isinstance(b, (list, tuple)):
                out.extend(Sched._flat(b))
            else:
                out.append(b)
        return out

    def _deps(self, eng, reads, writes):
        reads, writes = self._flat(reads), self._flat(writes)
        need = {}

        def add(t):
            if t is not None and need.get(t[0], 0) < t[1]:
                need[t[0]] = t[1]

        for b in reads:
            add(b.w)
        for b in writes:
            add(b.w)
            for t in b.r:
                add(t)
        out = []
        for k, v in need.items():
            if k == eng and eng == "pe":
                continue
            if self.waited[eng].get(k, 0) >= v:
                continue
            self.waited[eng][k] = v
            out.append((k, v))
        return out

    def _commit(self, tok, reads, writes):
        reads, writes = self._flat(reads), self._flat(writes)
        for b in reads:
            b.r.append(tok)
        for b in writes:
            b.w = tok
            b.r = []

    def op(self, eng, fn, reads=(), writes=()):
        waits = self._deps(eng, reads, writes)
        self.cnt[eng] = self.cnt.get(eng, 0) + 1
        tok = (eng, self.cnt[eng])
        self.prog[eng].append((waits, fn, (eng, 1)))
        self._commit(tok, reads, writes)
        return tok

    def dma(self, queue, semkey, fn, reads=(), writes=(), track=True):
        waits = self._deps(queue, reads, writes) if track else []
        self.cnt[semkey] = self.cnt.get(semkey, 0) + 16
        tok = (semkey, self.cnt[semkey])
        self.prog[queue].append((waits, fn, (semkey, 16)))
        if track:
            self._commit(tok, reads, writes)
        return tok

    def barrier(self):
        for e in self.ENGS:
            waits = []
            for k, v in self.cnt.items():
                if k == e and e == "pe":
                    continue
                if self.waited[e].get(k, 0) >= v:
                    continue
                self.waited[e][k] = v
                waits.append((k, v))
            if waits:
                self.prog[e].append((waits, None, None))

    def replay(self, eng, e, semh):
        for waits, fn, sig in self.prog[eng]:
            for k, v in waits:
                e.wait_ge(semh[k], v)
            if fn is None:
                continue
            ins = fn(e)
            ins.then_inc(semh[sig[0]], sig[1])


class Arena:
    def __init__(self, ap):
        self.ap = ap
        self.off = 0
        self.words = ap.shape[1]

    def alloc(self, n, dtype):
        nbytes = n * (2 if dtype == BF16 else 4)
        words = (nbytes + 31) // 32 * 8
        assert self.off + words <= self.words, ("arena overflow", self.off, words, self.words)
        v = self.ap[:, self.off:self.off + words]
        self.off += words
        if dtype == BF16:
            return v.bitcast(BF16)[:, 0:n]
        return v[:, 0:n]


def build_program(debug=False):
    nc = bass.Bass("TRN2", target_bir_lowering=False)
    dt_in = lambda name, shape: nc.dram_tensor(name, shape, F32, kind="ExternalInput").ap()
    xT = dt_in("xT", [D, S])
    w_in = dt_in("w_in", [D, INW])
    w_out = dt_in("w_out", [D, D])
    w_pool = dt_in("w_pool", [4, 128, 128])
    w_gate = dt_in("w_gate", [D, DFF])
    w_up = dt_in("w_up", [D, DFF])
    w_down = dt_in("w_down", [DFF, D])
    cf_d = dt_in("cf", [128, NCF])
    cb_d = dt_in("cb", [128, NCB])
    x1s = nc.dram_tensor("x1s", [D, S], F32, kind=("ExternalOutput" if debug else "Internal")).ap()
    yT = nc.dram_tensor("yT", [D, S], F32, kind="ExternalOutput").ap()

    xT_v = xT.rearrange("(c p) t -> p c t", p=128)
    x1_v = x1s.rearrange("(c p) t -> p c t", p=128)
    yT_v = yT.rearrange("(c p) t -> p c t", p=128)

    S_ = Sched()
    semnames = ["pe", "act", "dve", "pool", "cst", "cst2", "wi0", "wi1", "wi2", "wi3", "w1", "g0", "g1", "g2", "g3", "u0", "u1", "u2", "u3", "d0", "d1", "d2", "d3", "xl0", "xl1", "xs0", "xs1",
                "yl0", "yl1", "ys0", "ys1"]

    with ExitStack() as ctx:
        AW = 53100
        arena_t = ctx.enter_context(nc.sbuf_tensor("arena", [128, AW], F32))
        ps = [ctx.enter_context(nc.psum_tensor(f"ps{i}", [128, 512], F32)) for i in range(8)]
        semh = {n: ctx.enter_context(nc.semaphore(n)) for n in semnames}
        block = ctx.enter_context(nc.Block())

        ar = Arena(arena_t[:, :])
        cf = ar.alloc(NCF, F32)
        cb = ar.alloc(NCB, BF16)
        ones_bf = ar.alloc(128, BF16)
        ones_f = ar.alloc(128, F32)
        eps_t = ar.alloc(8, F32)
        xt = [ar.alloc(KC * TT, F32).rearrange("p (c t) -> p c t", t=TT) for _ in range(2)]
        base_off = ar.off
        B_cf, B_cb, B_ones = Buf("cf"), Buf("cb"), Buf("ones")
        B_cst = [B_cf, B_cb, B_ones]

        g1 = cf[:, CF_G1:CF_G1 + 8]
        g2 = cf[:, CF_G2:CF_G2 + 8]
        gf = cf[:, CF_GF:CF_GF + 8]
        pscale = cf[:, CF_PS:CF_PS + 4]
        bfor = cf[:, CF_BF:CF_BF + 32]
        invcnt = cf[:, CF_IC:CF_IC + 64].rearrange("p (g t) -> p g t", t=16)
        tri_f = cf[:, CF_TRI:CF_TRI + 128]
        e64_f = cf[:, CF_E64:CF_E64 + 128]
        ident_bf = cb[:, CB_ID:CB_ID + 128]
        maskb = cb[:, CB_MASK:CB_MASK + 128]
        selneg = cb[:, CB_SEL:CB_SEL + 1024].rearrange("p (h m) -> p h m", m=128)

        S_.dma("sp", "cst", lambda e: e.dma_start(out=cf, in_=cf_d), writes=[B_cf])
        S_.dma("pool", "cst2", lambda e: e.dma_start(out=cb, in_=cb_d), writes=[B_cb])
        B_xt = [[Buf(f"xt{i}_{c}") for c in range(KC)] for i in range(2)]
        S_.dma("sp", "xl0", lambda e: e.dma_start(out=xt[0][:], in_=xT_v[:, :, 0:TT]), writes=B_xt[0])
        S_.op("pool", lambda e: e.memset(ones_bf, 1.0), writes=[B_ones])
        S_.op("pool", lambda e: e.memset(ones_f, 1.0), writes=[B_ones])
        S_.op("pool", lambda e: e.memset(eps_t, EPS), writes=[B_ones])

        win_bf = ar.alloc(KC * INW, BF16).rearrange("p (c n) -> p c n", n=INW)
        wout_bf = ar.alloc(KC * D, BF16).rearrange("p (c n) -> p c n", n=D)
        wpool_bf = ar.alloc(4 * 128, BF16).rearrange("p (g n) -> p g n", n=128)
        kT = ar.alloc(4 * S, BF16).rearrange("p (a t) -> p a t", t=S)
        NJ = S // 128
        vst = ar.alloc(NJ * H * 65, BF16).rearrange("p (j h d) -> p j h d", h=H, d=65)
        NXSQ = 3
        xsq = [ar.alloc(TT, BF16) for _ in range(NXSQ)]
        hb2d = ar.alloc(KC * TT, BF16)
        hb = hb2d.rearrange("p (c t) -> p c t", t=TT)
        lnt = hb2d[:, 6 * TT:8 * TT].bitcast(F32)
        rstd = ar.alloc(TT, F32)
        qz = ar.alloc(H * TT, BF16).rearrange("p (h t) -> p h t", t=TT)
        UW = TT + 16
        uT = ar.alloc(4 * UW, F32).rearrange("p (g t) -> p g t", t=UW)
        ptmp = [ar.alloc(UW, F32) for _ in range(2)]
        pooledT = ar.alloc(4 * TT, BF16).rearrange("p (g t) -> p g t", t=TT)
        mixT = ar.alloc(4 * TT, BF16).rearrange("p (g t) -> p g t", t=TT)
        attnT = ar.alloc(4 * TT, BF16).rearrange("p (a t) -> p a t", t=TT)
        NPT = 3
        pT = [ar.alloc(TT, BF16) for _ in range(NPT)]
        nr = [ar.alloc(TT, F32) for _ in range(2)]
        CT_bf = ar.alloc(TT, BF16)
        fz = ar.alloc(32, F32)
        lst = ar.alloc(NJ * H, F32).rearrange("p (n h) -> p n h", h=H)
        accs = ar.alloc((NJ + 1) * H, F32).rearrange("p (n h) -> p n h", h=H)
        Cst = ar.alloc(NJ * H, F32).rearrange("p (n h) -> p n h", h=H)
        tmp16 = ar.alloc(16, F32)
        p1_end = ar.off

        ar.off = base_off
        FBLK = [(0, 6), (6, 12), (12, 17), (17, 22)]
        wg_blk, wu_blk = [None] * 4, [None] * 4
        for bi in (0,):
            f0, f1 = FBLK[bi]
            wg_blk[bi] = ar.alloc(KC * (f1 - f0) * 128, BF16).rearrange("p (c n) -> p c n", c=KC)
            wu_blk[bi] = ar.alloc(KC * (f1 - f0) * 128, BF16).rearrange("p (c n) -> p c n", c=KC)
        assert ar.off <= base_off + (KC * INW * 2 + 3) // 4, "prefetch blocks must fit in the w_in region"
        for bi in (1, 2, 3):
            f0, f1 = FBLK[bi]
            wg_blk[bi] = ar.alloc(KC * (f1 - f0) * 128, BF16).rearrange("p (c n) -> p c n", c=KC)
            wu_blk[bi] = ar.alloc(KC * (f1 - f0) * 128, BF16).rearrange("p (c n) -> p c n", c=KC)
        wd_bf = ar.alloc(FC * D, BF16).rearrange("p (c n) -> p c n", n=D)
        yt = xt
        hb2 = ar.alloc(KC * TT, BF16).rearrange("p (c t) -> p c t", t=TT)
        lnt2 = ar.alloc(TT, F32)
        xsq2 = [lnt2.bitcast(BF16)[:, 0:TT], lnt2.bitcast(BF16)[:, TT:2 * TT]]
        NXSQ2 = 2
        rstd2 = ar.alloc(TT, F32)
        actT = ar.alloc(FC * TT, BF16).rearrange("p (c t) -> p c t", t=TT)
        sil = [ar.alloc(TT, F32) for _ in range(2)]
        p2_end = ar.off
        wg_v = w_gate.rearrange("(c p) n -> p c n", p=128)
        wu_v = w_up.rearrange("(c p) n -> p c n", p=128)
        wd_v = w_down.rearrange("(c p) n -> p c n", p=128)
        B_wg = [Buf(f"wg{i}") for i in range(4)]
        B_wu = [Buf(f"wu{i}") for i in range(4)]
        B_wd = [Buf(f"wd{i}") for i in range(4)]

        def emit_w2_block(bi, extra_writes=()):
            f0, f1 = FBLK[bi]
            S_.dma("pool", f"g{bi}", lambda e: e.dma_start(out=wg_blk[bi][:, :, :], in_=wg_v[:, :, f0 * 128:f1 * 128]),
                   writes=[B_wg[bi]] + list(extra_writes))
            S_.dma("pool", f"u{bi}", lambda e: e.dma_start(out=wu_blk[bi][:, :, :], in_=wu_v[:, :, f0 * 128:f1 * 128]),
                   writes=[B_wu[bi]] + list(extra_writes))

        B_w1 = Buf("w1")
        win_v = w_in.rearrange("(c p) n -> p c n", p=128)
        B_win = [Buf(f"win{i}") for i in range(4)]
        for i, (c0, c1) in enumerate([(1024, 1544), (0, 512), (512, 1024), (1544, 2056)]):
            S_.dma("pool", f"wi{i}", lambda e, c0=c0, c1=c1: e.dma_start(out=win_bf[:, :, c0:c1], in_=win_v[:, :, c0:c1]),
                   writes=[B_win[i]])
        for c in range(KC):
            S_.dma("pool", "w1", lambda e, c=c: e.dma_start(out=wout_bf[:, c, :], in_=w_out[c * 128:(c + 1) * 128, :]),
                   track=False)
        for g in range(4):
            S_.dma("pool", "w1", lambda e, g=g: e.dma_start(out=wpool_bf[:, g, :], in_=w_pool[g]), track=False)
        B_w1.w = ("w1", S_.cnt["w1"])
        B_xsq = [Buf(f"xsq{i}") for i in range(NXSQ)]
        B_hb = [Buf(f"hb{c}") for c in range(KC)]
        B_rstd = Buf("rstd")
        B_ps = [Buf(f"ps{i}") for i in range(8)]
        B_qz = [Buf(f"qz{h}") for h in range(H)]
        B_kT = [[Buf(f"kT{a}_{t}") for t in range(NT)] for a in range(4)]
        B_v = [Buf(f"v{j}") for j in range(NJ)]
        B_ptmp = [Buf("ptmp0"), Buf("ptmp1")]
        B_pooled = [Buf(f"pooled{g}") for g in range(4)]
        B_mix = [Buf(f"mix{g}") for g in range(4)]
        B_attn = [Buf(f"attn{h}") for h in range(H)]
        B_pT = [Buf(f"pT{i}") for i in range(NPT)]
        B_nr, B_CT = [Buf("nr0"), Buf("nr1")], Buf("CTbf")
        B_fz = Buf("fz")
        B_lst = [Buf(f"lst{n}") for n in range(NJ)]
        B_Cst = [Buf(f"Cst{t}") for t in range(NT)]
        B_tmp16 = Buf("tmp16")

        ST_BANKS = [0, 1, 2]
        ACC_BANKS = [3, 4]
        PB_BANKS = [5, 6]
        MISC = 7
        pb_ctr = [0]

        def next_pb():
            b = PB_BANKS[pb_ctr[0] % 2]
            pb_ctr[0] += 1
            return b

        xsq_ctr = [0]

        B_vones = Buf("vones")
        B_uT = [Buf(f"uT{g}") for g in range(4)]
        B_uhalo = Buf("uhalo")
        B_acc = [Buf(f"acc{n}") for n in range(NJ + 1)]
        S_.op("pool", lambda e: e.memset(vst[:, :, :, 64:65], 1.0), writes=[B_vones])
        S_.op("pool", lambda e: e.memset(uT[:, :, 0:16], 0.0), writes=[B_uhalo])
        S_.op("pool", lambda e: e.memset(accs[:, 0, :], 0.0), writes=[B_acc[0]])
        B_init = Buf("init")
        S_.op("pool", lambda e: e.memset(qz[:, :, :], 0.0), writes=[B_init])
        S_.op("pool", lambda e: e.memset(CT_bf, 0.0), writes=[B_init])
        S_.op("pool", lambda e: e.memset(nr[0], 0.0), writes=[B_init])
        S_.op("pool", lambda e: e.memset(nr[1], 0.0), writes=[B_init])

        def emit_load(tt):
            i = tt % 2
            S_.dma("sp", f"xl{i}", lambda e: e.dma_start(out=xt[i][:], in_=xT_v[:, :, tt * TT:(tt + 1) * TT]),
                   writes=B_xt[i])

        def norm_steps(xtile, B_x, hbt, B_h, gcol, misc_bank=MISC, sq_eng="pool"):
            steps = []
            ks = []
            for c in range(KC):
                k = xsq_ctr[0] % NXSQ
                xsq_ctr[0] += 1
                ks.append(k)

            def sq(c):
                k = ks[c]
                if sq_eng == "act":
                    S_.op("act", lambda e: e.activation(out=xsq[k], in_=xtile[:, c, :], func=AF.Square),
                          reads=[B_x[c]], writes=[B_xsq[k]])
                else:
                    S_.op("pool", lambda e: e.tensor_tensor(out=xsq[k], in0=xtile[:, c, :], in1=xtile[:, c, :],
                                                            op=ALU.mult),
                          reads=[B_x[c]], writes=[B_xsq[k]])

            def mm(c):
                k = ks[c]
                S_.op("pe", lambda e: e.matmul(ps[misc_bank][:, :], ones_bf, xsq[k], start=(c == 0),
                                               stop=(c == KC - 1)),
                      reads=[B_xsq[k], B_cst], writes=[B_ps[misc_bank]])

            LAG = 2
            for c in range(KC + LAG):
                def st(c=c):
                    if c < KC:
                        sq(c)
                    if c - LAG >= 0:
                        mm(c - LAG)
                steps.append(st)

            def rs():
                S_.op("act", lambda e: e.activation(out=lnt, in_=ps[misc_bank][:, :], func=AF.Ln, bias=eps_t[:, 0:1],
                                                    scale=1.0 / D),
                      reads=[B_ps[misc_bank], B_cst], writes=[B_h[6], B_h[7]])
                S_.op("act", lambda e: e.activation(out=rstd, in_=lnt, func=AF.Exp, scale=-0.5),
                      reads=[B_h[6], B_h[7]], writes=[B_rstd])
            steps.append(rs)
            for c in range(KC):
                def hbs(c=c):
                    S_.op("dve", lambda e: e.scalar_tensor_tensor(out=hbt[:, c, :], in0=xtile[:, c, :],
                                                                  scalar=gcol[:, c:c + 1], in1=rstd,
                                                                  op0=ALU.mult, op1=ALU.mult),
                          reads=[B_x[c], B_rstd, B_cst], writes=[B_h[c]])
                steps.append(hbs)
            return steps

        evac_ctr = [0]

        def evac_copy(out_ap, in_ap, reads, writes, scale=None):
            k = evac_ctr[0]
            evac_ctr[0] += 1
            if k % 2 == 0:
                if scale is None:
                    S_.op("act", lambda e: e.activation(out=out_ap, in_=in_ap, func=AF.Copy), reads=reads, writes=writes)
                else:
                    S_.op("act", lambda e: e.activation(out=out_ap, in_=in_ap, func=AF.Copy, scale=scale),
                          reads=reads, writes=writes)
            else:
                if scale is None:
                    S_.op("dve", lambda e: e.tensor_copy(out=out_ap, in_=in_ap), reads=reads, writes=writes)
                else:
                    S_.op("dve", lambda e: e.tensor_scalar(out=out_ap, in0=in_ap, scalar1=scale, scalar2=None,
                                                           op0=ALU.mult), reads=reads, writes=writes)

        def emit_vf(tt):
            for blk in range(4):
                b = next_pb()
                j = tt * 4 + blk

                def mmv(e, blk=blk, b=b):
                    ins = None
                    for c in range(KC):
                        lhsT = hb[:, c, blk * 128:(blk + 1) * 128]
                        e.matmul(ps[b][:, :], lhsT, win_bf[:, c, 1024:1536], start=(c == 0), stop=(c == KC - 1))
                        ins = e.matmul(ps[MISC][:, blk * 8:(blk + 1) * 8], lhsT, win_bf[:, c, 1536:1544],
                                       start=(c == 0), stop=(c == KC - 1))
                    return ins

                S_.op("pe", mmv, reads=B_hb + [B_win[0]], writes=[B_ps[b], B_ps[MISC]])
                evac_copy(vst[:, j, :, 0:64], ps[b][:, :].rearrange("p (h d) -> p h d", d=64), [B_ps[b]], [B_v[j]])

        def emit_qku(tt):
            specs = []
            for a in range(4):
                specs.append(("q", a, a * 128))
            for a in range(4):
                specs.append(("k", a, 512 + a * 128))
            for g in range(4):
                specs.append(("u", g, 1544 + g * 128))
            for kind, idx, col in specs:
                b = next_pb()

                def mm(e, col=col, b=b):
                    ins = None
                    for c in range(KC):
                        ins = e.matmul(ps[b][:, :], win_bf[:, c, col:col + 128], hb[:, c, :], start=(c == 0),
                                       stop=(c == KC - 1))
                    return ins

                wb = {"q": B_win[1], "k": B_win[2], "u": B_win[3]}[kind]
                S_.op("pe", mm, reads=B_hb + [wb], writes=[B_ps[b]])
                if kind == "q":
                    evac_copy(qz[0:64, 2 * idx, :], ps[b][0:64, :], [B_ps[b], B_init], [B_qz[2 * idx]], scale=0.125)
                    evac_copy(qz[64:128, 2 * idx + 1, :], ps[b][64:128, :], [B_ps[b], B_init], [B_qz[2 * idx + 1]],
                              scale=0.125)
                elif kind == "k":
                    evac_copy(kT[:, idx, tt * TT:(tt + 1) * TT], ps[b][:, :], [B_ps[b]], [B_kT[idx][tt]])
                else:
                    evac_copy(uT[:, idx, 16:UW], ps[b][:, :], [B_ps[b]], [B_uT[idx]])

        def emit_fchain_a(tt):
            n0 = tt * 4
            S_.op("dve", lambda e: e.tensor_tensor(out=fz, in0=ps[MISC][:, 0:32], in1=bfor, op=ALU.add),
                  reads=[B_ps[MISC], B_cst], writes=[B_fz])
            S_.op("act", lambda e: e.activation(out=fz, in_=fz, func=AF.Exp, scale=-1.0), reads=[B_fz], writes=[B_fz])
            lview = lst[:, n0:n0 + 4, :].rearrange("p n h -> p (n h)")
            S_.op("act", lambda e: e.activation(out=lview, in_=fz, func=AF.Ln, bias=1.0, scale=1.0),
                  reads=[B_fz], writes=[B_lst[n0 + i] for i in range(4)])
            for i in range(4):
                n = n0 + i
                S_.op("dve", lambda e, n=n: e.tensor_tensor(out=accs[:, n + 1, :], in0=accs[:, n, :], in1=lst[:, n, :],
                                                            op=ALU.add),
                      reads=[B_acc[n], B_lst[n]], writes=[B_acc[n + 1]])

        def emit_fchain_b(tt):
            n0 = tt * 4
            bct = next_pb()

            def mmc(e):
                ins = None
                for i in range(4):
                    n = n0 + i
                    e.matmul(ps[MISC][:, 32 + i * 8:40 + i * 8], tri_f, lst[:, n, :], start=True, stop=False)
                    e.matmul(ps[MISC][:, 32 + i * 8:40 + i * 8], ones_f, accs[:, n, :], start=False, stop=True)
                for i in range(4):
                    n = n0 + i
                    e.matmul(ps[bct][0:8, i * 128:(i + 1) * 128], lst[:, n, :], tri_f, start=True, stop=False)
                    ins = e.matmul(ps[bct][0:8, i * 128:(i + 1) * 128], accs[:, n, :], ones_f, start=False, stop=True)
                return ins

            S_.op("pe", mmc, reads=[B_lst[n0 + i] for i in range(4)] + [B_acc[n0 + i] for i in range(4)] + [B_cst],
                  writes=[B_ps[MISC], B_ps[bct]])
            S_.op("dve", lambda e: e.tensor_copy(out=Cst[:, n0:n0 + 4, :].rearrange("p n h -> p (n h)"),
                                                 in_=ps[MISC][:, 32:64]),
                  reads=[B_ps[MISC]], writes=[B_Cst[tt]])
            S_.op("dve", lambda e: e.tensor_copy(out=CT_bf[0:8, :], in_=ps[bct][0:8, :]),
                  reads=[B_ps[bct], B_init], writes=[B_CT])

        def pool_sched(tt):
            sched = {}
            offs = [0, 1, 6, 12, 19]

            def chain(g):
                w = 2 << g
                src = uT[:, g, :]
                srcB = [B_uT[g], B_uhalo]
                shift, lo, step = 1, 0, 0
                while shift < w:
                    dst = ptmp[step % 2]
                    nlo = lo + shift
                    S_.op("pool", lambda e, dst=dst, src=src, nlo=nlo, shift=shift:
                          e.tensor_tensor(out=dst[:, nlo:UW], in0=src[:, nlo:UW], in1=src[:, nlo - shift:UW - shift],
                                          op=ALU.add),
                          reads=srcB, writes=[B_ptmp[step % 2]])
                    src = dst
                    srcB = [B_ptmp[step % 2]]
                    lo = nlo
                    shift *= 2
                    step += 1
                return src, srcB

            res = {}

            def do_chain(g):
                res[g] = chain(g)

            def do_final(g):
                w = 2 << g
                src, srcB = res[g]
                S_.op("dve", lambda e: e.scalar_tensor_tensor(
                    out=pooledT[:, g, :], in0=src[:, 16:UW], scalar=1.0 / w, in1=uT[:, g, 16:UW],
                    op0=ALU.mult, op1=ALU.subtract),
                    reads=srcB + [B_uT[g]], writes=[B_pooled[g]])
                if tt == 0:
                    S_.op("dve", lambda e: e.tensor_tensor(out=tmp16, in0=src[:, 16:32], in1=invcnt[:, g, :],
                                                           op=ALU.mult),
                          reads=srcB + [B_cst], writes=[B_tmp16])
                    S_.op("dve", lambda e: e.tensor_tensor(out=pooledT[:, g, 0:16], in0=tmp16, in1=uT[:, g, 16:32],
                                                           op=ALU.subtract),
                          reads=[B_tmp16, B_uT[g]], writes=[B_pooled[g]])

            def halo():
                S_.op("pool", lambda e: e.tensor_copy(out=uT[:, :, 0:16], in_=uT[:, :, TT:UW]),
                      reads=B_uT, writes=[B_uhalo])

            sched.setdefault(offs[0], []).append(lambda: do_chain(0))
            for g in range(4):
                sched.setdefault(offs[g + 1], []).append(lambda g=g: do_final(g))
                if g + 1 < 4:
                    sched.setdefault(offs[g + 1], []).append(lambda g=g: do_chain(g + 1))
            sched.setdefault(offs[4], []).append(halo)
            return sched

        def emit_poolmix(tt):
            for g in range(4):
                b = next_pb()
                S_.op("pe", lambda e, g=g, b=b: e.matmul(ps[b][:, :], wpool_bf[:, g, :], pooledT[:, g, :], start=True,
                                                         stop=True),
                      reads=[B_pooled[g], B_w1], writes=[B_ps[b]])
                S_.op("act", lambda e, g=g, b=b: e.activation(out=mixT[:, g, :], in_=ps[b][:, :], func=AF.Copy,
                                                              scale=pscale[:, g:g + 1]),
                      reads=[B_ps[b], B_cst], writes=[B_mix[g]])

        deferred = []

        def flush_deferred(now=None):
            keep = []
            for due, fn in deferred:
                if now is None or due <= now:
                    fn()
                else:
                    keep.append((due, fn))
            deferred[:] = keep

        def emit_attention(tt, steps, psched):
            nb = 4 * tt + 4
            blocks = [(h, j) for h in range(H) for j in range(nb)]
            LOOK = 2
            state = {}
            steps = list(steps)

            def qk(n):
                h, j = blocks[n]
                a = h // 2
                diag = j >= 4 * tt
                c0 = 128 * (j - 4 * tt) if diag else 0
                sb = ST_BANKS[n % 3]
                pi = n % NPT

                def mm(e):
                    e.matmul(ps[sb][:, c0:TT], kT[:, a, j * 128:(j + 1) * 128], qz[:, h, c0:TT],
                             start=True, stop=False)
                    ins = e.matmul(ps[sb][:, c0:TT], selneg[:, h, :], CT_bf[:, c0:TT], start=False,
                                   stop=(not diag))
                    if diag:
                        ins = e.matmul(ps[sb][:, c0:c0 + 128], ident_bf, maskb, start=False, stop=True)
                    return ins

                S_.op("pe", mm, reads=[B_kT[a][j // 4], B_qz[h], B_CT, B_cst], writes=[B_ps[sb]])
                S_.op("act", lambda e: e.activation(out=pT[pi][:, c0:TT], in_=ps[sb][:, c0:TT], func=AF.Exp,
                                                    bias=Cst[:, j, h:h + 1], scale=1.0),
                      reads=[B_ps[sb], B_Cst[j // 4]], writes=[B_pT[pi]])
                state[n] = (c0, pi)

            def pv(n, it):
                h, j = blocks[n]
                a, p0 = h // 2, 64 * (h % 2)
                c0, pi = state.pop(n)
                ab = ACC_BANKS[h % 2]
                S_.op("pe", lambda e: e.matmul(ps[ab][0:65, c0:TT], vst[:, j, h, :], pT[pi][:, c0:TT], start=(j == 0),
                                               stop=(j == nb - 1)),
                      reads=[B_pT[pi], B_v[j], B_vones], writes=[B_ps[ab]])
                if j == nb - 1:
                    nrb = nr[h % 2]
                    Bn = B_nr[h % 2]
                    S_.op("dve", lambda e: e.tensor_copy(out=nrb[0:64, :], in_=ps[ab][0:64, :]),
                          reads=[B_ps[ab], B_init], writes=[Bn])
                    S_.op("dve", lambda e: e.reciprocal(out=nrb[64:65, :], in_=ps[ab][64:65, :]),
                          reads=[B_ps[ab], B_init], writes=[Bn])

                    def fin():
                        b = next_pb()
                        S_.op("pe", lambda e: e.matmul(ps[b][:, :], e64_f, nrb, start=True, stop=True),
                              reads=[Bn, B_cst], writes=[B_ps[b]])
                        S_.op("dve", lambda e: e.tensor_tensor(out=attnT[p0:p0 + 64, a, :], in0=nrb[0:64, :],
                                                               in1=ps[b][0:64, :], op=ALU.mult),
                              reads=[Bn, B_ps[b]], writes=[B_attn[h]])
                    deferred.append((it + 7, fin))

            nblk = len(blocks)
            start_steps = 26 if nblk >= 64 else max(2, nblk // 8)
            for n in range(nblk + LOOK):
                if n < nblk:
                    qk(n)
                flush_deferred(n)
                if n - LOOK >= 0:
                    pv(n - LOOK, n)
                if n >= start_steps and steps:
                    steps.pop(0)()
                for fn in psched.pop(n - 3, []):
                    fn()
            for st in steps:
                st()
            for k in sorted(psched):
                for fn in psched[k]:
                    fn()

        def emit_wout(tt):
            i = tt % 2
            for m in range(KC):
                b = next_pb()

                def mm(e, m=m, b=b):
                    ins = None
                    for ec in range(8):
                        rhs = attnT[:, ec, :] if ec < 4 else mixT[:, ec - 4, :]
                        ins = e.matmul(ps[b][:, :], wout_bf[:, ec, m * 128:(m + 1) * 128], rhs, start=(ec == 0),
                                       stop=(ec == 7))
                    return ins

                S_.op("pe", mm, reads=B_attn + B_mix + [B_w1], writes=[B_ps[b]])
                S_.op("dve", lambda e, m=m, b=b: e.tensor_tensor(out=xt[i][:, m, :], in0=ps[b][:, :], in1=xt[i][:, m, :],
                                                                 op=ALU.add),
                      reads=[B_ps[b], B_xt[i][m]], writes=[B_xt[i][m]])
            if tt + 1 < NT or debug:
                S_.dma("sp", f"xs{i}", lambda e: e.dma_start(out=x1_v[:, :, tt * TT:(tt + 1) * TT], in_=xt[i][:]),
                       reads=B_xt[i])

        for st in norm_steps(xt[0], B_xt[0], hb, B_hb, g1, sq_eng="act"):
            st()
        emit_vf(0)
        emit_fchain_a(0)
        emit_qku(0)
        emit_fchain_b(0)
        for tt in range(NT):
            steps = []
            if tt + 1 < NT:
                emit_load(tt + 1)
                steps = norm_steps(xt[(tt + 1) % 2], B_xt[(tt + 1) % 2], hb, B_hb, g1)
            emit_attention(tt, steps, pool_sched(tt))
            emit_poolmix(tt)
            if tt + 1 < NT:
                emit_vf(tt + 1)
                emit_fchain_a(tt + 1)
            flush_deferred()
            emit_wout(tt)
            if tt + 1 < NT:
                emit_qku(tt + 1)
                emit_fchain_b(tt + 1)
            if tt + 1 == NT - 1 or NT == 1:
                emit_w2_block(0, extra_writes=B_win)

        S_.barrier()

        B_yt = B_xt
        ORDER = [NT - 1] + list(range(NT - 1))
        BUFI = lambda k: (NT - 1 + k) % 2
        B_hb2 = [Buf(f"hb2_{c}") for c in range(KC)]
        B_act = [Buf(f"act{c}") for c in range(FC)]
        B_sil = [Buf("sil0"), Buf("sil1")]
        B_xsq2 = [Buf(f"xsq2_{i}") for i in range(NXSQ2)]
        B_lnt2, B_rstd2 = Buf("lnt2"), Buf("rstd2")
        B_ps2 = [Buf(f"ps2_{i}") for i in range(8)]
        GB, UB, DB, NB = [0, 1], [2, 3], [4, 5], 6

        def emit_load2(k):
            i, tt = BUFI(k), ORDER[k]
            S_.dma("sp", f"yl{i}", lambda e: e.dma_start(out=yt[i][:], in_=x1_v[:, :, tt * TT:(tt + 1) * TT]),
                   writes=B_yt[i])

        for bi in (1, 2, 3):
            emit_w2_block(bi)
        for bi, (f0, f1) in enumerate(FBLK):
            S_.dma("pool", f"d{bi}", lambda e, f0=f0, f1=f1: e.dma_start(out=wd_bf[:, f0:f1, :], in_=wd_v[:, f0:f1, :]),
                   writes=[B_wd[bi]])

        def norm2_steps(xtile, B_x, out_fn, bank, sq_eng="pool"):
            steps = []

            def sq(c):
                k = c % NXSQ2
                if sq_eng == "act":
                    S_.op("act", lambda e: e.activation(out=xsq2[k], in_=xtile[:, c, :], func=AF.Square),
                          reads=[B_x[c]], writes=[B_xsq2[k]])
                else:
                    S_.op("pool", lambda e: e.tensor_tensor(out=xsq2[k], in0=xtile[:, c, :], in1=xtile[:, c, :],
                                                            op=ALU.mult),
                          reads=[B_x[c]], writes=[B_xsq2[k]])

            def mm(c):
                k = c % NXSQ2
                S_.op("pe", lambda e: e.matmul(ps[bank][:, :], ones_bf, xsq2[k], start=(c == 0), stop=(c == KC - 1)),
                      reads=[B_xsq2[k], B_cst], writes=[B_ps2[bank]])

            LAG = 1
            for c in range(KC + LAG):
                def st(c=c):
                    if c < KC:
                        sq(c)
                    if c - LAG >= 0:
                        mm(c - LAG)
                steps.append(st)

            def rs():
                S_.op("act", lambda e: e.activation(out=lnt2, in_=ps[bank][:, :], func=AF.Ln, bias=eps_t[:, 0:1],
                                                    scale=1.0 / D),
                      reads=[B_ps2[bank], B_cst], writes=[B_lnt2] + B_xsq2)
                S_.op("act", lambda e: e.activation(out=rstd2, in_=lnt2, func=AF.Exp, scale=-0.5),
                      reads=[B_lnt2] + B_xsq2, writes=[B_rstd2])
            steps.append(rs)
            for c in range(KC):
                steps.append(lambda c=c: out_fn(c))
            return steps

        def hb_steps(k, sq_eng="pool"):
            i = BUFI(k)
            xtile = yt[i]

            def mk_hb(c):
                S_.op("dve", lambda e: e.scalar_tensor_tensor(out=hb2[:, c, :], in0=xtile[:, c, :],
                                                              scalar=g2[:, c:c + 1], in1=rstd2,
                                                              op0=ALU.mult, op1=ALU.mult),
                      reads=[B_yt[i][c], B_rstd2, B_cst], writes=[B_hb2[c]])
            return norm2_steps(xtile, B_yt[i], mk_hb, NB, sq_eng)

        def final_steps(k, sq_eng="pool"):
            i, tt = BUFI(k), ORDER[k]
            xtile = yt[i]

            def mk_y(c):
                S_.op("dve", lambda e: e.scalar_tensor_tensor(out=xtile[:, c, :], in0=xtile[:, c, :],
                                                              scalar=gf[:, c:c + 1], in1=rstd2,
                                                              op0=ALU.mult, op1=ALU.mult),
                      reads=[B_yt[i][c], B_rstd2, B_cst], writes=[B_yt[i][c]])
                S_.dma("sp", f"ys{i}", lambda e: e.dma_start(out=yT_v[:, c, tt * TT:(tt + 1) * TT], in_=xtile[:, c, :]),
                       reads=[B_yt[i][c]])
            return norm2_steps(xtile, B_yt[i], mk_y, 7, sq_eng)

        def fblk_of(fc):
            for bi, (f0, f1) in enumerate(FBLK):
                if f0 <= fc < f1:
                    return bi, fc - f0
            raise AssertionError

        def emit_gateup(tt, pending):
            for fc in range(FC):
                gb, ub = GB[fc % 2], UB[fc % 2]
                bi, fo = fblk_of(fc)

                def mmg(e, bi=bi, fo=fo, gb=gb):
                    ins = None
                    for c in range(KC):
                        ins = e.matmul(ps[gb][:, :], wg_blk[bi][:, c, fo * 128:(fo + 1) * 128], hb2[:, c, :],
                                       start=(c == 0), stop=(c == KC - 1))
                    return ins

                def mmu(e, bi=bi, fo=fo, ub=ub):
                    ins = None
                    for c in range(KC):
                        ins = e.matmul(ps[ub][:, :], wu_blk[bi][:, c, fo * 128:(fo + 1) * 128], hb2[:, c, :],
                                       start=(c == 0), stop=(c == KC - 1))
                    return ins

                S_.op("pe", mmg, reads=B_hb2 + [B_wg[bi]], writes=[B_ps2[gb]])
                S_.op("pe", mmu, reads=B_hb2 + [B_wu[bi]], writes=[B_ps2[ub]])
                sl = fc % 2
                S_.op("act", lambda e, gb=gb, sl=sl: e.activation(out=sil[sl], in_=ps[gb][:, :], func=AF.Silu),
                      reads=[B_ps2[gb]], writes=[B_sil[sl]])
                S_.op("dve", lambda e, fc=fc, ub=ub, sl=sl: e.tensor_tensor(out=actT[:, fc, :], in0=ps[ub][:, :],
                                                                            in1=sil[sl], op=ALU.mult),
                      reads=[B_ps2[ub], B_sil[sl]], writes=[B_act[fc]])
                if pending and fc >= 1:
                    pending.pop(0)()
            while pending:
                pending.pop(0)()

        def emit_down(k, pending, pops=3):
            i = BUFI(k)
            xtile = yt[i]
            for m in range(KC):
                db = DB[m % 2]

                def mmd(e, m=m, db=db):
                    ins = None
                    for fc in range(FC):
                        ins = e.matmul(ps[db][:, :], wd_bf[:, fc, m * 128:(m + 1) * 128], actT[:, fc, :],
                                       start=(fc == 0), stop=(fc == FC - 1))
                    return ins

                S_.op("pe", mmd, reads=B_act + B_wd, writes=[B_ps2[db]])
                S_.op("dve", lambda e, m=m, db=db: e.tensor_tensor(out=xtile[:, m, :], in0=ps[db][:, :],
                                                                   in1=xtile[:, m, :], op=ALU.add),
                      reads=[B_ps2[db], B_yt[i][m]], writes=[B_yt[i][m]])
                for _ in range(pops):
                    if pending:
                        pending.pop(0)()
            while pending:
                pending.pop(0)()

        for st in hb_steps(0, sq_eng="act"):
            st()
        pend_final = []
        for k in range(NT):
            emit_gateup(k, pend_final)
            if k + 1 < NT:
                emit_load2(k + 1)
            if k + 1 < NT:
                emit_down(k, hb_steps(k + 1))
                pend_final = final_steps(k)
            else:
                emit_down(k, final_steps(k, sq_eng="act"), pops=1)
                pend_final = []
        for st in pend_final:
            st()

        S_.barrier()

        print(f"[kernel] arena words: phase1 {p1_end} phase2 {p2_end} of {AW}; "
              f"sem counts: { {k: v for k, v in S_.cnt.items()} }")

        @block.tensor
        def _(e):
            S_.replay("pe", e, semh)

        @block.scalar
        def _(e):
            S_.replay("act", e, semh)

        @block.vector
        def _(e):
            S_.replay("dve", e, semh)

        @block.gpsimd
        def _(e):
            S_.replay("pool", e, semh)

        @block.sync
        def _(e):
            S_.replay("sp", e, semh)

    return nc


def _consts(norm1_g, norm2_g, final_g, pool_scale, b_forget):
    cf = np.zeros((128, NCF), np.float32)
    cf[:, CF_G1:CF_G1 + 8] = np.asarray(norm1_g, np.float32).reshape(8, 128).T
    cf[:, CF_G2:CF_G2 + 8] = np.asarray(norm2_g, np.float32).reshape(8, 128).T
    cf[:, CF_GF:CF_GF + 8] = np.asarray(final_g, np.float32).reshape(8, 128).T
    cf[:, CF_PS:CF_PS + 4] = np.asarray(pool_scale, np.float32).reshape(4, 128).T
    cf[:, CF_BF:CF_BF + 32] = np.tile(np.asarray(b_forget, np.float32).reshape(1, 8), (128, 4))
    t = np.arange(16)
    ic = np.stack([1.0 / np.minimum(t + 1, 2 << g) for g in range(4)]).astype(np.float32)
    cf[:, CF_IC:CF_IC + 64] = ic.reshape(1, 64)
    kk = np.arange(128)
    cf[:, CF_TRI:CF_TRI + 128] = (kk[:, None] <= kk[None, :]).astype(np.float32)
    cf[64, CF_E64:CF_E64 + 128] = 1.0
    cb = np.zeros((128, NCB), np.float32)
    cb[:, CB_ID:CB_ID + 128] = np.eye(128, dtype=np.float32)
    cb[:, CB_MASK:CB_MASK + 128] = np.where(kk[:, None] <= kk[None, :], 0.0, NEG).astype(np.float32)
    for h in range(8):
        cb[h, CB_SEL + h * 128:CB_SEL + (h + 1) * 128] = -1.0
    return cf, cb


_NC_CACHE = {}


def kernel(x, norm1_g, w_in, b_forget, w_pool, pool_scale, w_out, norm2_g, w_gate, w_up, w_down, final_g):
    x = np.asarray(x, np.float32)
    B = x.shape[0]
    cf, cb = _consts(np.asarray(norm1_g)[0], np.asarray(norm2_g)[0], np.asarray(final_g), np.asarray(pool_scale)[0],
                     np.asarray(b_forget)[0])
    shared = {
        "w_in": np.ascontiguousarray(np.asarray(w_in, np.float32)[0]),
        "w_out": np.ascontiguousarray(np.asarray(w_out, np.float32)[0]),
        "w_pool": np.ascontiguousarray(np.asarray(w_pool, np.float32)[0]),
        "w_gate": np.ascontiguousarray(np.asarray(w_gate, np.float32)[0]),
        "w_up": np.ascontiguousarray(np.asarray(w_up, np.float32)[0]),
        "w_down": np.ascontiguousarray(np.asarray(w_down, np.float32)[0]),
        "cf": cf,
        "cb": cb,
    }
    in_maps = [dict(shared, xT=np.ascontiguousarray(x[b].T)) for b in range(B)]
    if "nc" not in _NC_CACHE:
        _NC_CACHE["nc"] = build_program()
    nc = _NC_CACHE["nc"]
    res = run_bass_kernel_spmd(nc, in_maps, core_ids=list(range(B)))
    out = np.stack([np.asarray(res.results[b]["yT"]).T for b in range(B)])
    return np.ascontiguousarray(out.astype(np.float32))
```

```python
from contextlib import ExitStack

import numpy as np
import concourse.bass as bass
import concourse.mybir as mybir
from concourse.bass_utils import run_bass_kernel_spmd

F32 = mybir.dt.float32
BF16 = mybir.dt.bfloat16
AF = mybir.ActivationFunctionType
ALU = mybir.AluOpType

S = 4096
D = 1024
TT = 512
NT = S // TT
KC = D // 128
H = 8
DFF = 2816
FC = DFF // 128
INW = 2056
EPS = 1e-6
NEG = -30000.0

CF_G1, CF_G2, CF_GF, CF_PS, CF_BF, CF_IC, CF_TRI = 0, 8, 16, 24, 28, 60, 124
CF_E64 = 124 + 128
NCF = 124 + 128 + 128
CB_ID, CB_MASK, CB_SEL = 0, 128, 256
NCB = 256 + 1024


class Buf:
    __slots__ = ("name", "w", "r")

    def __init__(self, name):
        self.name = name
        self.w = None
        self.r = []


class Sched:
    ENGS = ("pe", "act", "dve", "pool", "sp")

    def __init__(self):
        self.prog = {e: [] for e in self.ENGS}
        self.cnt = {}
        self.waited = {e: {} for e in self.ENGS}

    @staticmethod
    def _flat(bufs):
        out = []
        for b in bufs:
            if isinstance(b, (list, tuple)):
                out.extend(Sched._flat(b))
            else:
                out.append(b)
        return out

    def _deps(self, eng, reads, writes):
        reads, writes = self._flat(reads), self._flat(writes)
        need = {}

        def add(t):
            if t is not None and need.get(t[0], 0) < t[1]:
                need[t[0]] = t[1]

        for b in reads:
            add(b.w)
        for b in writes:
            add(b.w)
            for t in b.r:
                add(t)
        out = []
        for k, v in need.items():
            if k == eng and eng == "pe":
                continue
            if self.waited[eng].get(k, 0) >= v:
                continue
            self.waited[eng][k] = v
            out.append((k, v))
        return out

    def _commit(self, tok, reads, writes):
        reads, writes = self._flat(reads), self._flat(writes)
        for b in reads:
            b.r.append(tok)
        for b in writes:
            b.w = tok
            b.r = []

    def op(self, eng, fn, reads=(), writes=()):
        waits = self._deps(eng, reads, writes)
        self.cnt[eng] = self.cnt.get(eng, 0) + 1
        tok = (eng, self.cnt[eng])
        self.prog[eng].append((waits, fn, (eng, 1)))
        self._commit(tok, reads, writes)
        return tok

    def dma(self, queue, semkey, fn, reads=(), writes=(), track=True):
        waits = self._deps(queue, reads, writes) if track else []
        self.cnt[semkey] = self.cnt.get(semkey, 0) + 16
        tok = (semkey, self.cnt[semkey])
        self.prog[queue].append((waits, fn, (semkey, 16)))
        if track:
            self._commit(tok, reads, writes)
        return tok

    def barrier(self):
        for e in self.ENGS:
            waits = []
            for k, v in self.cnt.items():
                if k == e and e == "pe":
                    continue
                if self.waited[e].get(k, 0) >= v:
                    continue
                self.waited[e][k] = v
                waits.append((k, v))
            if waits:
                self.prog[e].append((waits, None, None))

    def replay(self, eng, e, semh):
        for waits, fn, sig in self.prog[eng]:
            for k, v in waits:
                e.wait_ge(semh[k], v)
            if fn is None:
                continue
            ins = fn(e)
            ins.then_inc(semh[sig[0]], sig[1])


class Arena:
    def __init__(self, ap):
        self.ap = ap
        self.off = 0
        self.words = ap.shape[1]

    def alloc(self, n, dtype):
        nbytes = n * (2 if dtype == BF16 else 4)
        words = (nbytes + 31) // 32 * 8
        assert self.off + words <= self.words, ("arena overflow", self.off, words, self.words)
        v = self.ap[:, self.off:self.off + words]
        self.off += words
        if dtype == BF16:
            return v.bitcast(BF16)[:, 0:n]
        return v[:, 0:n]


def build_program(debug=False):
    nc = bass.Bass("TRN2", target_bir_lowering=False)
    dt_in = lambda name, shape: nc.dram_tensor(name, shape, F32, kind="ExternalInput").ap()
    xT = dt_in("xT", [D, S])
    w_in = dt_in("w_in", [D, INW])
    w_out = dt_in("w_out", [D, D])
    w_pool = dt_in("w_pool", [4, 128, 128])
    w_gate = dt_in("w_gate", [D, DFF])
    w_up = dt_in("w_up", [D, DFF])
    w_down = dt_in("w_down", [DFF, D])
    cf_d = dt_in("cf", [128, NCF])
    cb_d = dt_in("cb", [128, NCB])
    x1s = nc.dram_tensor("x1s", [D, S], F32, kind=("ExternalOutput" if debug else "Internal")).ap()
    yT = nc.dram_tensor("yT", [D, S], F32, kind="ExternalOutput").ap()

    xT_v = xT.rearrange("(c p) t -> p c t", p=128)
    x1_v = x1s.rearrange("(c p) t -> p c t", p=128)
    yT_v = yT.rearrange("(c p) t -> p c t", p=128)

    S_ = Sched()
    semnames = ["pe", "act", "dve", "pool", "cst", "cst2", "wi0", "wi1", "wi2", "wi3", "w1", "g0", "g1", "g2", "g3", "u0", "u1", "u2", "u3", "d0", "d1", "d2", "d3", "xl0", "xl1", "xs0", "xs1",
                "yl0", "yl1", "ys0", "ys1"]

    with ExitStack() as ctx:
        AW = 53100
        arena_t = ctx.enter_context(nc.sbuf_tensor("arena", [128, AW], F32))
        ps = [ctx.enter_context(nc.psum_tensor(f"ps{i}", [128, 512], F32)) for i in range(8)]
        semh = {n: ctx.enter_context(nc.semaphore(n)) for n in semnames}
        block = ctx.enter_context(nc.Block())

        ar = Arena(arena_t[:, :])
        cf = ar.alloc(NCF, F32)
        cb = ar.alloc(NCB, BF16)
        ones_bf = ar.alloc(128, BF16)
        ones_f = ar.alloc(128, F32)
        eps_t = ar.alloc(8, F32)
        xt = [ar.alloc(KC * TT, F32).rearrange("p (c t) -> p c t", t=TT) for _ in range(2)]
        base_off = ar.off
        B_cf, B_cb, B_ones = Buf("cf"), Buf("cb"), Buf("ones")
        B_cst = [B_cf, B_cb, B_ones]

        g1 = cf[:, CF_G1:CF_G1 + 8]
        g2 = cf[:, CF_G2:CF_G2 + 8]
        gf = cf[:, CF_GF:CF_GF + 8]
        pscale = cf[:, CF_PS:CF_PS + 4]
        bfor = cf[:, CF_BF:CF_BF + 32]
        invcnt = cf[:, CF_IC:CF_IC + 64].rearrange("p (g t) -> p g t", t=16)
        tri_f = cf[:, CF_TRI:CF_TRI + 128]
        e64_f = cf[:, CF_E64:CF_E64 + 128]
        ident_bf = cb[:, CB_ID:CB_ID + 128]
        maskb = cb[:, CB_MASK:CB_MASK + 128]
        selneg = cb[:, CB_SEL:CB_SEL + 1024].rearrange("p (h m) -> p h m", m=128)

        S_.dma("sp", "cst", lambda e: e.dma_start(out=cf, in_=cf_d), writes=[B_cf])
        S_.dma("pool", "cst2", lambda e: e.dma_start(out=cb, in_=cb_d), writes=[B_cb])
        B_xt = [[Buf(f"xt{i}_{c}") for c in range(KC)] for i in range(2)]
        S_.dma("sp", "xl0", lambda e: e.dma_start(out=xt[0][:], in_=xT_v[:, :, 0:TT]), writes=B_xt[0])
        S_.op("pool", lambda e: e.memset(ones_bf, 1.0), writes=[B_ones])
        S_.op("pool", lambda e: e.memset(ones_f, 1.0), writes=[B_ones])
        S_.op("pool", lambda e: e.memset(eps_t, EPS), writes=[B_ones])

        win_bf = ar.alloc(KC * INW, BF16).rearrange("p (c n) -> p c n", n=INW)
        wout_bf = ar.alloc(KC * D, BF16).rearrange("p (c n) -> p c n", n=D)
        wpool_bf = ar.alloc(4 * 128, BF16).rearrange("p (g n) -> p g n", n=128)
        kT = ar.alloc(4 * S, BF16).rearrange("p (a t) -> p a t", t=S)
        NJ = S // 128
        vst2d = ar.alloc(NJ * H * 65 + 64, BF16)
        vst = vst2d[:, 0:NJ * H * 65].rearrange("p (j h d) -> p j h d", h=H, d=65)
        NXSQ = 3
        xsq = [ar.alloc(TT, BF16) for _ in range(NXSQ)]
        hb2d = ar.alloc(KC * TT, BF16)
        hb = hb2d.rearrange("p (c t) -> p c t", t=TT)
        lnt = hb2d[:, 6 * TT:8 * TT].bitcast(F32)
        rstd = ar.alloc(TT, F32)
        qz = ar.alloc(H * TT, BF16).rearrange("p (h t) -> p h t", t=TT)
        UW = TT + 16
        uT = ar.alloc(4 * UW, F32).rearrange("p (g t) -> p g t", t=UW)
        ptmp = [ar.alloc(UW, F32) for _ in range(2)]
        pooledT = ar.alloc(4 * TT, BF16).rearrange("p (g t) -> p g t", t=TT)
        mixT = ar.alloc(4 * TT, BF16).rearrange("p (g t) -> p g t", t=TT)
        attnT = ar.alloc(4 * TT, BF16).rearrange("p (a t) -> p a t", t=TT)
        NPT = 3
        pT = [ar.alloc(TT, BF16) for _ in range(NPT)]
        nr = [ar.alloc(TT, F32) for _ in range(2)]
        CT_bf = ar.alloc(TT, BF16)
        fz = ar.alloc(32, F32)
        lst = ar.alloc(NJ * H, F32).rearrange("p (n h) -> p n h", h=H)
        accs = ar.alloc((NJ + 1) * H, F32).rearrange("p (n h) -> p n h", h=H)
        Cst = ar.alloc(NJ * H, F32).rearrange("p (n h) -> p n h", h=H)
        tmp16 = ar.alloc(16, F32)
        p1_end = ar.off

        ar.off = base_off
        FBLK = [(0, 6), (6, 12), (12, 17), (17, 22)]
        wg_blk, wu_blk = [None] * 4, [None] * 4
        for bi in (0,):
            f0, f1 = FBLK[bi]
            wg_blk[bi] = ar.alloc(KC * (f1 - f0) * 128, BF16).rearrange("p (c n) -> p c n", c=KC)
            wu_blk[bi] = ar.alloc(KC * (f1 - f0) * 128, BF16).rearrange("p (c n) -> p c n", c=KC)
        assert ar.off <= base_off + (KC * INW * 2 + 3) // 4, "prefetch blocks must fit in the w_in region"
        for bi in (1, 2, 3):
            f0, f1 = FBLK[bi]
            wg_blk[bi] = ar.alloc(KC * (f1 - f0) * 128, BF16).rearrange("p (c n) -> p c n", c=KC)
            wu_blk[bi] = ar.alloc(KC * (f1 - f0) * 128, BF16).rearrange("p (c n) -> p c n", c=KC)
        wd_bf = ar.alloc(FC * D, BF16).rearrange("p (c n) -> p c n", n=D)
        yt = xt
        hb2 = ar.alloc(KC * TT, BF16).rearrange("p (c t) -> p c t", t=TT)
        lnt2 = ar.alloc(TT, F32)
        xsq2 = [lnt2.bitcast(BF16)[:, 0:TT], lnt2.bitcast(BF16)[:, TT:2 * TT]]
        NXSQ2 = 2
        rstd2 = ar.alloc(TT, F32)
        actT = ar.alloc(FC * TT, BF16).rearrange("p (c t) -> p c t", t=TT)
        sil = [ar.alloc(TT, F32) for _ in range(2)]
        p2_end = ar.off
        wg_v = w_gate.rearrange("(c p) n -> p c n", p=128)
        wu_v = w_up.rearrange("(c p) n -> p c n", p=128)
        wd_v = w_down.rearrange("(c p) n -> p c n", p=128)
        B_wg = [Buf(f"wg{i}") for i in range(4)]
        B_wu = [Buf(f"wu{i}") for i in range(4)]
        B_wd = [Buf(f"wd{i}") for i in range(4)]

        def emit_w2_block(bi, extra_writes=()):
            f0, f1 = FBLK[bi]
            S_.dma("pool", f"g{bi}", lambda e: e.dma_start(out=wg_blk[bi][:, :, :], in_=wg_v[:, :, f0 * 128:f1 * 128]),
                   writes=[B_wg[bi]] + list(extra_writes))
            S_.dma("pool", f"u{bi}", lambda e: e.dma_start(out=wu_blk[bi][:, :, :], in_=wu_v[:, :, f0 * 128:f1 * 128]),
                   writes=[B_wu[bi]] + list(extra_writes))

        B_w1 = Buf("w1")
        win_v = w_in.rearrange("(c p) n -> p c n", p=128)
        B_win = [Buf(f"win{i}") for i in range(4)]
        for i, (c0, c1) in enumerate([(1024, 1544), (0, 512), (512, 1024), (1544, 2056)]):
            S_.dma("pool", f"wi{i}", lambda e, c0=c0, c1=c1: e.dma_start(out=win_bf[:, :, c0:c1], in_=win_v[:, :, c0:c1]),
                   writes=[B_win[i]])
        for c in range(KC):
            S_.dma("pool", "w1", lambda e, c=c: e.dma_start(out=wout_bf[:, c, :], in_=w_out[c * 128:(c + 1) * 128, :]),
                   track=False)
        for g in range(4):
            S_.dma("pool", "w1", lambda e, g=g: e.dma_start(out=wpool_bf[:, g, :], in_=w_pool[g]), track=False)
        B_w1.w = ("w1", S_.cnt["w1"])
        B_xsq = [Buf(f"xsq{i}") for i in range(NXSQ)]
        B_hb = [Buf(f"hb{c}") for c in range(KC)]
        B_rstd = Buf("rstd")
        B_ps = [Buf(f"ps{i}") for i in range(8)]
        B_qz = [Buf(f"qz{h}") for h in range(H)]
        B_kT = [[Buf(f"kT{a}_{t}") for t in range(NT)] for a in range(4)]
        B_v = [Buf(f"v{j}") for j in range(NJ)]
        B_ptmp = [Buf("ptmp0"), Buf("ptmp1")]
        B_pooled = [Buf(f"pooled{g}") for g in range(4)]
        B_mix = [Buf(f"mix{g}") for g in range(4)]
        B_attn = [Buf(f"attn{h}") for h in range(H)]
        B_pT = [Buf(f"pT{i}") for i in range(NPT)]
        B_nr, B_CT = [Buf("nr0"), Buf("nr1")], Buf("CTbf")
        B_fz = Buf("fz")
        B_lst = [Buf(f"lst{n}") for n in range(NJ)]
        B_Cst = [Buf(f"Cst{t}") for t in range(NT)]
        B_tmp16 = Buf("tmp16")

        ST_BANKS = [0, 1, 2]
        ACC_BANKS = [3, 4]
        PB_BANKS = [5, 6]
        MISC = 7
        pb_ctr = [0]

        def next_pb():
            b = PB_BANKS[pb_ctr[0] % 2]
            pb_ctr[0] += 1
            return b

        xsq_ctr = [0]

        B_vones = Buf("vones")
        B_uT = [Buf(f"uT{g}") for g in range(4)]
        B_uhalo = Buf("uhalo")
        B_acc = [Buf(f"acc{n}") for n in range(NJ + 1)]
        S_.op("pool", lambda e: e.memset(vst2d, 1.0), writes=[B_vones])
        S_.op("pool", lambda e: e.memset(uT[:, :, 0:16], 0.0), writes=[B_uhalo])
        S_.op("pool", lambda e: e.memset(accs[:, 0, :], 0.0), writes=[B_acc[0]])
        B_init = Buf("init")
        S_.op("pool", lambda e: e.memset(qz[:, :, :], 0.0), writes=[B_init])
        S_.op("pool", lambda e: e.memset(CT_bf, 0.0), writes=[B_init])
        S_.op("pool", lambda e: e.memset(nr[0], 0.0), writes=[B_init])
        S_.op("pool", lambda e: e.memset(nr[1], 0.0), writes=[B_init])

        def emit_load(tt):
            i = tt % 2
            S_.dma("sp", f"xl{i}", lambda e: e.dma_start(out=xt[i][:], in_=xT_v[:, :, tt * TT:(tt + 1) * TT]),
                   writes=B_xt[i])

        def norm_steps(xtile, B_x, hbt, B_h, gcol, misc_bank=MISC, sq_eng="pool"):
            steps = []
            ks = []
            for c in range(KC):
                k = xsq_ctr[0] % NXSQ
                xsq_ctr[0] += 1
                ks.append(k)

            def sq(c):
                k = ks[c]
                if sq_eng == "act":
                    S_.op("act", lambda e: e.activation(out=xsq[k], in_=xtile[:, c, :], func=AF.Square),
                          reads=[B_x[c]], writes=[B_xsq[k]])
                else:
                    S_.op("pool", lambda e: e.tensor_tensor(out=xsq[k], in0=xtile[:, c, :], in1=xtile[:, c, :],
                                                            op=ALU.mult),
                          reads=[B_x[c]], writes=[B_xsq[k]])

            def mm(c):
                k = ks[c]
                S_.op("pe", lambda e: e.matmul(ps[misc_bank][:, :], ones_bf, xsq[k], start=(c == 0),
                                               stop=(c == KC - 1)),
                      reads=[B_xsq[k], B_cst], writes=[B_ps[misc_bank]])

            LAG = 2
            for c in range(KC + LAG):
                def st(c=c):
                    if c < KC:
                        sq(c)
                    if c - LAG >= 0:
                        mm(c - LAG)
                steps.append(st)

            def rs():
                S_.op("act", lambda e: e.activation(out=lnt, in_=ps[misc_bank][:, :], func=AF.Ln, bias=eps_t[:, 0:1],
                                                    scale=1.0 / D),
                      reads=[B_ps[misc_bank], B_cst], writes=[B_h[6], B_h[7]])
                S_.op("act", lambda e: e.activation(out=rstd, in_=lnt, func=AF.Exp, scale=-0.5),
                      reads=[B_h[6], B_h[7]], writes=[B_rstd])
            steps.append(rs)
            for c in range(KC):
                def hbs(c=c):
                    S_.op("dve", lambda e: e.scalar_tensor_tensor(out=hbt[:, c, :], in0=xtile[:, c, :],
                                                                  scalar=gcol[:, c:c + 1], in1=rstd,
                                                                  op0=ALU.mult, op1=ALU.mult),
                          reads=[B_x[c], B_rstd, B_cst], writes=[B_h[c]])
                steps.append(hbs)
            return steps

        evac_ctr = [0]

        def evac_copy(out_ap, in_ap, reads, writes, scale=None):
            k = evac_ctr[0]
            evac_ctr[0] += 1
            if k % 2 == 0:
                if scale is None:
                    S_.op("act", lambda e: e.activation(out=out_ap, in_=in_ap, func=AF.Copy), reads=reads, writes=writes)
                else:
                    S_.op("act", lambda e: e.activation(out=out_ap, in_=in_ap, func=AF.Copy, scale=scale),
                          reads=reads, writes=writes)
            else:
                if scale is None:
                    S_.op("dve", lambda e: e.tensor_copy(out=out_ap, in_=in_ap), reads=reads, writes=writes)
                else:
                    S_.op("dve", lambda e: e.tensor_scalar(out=out_ap, in0=in_ap, scalar1=scale, scalar2=None,
                                                           op0=ALU.mult), reads=reads, writes=writes)

        def emit_vf(tt):
            for blk in range(4):
                b = next_pb()
                j = tt * 4 + blk

                def mmv(e, blk=blk, b=b):
                    ins = None
                    for c in range(KC):
                        lhsT = hb[:, c, blk * 128:(blk + 1) * 128]
                        e.matmul(ps[b][:, :], lhsT, win_bf[:, c, 1024:1536], start=(c == 0), stop=(c == KC - 1))
                        ins = e.matmul(ps[MISC][:, blk * 8:(blk + 1) * 8], lhsT, win_bf[:, c, 1536:1544],
                                       start=(c == 0), stop=(c == KC - 1))
                    return ins

                S_.op("pe", mmv, reads=B_hb + [B_win[0]], writes=[B_ps[b], B_ps[MISC]])
                evac_copy(vst[:, j, :, 0:64], ps[b][:, :].rearrange("p (h d) -> p h d", d=64), [B_ps[b], B_vones],
                          [B_v[j]])

        def emit_qku(tt):
            specs = []
            for a in range(4):
                specs.append(("q", a, a * 128))
            for a in range(4):
                specs.append(("k", a, 512 + a * 128))
            for g in range(4):
                specs.append(("u", g, 1544 + g * 128))
            for kind, idx, col in specs:
                b = next_pb()

                def mm(e, col=col, b=b):
                    ins = None
                    for c in range(KC):
                        ins = e.matmul(ps[b][:, :], win_bf[:, c, col:col + 128], hb[:, c, :], start=(c == 0),
                                       stop=(c == KC - 1))
                    return ins

                wb = {"q": B_win[1], "k": B_win[2], "u": B_win[3]}[kind]
                S_.op("pe", mm, reads=B_hb + [wb], writes=[B_ps[b]])
                if kind == "q":
                    evac_copy(qz[0:64, 2 * idx, :], ps[b][0:64, :], [B_ps[b], B_init], [B_qz[2 * idx]], scale=0.125)
                    evac_copy(qz[64:128, 2 * idx + 1, :], ps[b][64:128, :], [B_ps[b], B_init], [B_qz[2 * idx + 1]],
                              scale=0.125)
                elif kind == "k":
                    evac_copy(kT[:, idx, tt * TT:(tt + 1) * TT], ps[b][:, :], [B_ps[b]], [B_kT[idx][tt]])
                else:
                    evac_copy(uT[:, idx, 16:UW], ps[b][:, :], [B_ps[b]], [B_uT[idx]])

        def emit_fchain_a(tt):
            n0 = tt * 4
            S_.op("dve", lambda e: e.tensor_tensor(out=fz, in0=ps[MISC][:, 0:32], in1=bfor, op=ALU.add),
                  reads=[B_ps[MISC], B_cst], writes=[B_fz])
            S_.op("act", lambda e: e.activation(out=fz, in_=fz, func=AF.Exp, scale=-1.0), reads=[B_fz], writes=[B_fz])
            lview = lst[:, n0:n0 + 4, :].rearrange("p n h -> p (n h)")
            S_.op("act", lambda e: e.activation(out=lview, in_=fz, func=AF.Ln, bias=1.0, scale=1.0),
                  reads=[B_fz], writes=[B_lst[n0 + i] for i in range(4)])
            for i in range(4):
                n = n0 + i
                S_.op("dve", lambda e, n=n: e.tensor_tensor(out=accs[:, n + 1, :], in0=accs[:, n, :], in1=lst[:, n, :],
                                                            op=ALU.add),
                      reads=[B_acc[n], B_lst[n]], writes=[B_acc[n + 1]])

        def emit_fchain_b(tt):
            n0 = tt * 4
            bct = next_pb()

            def mmc(e):
                ins = None
                for i in range(4):
                    n = n0 + i
                    e.matmul(ps[MISC][:, 32 + i * 8:40 + i * 8], tri_f, lst[:, n, :], start=True, stop=False)
                    e.matmul(ps[MISC][:, 32 + i * 8:40 + i * 8], ones_f, accs[:, n, :], start=False, stop=True)
                for i in range(4):
                    n = n0 + i
                    e.matmul(ps[bct][0:8, i * 128:(i + 1) * 128], lst[:, n, :], tri_f, start=True, stop=False)
                    ins = e.matmul(ps[bct][0:8, i * 128:(i + 1) * 128], accs[:, n, :], ones_f, start=False, stop=True)
                return ins

            S_.op("pe", mmc, reads=[B_lst[n0 + i] for i in range(4)] + [B_acc[n0 + i] for i in range(4)] + [B_cst],
                  writes=[B_ps[MISC], B_ps[bct]])
            S_.op("dve", lambda e: e.tensor_copy(out=Cst[:, n0:n0 + 4, :].rearrange("p n h -> p (n h)"),
                                                 in_=ps[MISC][:, 32:64]),
                  reads=[B_ps[MISC]], writes=[B_Cst[tt]])
            S_.op("dve", lambda e: e.tensor_copy(out=CT_bf[0:8, :], in_=ps[bct][0:8, :]),
                  reads=[B_ps[bct], B_init], writes=[B_CT])

        def pool_sched(tt):
            sched = {}
            offs = [0, 1, 6, 12, 19]

            def chain(g):
                w = 2 << g
                src = uT[:, g, :]
                srcB = [B_uT[g], B_uhalo]
                shift, lo, step = 1, 0, 0
                while shift < w:
                    dst = ptmp[step % 2]
                    nlo = lo + shift
                    S_.op("pool", lambda e, dst=dst, src=src, nlo=nlo, shift=shift:
                          e.tensor_tensor(out=dst[:, nlo:UW], in0=src[:, nlo:UW], in1=src[:, nlo - shift:UW - shift],
                                          op=ALU.add),
                          reads=srcB, writes=[B_ptmp[step % 2]])
                    src = dst
                    srcB = [B_ptmp[step % 2]]
                    lo = nlo
                    shift *= 2
                    step += 1
                return src, srcB

            res = {}

            def do_chain(g):
                res[g] = chain(g)

            def do_final(g):
                w = 2 << g
                src, srcB = res[g]
                S_.op("dve", lambda e: e.scalar_tensor_tensor(
                    out=pooledT[:, g, :], in0=src[:, 16:UW], scalar=1.0 / w, in1=uT[:, g, 16:UW],
                    op0=ALU.mult, op1=ALU.subtract),
                    reads=srcB + [B_uT[g]], writes=[B_pooled[g]])
                if tt == 0:
                    S_.op("dve", lambda e: e.tensor_tensor(out=tmp16, in0=src[:, 16:32], in1=invcnt[:, g, :],
                                                           op=ALU.mult),
                          reads=srcB + [B_cst], writes=[B_tmp16])
                    S_.op("dve", lambda e: e.tensor_tensor(out=pooledT[:, g, 0:16], in0=tmp16, in1=uT[:, g, 16:32],
                                                           op=ALU.subtract),
                          reads=[B_tmp16, B_uT[g]], writes=[B_pooled[g]])

            def halo():
                S_.op("pool", lambda e: e.tensor_copy(out=uT[:, :, 0:16], in_=uT[:, :, TT:UW]),
                      reads=B_uT, writes=[B_uhalo])

            sched.setdefault(offs[0], []).append(lambda: do_chain(0))
            for g in range(4):
                sched.setdefault(offs[g + 1], []).append(lambda g=g: do_final(g))
                if g + 1 < 4:
                    sched.setdefault(offs[g + 1], []).append(lambda g=g: do_chain(g + 1))
            sched.setdefault(offs[4], []).append(halo)
            return sched

        def emit_poolmix(tt):
            for g in range(4):
                b = next_pb()
                S_.op("pe", lambda e, g=g, b=b: e.matmul(ps[b][:, :], wpool_bf[:, g, :], pooledT[:, g, :], start=True,
                                                         stop=True),
                      reads=[B_pooled[g], B_w1], writes=[B_ps[b]])
                S_.op("act", lambda e, g=g, b=b: e.activation(out=mixT[:, g, :], in_=ps[b][:, :], func=AF.Copy,
                                                              scale=pscale[:, g:g + 1]),
                      reads=[B_ps[b], B_cst], writes=[B_mix[g]])

        deferred = []

        def flush_deferred(now=None):
            keep = []
            for due, fn in deferred:
                if now is None or due <= now:
                    fn()
                else:
                    keep.append((due, fn))
            deferred[:] = keep

        def emit_attention(tt, steps, psched):
            nb = 4 * tt + 4
            blocks = [(h, j) for h in range(H) for j in range(nb)]
            LOOK = 2
            state = {}
            steps = list(steps)

            def qk(n):
                h, j = blocks[n]
                a = h // 2
                diag = j >= 4 * tt
                c0 = 128 * (j - 4 * tt) if diag else 0
                sb = ST_BANKS[n % 3]
                pi = n % NPT

                def mm(e):
                    e.matmul(ps[sb][:, c0:TT], kT[:, a, j * 128:(j + 1) * 128], qz[:, h, c0:TT],
                             start=True, stop=False)
                    ins = e.matmul(ps[sb][:, c0:TT], selneg[:, h, :], CT_bf[:, c0:TT], start=False,
                                   stop=(not diag))
                    if diag:
                        ins = e.matmul(ps[sb][:, c0:c0 + 128], ident_bf, maskb, start=False, stop=True)
                    return ins

                S_.op("pe", mm, reads=[B_kT[a][j // 4], B_qz[h], B_CT, B_cst], writes=[B_ps[sb]])
                S_.op("act", lambda e: e.activation(out=pT[pi][:, c0:TT], in_=ps[sb][:, c0:TT], func=AF.Exp,
                                                    bias=Cst[:, j, h:h + 1], scale=1.0),
                      reads=[B_ps[sb], B_Cst[j // 4]], writes=[B_pT[pi]])
                state[n] = (c0, pi)

            def pv(n, it):
                h, j = blocks[n]
                a, p0 = h // 2, 64 * (h % 2)
                c0, pi = state.pop(n)
                ab = ACC_BANKS[h % 2]
                vo = (j * H + h) * 65
                S_.op("pe", lambda e: e.matmul(ps[ab][:, c0:TT], vst2d[:, vo:vo + 128], pT[pi][:, c0:TT], start=(j == 0),
                                               stop=(j == nb - 1)),
                      reads=[B_pT[pi], B_v[j], B_vones], writes=[B_ps[ab]])
                if j == nb - 1:
                    nrb = nr[h % 2]
                    Bn = B_nr[h % 2]
                    S_.op("dve", lambda e: e.tensor_copy(out=nrb[0:64, :], in_=ps[ab][0:64, :]),
                          reads=[B_ps[ab], B_init], writes=[Bn])
                    S_.op("dve", lambda e: e.reciprocal(out=nrb[64:65, :], in_=ps[ab][64:65, :]),
                          reads=[B_ps[ab], B_init], writes=[Bn])

                    def fin():
                        b = next_pb()
                        S_.op("pe", lambda e: e.matmul(ps[b][:, :], e64_f, nrb, start=True, stop=True),
                              reads=[Bn, B_cst], writes=[B_ps[b]])
                        S_.op("dve", lambda e: e.tensor_tensor(out=attnT[p0:p0 + 64, a, :], in0=nrb[0:64, :],
                                                               in1=ps[b][0:64, :], op=ALU.mult),
                              reads=[Bn, B_ps[b]], writes=[B_attn[h]])
                    deferred.append((it + 7, fin))

            nblk = len(blocks)
            start_steps = 26 if nblk >= 64 else max(2, nblk // 8)
            for n in range(nblk + LOOK):
                if n < nblk:
                    qk(n)
                flush_deferred(n)
                if n - LOOK >= 0:
                    pv(n - LOOK, n)
                if n >= start_steps and steps:
                    steps.pop(0)()
                for fn in psched.pop(n - 3, []):
                    fn()
            for st in steps:
                st()
            for k in sorted(psched):
                for fn in psched[k]:
                    fn()

        def emit_wout(tt):
            i = tt % 2
            for m in range(KC):
                b = next_pb()

                def mm(e, m=m, b=b):
                    ins = None
                    for ec in range(8):
                        rhs = attnT[:, ec, :] if ec < 4 else mixT[:, ec - 4, :]
                        ins = e.matmul(ps[b][:, :], wout_bf[:, ec, m * 128:(m + 1) * 128], rhs, start=(ec == 0),
                                       stop=(ec == 7))
                    return ins

                S_.op("pe", mm, reads=B_attn + B_mix + [B_w1], writes=[B_ps[b]])
                S_.op("dve", lambda e, m=m, b=b: e.tensor_tensor(out=xt[i][:, m, :], in0=ps[b][:, :], in1=xt[i][:, m, :],
                                                                 op=ALU.add),
                      reads=[B_ps[b], B_xt[i][m]], writes=[B_xt[i][m]])
            if tt + 1 < NT or debug:
                S_.dma("sp", f"xs{i}", lambda e: e.dma_start(out=x1_v[:, :, tt * TT:(tt + 1) * TT], in_=xt[i][:]),
                       reads=B_xt[i])

        for st in norm_steps(xt[0], B_xt[0], hb, B_hb, g1, sq_eng="act"):
            st()
        emit_vf(0)
        emit_fchain_a(0)
        emit_qku(0)
        emit_fchain_b(0)
        for tt in range(NT):
            steps = []
            if tt + 1 < NT:
                emit_load(tt + 1)
                steps = norm_steps(xt[(tt + 1) % 2], B_xt[(tt + 1) % 2], hb, B_hb, g1)
            emit_attention(tt, steps, pool_sched(tt))
            emit_poolmix(tt)
            if tt + 1 < NT:
                emit_vf(tt + 1)
                emit_fchain_a(tt + 1)
            flush_deferred()
            emit_wout(tt)
            if tt + 1 < NT:
                emit_qku(tt + 1)
                emit_fchain_b(tt + 1)
            if tt + 1 == NT - 1 or NT == 1:
                emit_w2_block(0, extra_writes=B_win)

        S_.barrier()

        B_yt = B_xt
        ORDER = [NT - 1] + list(range(NT - 1))
        BUFI = lambda k: (NT - 1 + k) % 2
        B_hb2 = [Buf(f"hb2_{c}") for c in range(KC)]
        B_act = [Buf(f"act{c}") for c in range(FC)]
        B_sil = [Buf("sil0"), Buf("sil1")]
        B_xsq2 = [Buf(f"xsq2_{i}") for i in range(NXSQ2)]
        B_lnt2, B_rstd2 = Buf("lnt2"), Buf("rstd2")
        B_ps2 = [Buf(f"ps2_{i}") for i in range(8)]
        GB, UB, DB, NB = [0, 1], [2, 3], [4, 5], 6

        def emit_load2(k):
            i, tt = BUFI(k), ORDER[k]
            S_.dma("sp", f"yl{i}", lambda e: e.dma_start(out=yt[i][:], in_=x1_v[:, :, tt * TT:(tt + 1) * TT]),
                   writes=B_yt[i])

        for bi in (1, 2, 3):
            emit_w2_block(bi)
        for bi, (f0, f1) in enumerate(FBLK):
            S_.dma("pool", f"d{bi}", lambda e, f0=f0, f1=f1: e.dma_start(out=wd_bf[:, f0:f1, :], in_=wd_v[:, f0:f1, :]),
                   writes=[B_wd[bi]])

        def norm2_steps(xtile, B_x, out_fn, bank, sq_eng="pool"):
            steps = []

            def sq(c):
                k = c % NXSQ2
                if sq_eng == "act":
                    S_.op("act", lambda e: e.activation(out=xsq2[k], in_=xtile[:, c, :], func=AF.Square),
                          reads=[B_x[c]], writes=[B_xsq2[k]])
                else:
                    S_.op("pool", lambda e: e.tensor_tensor(out=xsq2[k], in0=xtile[:, c, :], in1=xtile[:, c, :],
                                                            op=ALU.mult),
                          reads=[B_x[c]], writes=[B_xsq2[k]])

            def mm(c):
                k = c % NXSQ2
                S_.op("pe", lambda e: e.matmul(ps[bank][:, :], ones_bf, xsq2[k], start=(c == 0), stop=(c == KC - 1)),
                      reads=[B_xsq2[k], B_cst], writes=[B_ps2[bank]])

            LAG = 1
            for c in range(KC + LAG):
                def st(c=c):
                    if c < KC:
                        sq(c)
                    if c - LAG >= 0:
                        mm(c - LAG)
                steps.append(st)

            def rs():
                S_.op("act", lambda e: e.activation(out=lnt2, in_=ps[bank][:, :], func=AF.Ln, bias=eps_t[:, 0:1],
                                                    scale=1.0 / D),
                      reads=[B_ps2[bank], B_cst], writes=[B_lnt2] + B_xsq2)
                S_.op("act", lambda e: e.activation(out=rstd2, in_=lnt2, func=AF.Exp, scale=-0.5),
                      reads=[B_lnt2] + B_xsq2, writes=[B_rstd2])
            steps.append(rs)
            for c in range(KC):
                steps.append(lambda c=c: out_fn(c))
            return steps

        def hb_steps(k, sq_eng="pool"):
            i = BUFI(k)
            xtile = yt[i]

            def mk_hb(c):
                S_.op("dve", lambda e: e.scalar_tensor_tensor(out=hb2[:, c, :], in0=xtile[:, c, :],
                                                              scalar=g2[:, c:c + 1], in1=rstd2,
                                                              op0=ALU.mult, op1=ALU.mult),
                      reads=[B_yt[i][c], B_rstd2, B_cst], writes=[B_hb2[c]])
            return norm2_steps(xtile, B_yt[i], mk_hb, NB, sq_eng)

        def final_steps(k, sq_eng="pool"):
            i, tt = BUFI(k), ORDER[k]
            xtile = yt[i]

            def mk_y(c):
                S_.op("dve", lambda e: e.scalar_tensor_tensor(out=xtile[:, c, :], in0=xtile[:, c, :],
                                                              scalar=gf[:, c:c + 1], in1=rstd2,
                                                              op0=ALU.mult, op1=ALU.mult),
                      reads=[B_yt[i][c], B_rstd2, B_cst], writes=[B_yt[i][c]])
                S_.dma("sp", f"ys{i}", lambda e: e.dma_start(out=yT_v[:, c, tt * TT:(tt + 1) * TT], in_=xtile[:, c, :]),
                       reads=[B_yt[i][c]])
            return norm2_steps(xtile, B_yt[i], mk_y, 7, sq_eng)

        def fblk_of(fc):
            for bi, (f0, f1) in enumerate(FBLK):
                if f0 <= fc < f1:
                    return bi, fc - f0
            raise AssertionError

        def emit_gateup(tt, pending):
            for fc in range(FC):
                gb, ub = GB[fc % 2], UB[fc % 2]
                bi, fo = fblk_of(fc)

                def mmg(e, bi=bi, fo=fo, gb=gb):
                    ins = None
                    for c in range(KC):
                        ins = e.matmul(ps[gb][:, :], wg_blk[bi][:, c, fo * 128:(fo + 1) * 128], hb2[:, c, :],
                                       start=(c == 0), stop=(c == KC - 1))
                    return ins

                def mmu(e, bi=bi, fo=fo, ub=ub):
                    ins = None
                    for c in range(KC):
                        ins = e.matmul(ps[ub][:, :], wu_blk[bi][:, c, fo * 128:(fo + 1) * 128], hb2[:, c, :],
                                       start=(c == 0), stop=(c == KC - 1))
                    return ins

                S_.op("pe", mmg, reads=B_hb2 + [B_wg[bi]], writes=[B_ps2[gb]])
                S_.op("pe", mmu, reads=B_hb2 + [B_wu[bi]], writes=[B_ps2[ub]])
                sl = fc % 2
                S_.op("act", lambda e, gb=gb, sl=sl: e.activation(out=sil[sl], in_=ps[gb][:, :], func=AF.Silu),
                      reads=[B_ps2[gb]], writes=[B_sil[sl]])
                S_.op("dve", lambda e, fc=fc, ub=ub, sl=sl: e.tensor_tensor(out=actT[:, fc, :], in0=ps[ub][:, :],
                                                                            in1=sil[sl], op=ALU.mult),
                      reads=[B_ps2[ub], B_sil[sl]], writes=[B_act[fc]])
                if pending and fc >= 1:
                    pending.pop(0)()
            while pending:
                pending.pop(0)()

        def emit_down(k, pending, pops=3):
            i = BUFI(k)
            xtile = yt[i]
            for m in range(KC):
                db = DB[m % 2]

                def mmd(e, m=m, db=db):
                    ins = None
                    for fc in range(FC):
                        ins = e.matmul(ps[db][:, :], wd_bf[:, fc, m * 128:(m + 1) * 128], actT[:, fc, :],
                                       start=(fc == 0), stop=(fc == FC - 1))
                    return ins

                S_.op("pe", mmd, reads=B_act + B_wd, writes=[B_ps2[db]])
                S_.op("dve", lambda e, m=m, db=db: e.tensor_tensor(out=xtile[:, m, :], in0=ps[db][:, :],
                                                                   in1=xtile[:, m, :], op=ALU.add),
                      reads=[B_ps2[db], B_yt[i][m]], writes=[B_yt[i][m]])
                for _ in range(pops):
                    if pending:
                        pending.pop(0)()
            while pending:
                pending.pop(0)()

        for st in hb_steps(0, sq_eng="act"):
            st()
        pend_final = []
        for k in range(NT):
            emit_gateup(k, pend_final)
            if k + 1 < NT:
                emit_load2(k + 1)
            if k + 1 < NT:
                emit_down(k, hb_steps(k + 1))
                pend_final = final_steps(k)
            else:
                emit_down(k, final_steps(k, sq_eng="act"), pops=1)
                pend_final = []
        for st in pend_final:
            st()

        S_.barrier()

        print(f"[kernel] arena words: phase1 {p1_end} phase2 {p2_end} of {AW}; "
              f"sem counts: { {k: v for k, v in S_.cnt.items()} }")

        @block.tensor
        def _(e):
            S_.replay("pe", e, semh)

        @block.scalar
        def _(e):
            S_.replay("act", e, semh)

        @block.vector
        def _(e):
            S_.replay("dve", e, semh)

        @block.gpsimd
        def _(e):
            S_.replay("pool", e, semh)

        @block.sync
        def _(e):
            S_.replay("sp", e, semh)

    return nc


def _consts(norm1_g, norm2_g, final_g, pool_scale, b_forget):
    cf = np.zeros((128, NCF), np.float32)
    cf[:, CF_G1:CF_G1 + 8] = np.asarray(norm1_g, np.float32).reshape(8, 128).T
    cf[:, CF_G2:CF_G2 + 8] = np.asarray(norm2_g, np.float32).reshape(8, 128).T
    cf[:, CF_GF:CF_GF + 8] = np.asarray(final_g, np.float32).reshape(8, 128).T
    cf[:, CF_PS:CF_PS + 4] = np.asarray(pool_scale, np.float32).reshape(4, 128).T
    cf[:, CF_BF:CF_BF + 32] = np.tile(np.asarray(b_forget, np.float32).reshape(1, 8), (128, 4))
    t = np.arange(16)
    ic = np.stack([1.0 / np.minimum(t + 1, 2 << g) for g in range(4)]).astype(np.float32)
    cf[:, CF_IC:CF_IC + 64] = ic.reshape(1, 64)
    kk = np.arange(128)
    cf[:, CF_TRI:CF_TRI + 128] = (kk[:, None] <= kk[None, :]).astype(np.float32)
    cf[64, CF_E64:CF_E64 + 128] = 1.0
    cb = np.zeros((128, NCB), np.float32)
    cb[:, CB_ID:CB_ID + 128] = np.eye(128, dtype=np.float32)
    cb[:, CB_MASK:CB_MASK + 128] = np.where(kk[:, None] <= kk[None, :], 0.0, NEG).astype(np.float32)
    for h in range(8):
        cb[h, CB_SEL + h * 128:CB_SEL + (h + 1) * 128] = -1.0
    return cf, cb


_NC_CACHE = {}


def kernel(x, norm1_g, w_in, b_forget, w_pool, pool_scale, w_out, norm2_g, w_gate, w_up, w_down, final_g):
    x = np.asarray(x, np.float32)
    B = x.shape[0]
    cf, cb = _consts(np.asarray(norm1_g)[0], np.asarray(norm2_g)[0], np.asarray(final_g), np.asarray(pool_scale)[0],
                     np.asarray(b_forget)[0])
    shared = {
        "w_in": np.ascontiguousarray(np.asarray(w_in, np.float32)[0]),
        "w_out": np.ascontiguousarray(np.asarray(w_out, np.float32)[0]),
        "w_pool": np.ascontiguousarray(np.asarray(w_pool, np.float32)[0]),
        "w_gate": np.ascontiguousarray(np.asarray(w_gate, np.float32)[0]),
        "w_up": np.ascontiguousarray(np.asarray(w_up, np.float32)[0]),
        "w_down": np.ascontiguousarray(np.asarray(w_down, np.float32)[0]),
        "cf": cf,
        "cb": cb,
    }
    in_maps = [dict(shared, xT=np.ascontiguousarray(x[b].T)) for b in range(B)]
    if "nc" not in _NC_CACHE:
        _NC_CACHE["nc"] = build_program()
    nc = _NC_CACHE["nc"]
    res = run_bass_kernel_spmd(nc, in_maps, core_ids=list(range(B)))
    out = np.stack([np.asarray(res.results[b]["yT"]).T for b in range(B)])
    return np.ascontiguousarray(out.astype(np.float32))
```

```python
from contextlib import ExitStack

import numpy as np
import concourse.bass as bass
import concourse.mybir as mybir
from concourse.bass_utils import run_bass_kernel_spmd

F32 = mybir.dt.float32
BF16 = mybir.dt.bfloat16
AF = mybir.ActivationFunctionType
ALU = mybir.AluOpType

S = 4096
D = 1024
TT = 512
NT = S // TT
KC = D // 128
H = 8
DFF = 2816
FC = DFF // 128
INW = 2056
EPS = 1e-6
NEG = -30000.0

CF_G1, CF_G2, CF_GF, CF_PS, CF_BF, CF_IC, CF_TRI = 0, 8, 16, 24, 28, 60, 124
CF_E64 = 124 + 128
NCF = 124 + 128 + 128
CB_ID, CB_MASK, CB_SEL = 0, 128, 256
NCB = 256 + 1024


class Buf:
    __slots__ = ("name", "w", "r")

    def __init__(self, name):
        self.name = name
        self.w = None
        self.r = []


class Sched:
    ENGS = ("pe", "act", "dve", "pool", "sp")

    def __init__(self):
        self.prog = {e: [] for e in self.ENGS}
        self.cnt = {}
        self.waited = {e: {} for e in self.ENGS}

    @staticmethod
    def _flat(bufs):
        out = []
        for b in bufs:
            if isinstance(b, (list, tuple)):
                out.extend(Sched._flat(b))
            else:
                out.append(b)
        return out

    def _deps(self, eng, reads, writes):
        reads, writes = self._flat(reads), self._flat(writes)
        need = {}

        def add(t):
            if t is not None and need.get(t[0], 0) < t[1]:
                need[t[0]] = t[1]

        for b in reads:
            add(b.w)
        for b in writes:
            add(b.w)
            for t in b.r:
                add(t)
        out = []
        for k, v in need.items():
            if k == eng and eng == "pe":
                continue
            if self.waited[eng].get(k, 0) >= v:
                continue
            self.waited[eng][k] = v
            out.append((k, v))
        return out

    def _commit(self, tok, reads, writes):
        reads, writes = self._flat(reads), self._flat(writes)
        for b in reads:
            b.r.append(tok)
        for b in writes:
            b.w = tok
            b.r = []

    def op(self, eng, fn, reads=(), writes=()):
        waits = self._deps(eng, reads, writes)
        self.cnt[eng] = self.cnt.get(eng, 0) + 1
        tok = (eng, self.cnt[eng])
        self.prog[eng].append((waits, fn, (eng, 1)))
        self._commit(tok, reads, writes)
        return tok

    def dma(self, queue, semkey, fn, reads=(), writes=(), track=True):
        waits = self._deps(queue, reads, writes) if track else []
        self.cnt[semkey] = self.cnt.get(semkey, 0) + 16
        tok = (semkey, self.cnt[semkey])
        self.prog[queue].append((waits, fn, (semkey, 16)))
        if track:
            self._commit(tok, reads, writes)
        return tok

    def barrier(self):
        for e in self.ENGS:
            waits = []
            for k, v in self.cnt.items():
                if k == e and e == "pe":
                    continue
                if self.waited[e].get(k, 0) >= v:
                    continue
                self.waited[e][k] = v
                waits.append((k, v))
            if waits:
                self.prog[e].append((waits, None, None))

    def replay(self, eng, e, semh):
        for waits, fn, sig in self.prog[eng]:
            for k, v in waits:
                e.wait_ge(semh[k], v)
            if fn is None:
                continue
            ins = fn(e)
            ins.then_inc(semh[sig[0]], sig[1])


class Arena:
    def __init__(self, ap):
        self.ap = ap
        self.off = 0
        self.words = ap.shape[1]

    def alloc(self, n, dtype):
        nbytes = n * (2 if dtype == BF16 else 4)
        words = (nbytes + 31) // 32 * 8
        assert self.off + words <= self.words, ("arena overflow", self.off, words, self.words)
        v = self.ap[:, self.off:self.off + words]
        self.off += words
        if dtype == BF16:
            return v.bitcast(BF16)[:, 0:n]
        return v[:, 0:n]


def build_program(debug=False):
    nc = bass.Bass("TRN2", target_bir_lowering=False)
    dt_in = lambda name, shape: nc.dram_tensor(name, shape, F32, kind="ExternalInput").ap()
    xT = dt_in("xT", [D, S])
    w_in = dt_in("w_in", [D, INW])
    w_out = dt_in("w_out", [D, D])
    w_pool = dt_in("w_pool", [4, 128, 128])
    w_gate = dt_in("w_gate", [D, DFF])
    w_up = dt_in("w_up", [D, DFF])
    w_down = dt_in("w_down", [DFF, D])
    cf_d = dt_in("cf", [128, NCF])
    cb_d = dt_in("cb", [128, NCB])
    x1s = nc.dram_tensor("x1s", [D, S], F32, kind=("ExternalOutput" if debug else "Internal")).ap()
    yT = nc.dram_tensor("yT", [D, S], F32, kind="ExternalOutput").ap()

    xT_v = xT.rearrange("(c p) t -> p c t", p=128)
    x1_v = x1s.rearrange("(c p) t -> p c t", p=128)
    yT_v = yT.rearrange("(c p) t -> p c t", p=128)

    S_ = Sched()
    semnames = ["pe", "act", "dve", "pool", "cst", "cst2", "wi0", "wi1", "wi2", "wi3", "xl0b", "w1", "g0", "g1", "g2", "g3", "u0", "u1", "u2", "u3", "d0", "d1", "d2", "d3", "xl0", "xl1", "xs0", "xs1",
                "yl0", "yl1", "ys0", "ys1"]

    with ExitStack() as ctx:
        AW = 53100
        arena_t = ctx.enter_context(nc.sbuf_tensor("arena", [128, AW], F32))
        ps = [ctx.enter_context(nc.psum_tensor(f"ps{i}", [128, 512], F32)) for i in range(8)]
        semh = {n: ctx.enter_context(nc.semaphore(n)) for n in semnames}
        block = ctx.enter_context(nc.Block())

        ar = Arena(arena_t[:, :])
        cf = ar.alloc(NCF, F32)
        cb = ar.alloc(NCB, BF16)
        ones_bf = ar.alloc(128, BF16)
        ones_f = ar.alloc(128, F32)
        eps_t = ar.alloc(8, F32)
        xt = [ar.alloc(KC * TT, F32).rearrange("p (c t) -> p c t", t=TT) for _ in range(2)]
        base_off = ar.off
        B_cf, B_cb, B_ones = Buf("cf"), Buf("cb"), Buf("ones")
        B_cst = [B_cf, B_cb, B_ones]

        g1 = cf[:, CF_G1:CF_G1 + 8]
        g2 = cf[:, CF_G2:CF_G2 + 8]
        gf = cf[:, CF_GF:CF_GF + 8]
        pscale = cf[:, CF_PS:CF_PS + 4]
        bfor = cf[:, CF_BF:CF_BF + 32]
        invcnt = cf[:, CF_IC:CF_IC + 64].rearrange("p (g t) -> p g t", t=16)
        tri_f = cf[:, CF_TRI:CF_TRI + 128]
        e64_f = cf[:, CF_E64:CF_E64 + 128]
        ident_bf = cb[:, CB_ID:CB_ID + 128]
        maskb = cb[:, CB_MASK:CB_MASK + 128]
        selneg = cb[:, CB_SEL:CB_SEL + 1024].rearrange("p (h m) -> p h m", m=128)

        S_.dma("sp", "cst", lambda e: e.dma_start(out=cf, in_=cf_d), writes=[B_cf])
        S_.dma("pool", "cst2", lambda e: e.dma_start(out=cb, in_=cb_d), writes=[B_cb])
        B_xt = [[Buf(f"xt{i}_{c}") for c in range(KC)] for i in range(2)]
        S_.dma("sp", "xl0", lambda e: e.dma_start(out=xt[0][:, 0:4, :], in_=xT_v[:, 0:4, 0:TT]), writes=B_xt[0][0:4])
        S_.dma("sp", "xl0b", lambda e: e.dma_start(out=xt[0][:, 4:8, :], in_=xT_v[:, 4:8, 0:TT]), writes=B_xt[0][4:8])
        S_.op("pool", lambda e: e.memset(ones_bf, 1.0), writes=[B_ones])
        S_.op("pool", lambda e: e.memset(ones_f, 1.0), writes=[B_ones])
        S_.op("pool", lambda e: e.memset(eps_t, EPS), writes=[B_ones])

        win_bf = ar.alloc(KC * INW, BF16).rearrange("p (c n) -> p c n", n=INW)
        wout_bf = ar.alloc(KC * D, BF16).rearrange("p (c n) -> p c n", n=D)
        wpool_bf = ar.alloc(4 * 128, BF16).rearrange("p (g n) -> p g n", n=128)
        kT = ar.alloc(4 * S, BF16).rearrange("p (a t) -> p a t", t=S)
        NJ = S // 128
        vst2d = ar.alloc(NJ * H * 65 + 64, BF16)
        vst = vst2d[:, 0:NJ * H * 65].rearrange("p (j h d) -> p j h d", h=H, d=65)
        NXSQ = 3
        xsq = [ar.alloc(TT, BF16) for _ in range(NXSQ)]
        hb2d = ar.alloc(KC * TT, BF16)
        hb = hb2d.rearrange("p (c t) -> p c t", t=TT)
        lnt = hb2d[:, 6 * TT:8 * TT].bitcast(F32)
        rstd = ar.alloc(TT, F32)
        qz = ar.alloc(H * TT, BF16).rearrange("p (h t) -> p h t", t=TT)
        UW = TT + 16
        uT = ar.alloc(4 * UW, F32).rearrange("p (g t) -> p g t", t=UW)
        ptmp = [ar.alloc(UW, F32) for _ in range(2)]
        pooledT = ar.alloc(4 * TT, BF16).rearrange("p (g t) -> p g t", t=TT)
        mixT = ar.alloc(4 * TT, BF16).rearrange("p (g t) -> p g t", t=TT)
        attnT = ar.alloc(4 * TT, BF16).rearrange("p (a t) -> p a t", t=TT)
        NPT = 3
        pT = [ar.alloc(TT, BF16) for _ in range(NPT)]
        nr = [ar.alloc(TT, F32) for _ in range(2)]
        CT_bf = ar.alloc(TT, BF16)
        fz = ar.alloc(32, F32)
        lst = ar.alloc(NJ * H, F32).rearrange("p (n h) -> p n h", h=H)
        accs = ar.alloc((NJ + 1) * H, F32).rearrange("p (n h) -> p n h", h=H)
        Cst = ar.alloc(NJ * H, F32).rearrange("p (n h) -> p n h", h=H)
        tmp16 = ar.alloc(16, F32)
        p1_end = ar.off

        ar.off = base_off
        FBLK = [(0, 6), (6, 12), (12, 17), (17, 22)]
        wg_blk, wu_blk = [None] * 4, [None] * 4
        for bi in (0,):
            f0, f1 = FBLK[bi]
            wg_blk[bi] = ar.alloc(KC * (f1 - f0) * 128, BF16).rearrange("p (c n) -> p c n", c=KC)
            wu_blk[bi] = ar.alloc(KC * (f1 - f0) * 128, BF16).rearrange("p (c n) -> p c n", c=KC)
        assert ar.off <= base_off + (KC * INW * 2 + 3) // 4, "prefetch blocks must fit in the w_in region"
        for bi in (1, 2, 3):
            f0, f1 = FBLK[bi]
            wg_blk[bi] = ar.alloc(KC * (f1 - f0) * 128, BF16).rearrange("p (c n) -> p c n", c=KC)
            wu_blk[bi] = ar.alloc(KC * (f1 - f0) * 128, BF16).rearrange("p (c n) -> p c n", c=KC)
        wd_bf = ar.alloc(FC * D, BF16).rearrange("p (c n) -> p c n", n=D)
        yt = xt
        hb2 = ar.alloc(KC * TT, BF16).rearrange("p (c t) -> p c t", t=TT)
        lnt2 = ar.alloc(TT, F32)
        xsq2 = [lnt2.bitcast(BF16)[:, 0:TT], lnt2.bitcast(BF16)[:, TT:2 * TT]]
        NXSQ2 = 2
        rstd2 = ar.alloc(TT, F32)
        actT = ar.alloc(FC * TT, BF16).rearrange("p (c t) -> p c t", t=TT)
        sil = [ar.alloc(TT, F32) for _ in range(2)]
        p2_end = ar.off
        wg_v = w_gate.rearrange("(c p) n -> p c n", p=128)
        wu_v = w_up.rearrange("(c p) n -> p c n", p=128)
        wd_v = w_down.rearrange("(c p) n -> p c n", p=128)
        B_wg = [Buf(f"wg{i}") for i in range(4)]
        B_wu = [Buf(f"wu{i}") for i in range(4)]
        B_wd = [Buf(f"wd{i}") for i in range(4)]

        def emit_w2_block(bi, extra_writes=()):
            f0, f1 = FBLK[bi]
            S_.dma("pool", f"g{bi}", lambda e: e.dma_start(out=wg_blk[bi][:, :, :], in_=wg_v[:, :, f0 * 128:f1 * 128]),
                   writes=[B_wg[bi]] + list(extra_writes))
            S_.dma("pool", f"u{bi}", lambda e: e.dma_start(out=wu_blk[bi][:, :, :], in_=wu_v[:, :, f0 * 128:f1 * 128]),
                   writes=[B_wu[bi]] + list(extra_writes))

        B_w1 = Buf("w1")
        win_v = w_in.rearrange("(c p) n -> p c n", p=128)
        B_win = [Buf(f"win{i}") for i in range(4)]
        for i, (c0, c1) in enumerate([(1024, 1544), (0, 512), (512, 1024), (1544, 2056)]):
            S_.dma("pool", f"wi{i}", lambda e, c0=c0, c1=c1: e.dma_start(out=win_bf[:, :, c0:c1], in_=win_v[:, :, c0:c1]),
                   writes=[B_win[i]])
        for c in range(KC):
            S_.dma("pool", "w1", lambda e, c=c: e.dma_start(out=wout_bf[:, c, :], in_=w_out[c * 128:(c + 1) * 128, :]),
                   track=False)
        for g in range(4):
            S_.dma("pool", "w1", lambda e, g=g: e.dma_start(out=wpool_bf[:, g, :], in_=w_pool[g]), track=False)
        B_w1.w = ("w1", S_.cnt["w1"])
        B_xsq = [Buf(f"xsq{i}") for i in range(NXSQ)]
        B_hb = [Buf(f"hb{c}") for c in range(KC)]
        B_rstd = Buf("rstd")
        B_ps = [Buf(f"ps{i}") for i in range(8)]
        B_qz = [Buf(f"qz{h}") for h in range(H)]
        B_kT = [[Buf(f"kT{a}_{t}") for t in range(NT)] for a in range(4)]
        B_v = [Buf(f"v{j}") for j in range(NJ)]
        B_ptmp = [Buf("ptmp0"), Buf("ptmp1")]
        B_pooled = [Buf(f"pooled{g}") for g in range(4)]
        B_mix = [Buf(f"mix{g}") for g in range(4)]
        B_attn = [Buf(f"attn{h}") for h in range(H)]
        B_pT = [Buf(f"pT{i}") for i in range(NPT)]
        B_nr, B_CT = [Buf("nr0"), Buf("nr1")], Buf("CTbf")
        B_fz = Buf("fz")
        B_lst = [Buf(f"lst{n}") for n in range(NJ)]
        B_Cst = [Buf(f"Cst{t}") for t in range(NT)]
        B_tmp16 = Buf("tmp16")

        ST_BANKS = [0, 1, 2]
        ACC_BANKS = [3, 4]
        PB_BANKS = [5, 6]
        MISC = 7
        pb_ctr = [0]

        def next_pb():
            b = PB_BANKS[pb_ctr[0] % 2]
            pb_ctr[0] += 1
            return b

        xsq_ctr = [0]

        B_vones = Buf("vones")
        B_uT = [Buf(f"uT{g}") for g in range(4)]
        B_uhalo = Buf("uhalo")
        B_acc = [Buf(f"acc{n}") for n in range(NJ + 1)]
        S_.op("pool", lambda e: e.memset(vst2d, 1.0), writes=[B_vones])
        S_.op("pool", lambda e: e.memset(uT[:, :, 0:16], 0.0), writes=[B_uhalo])
        S_.op("pool", lambda e: e.memset(accs[:, 0, :], 0.0), writes=[B_acc[0]])
        B_init = Buf("init")
        S_.op("pool", lambda e: e.memset(qz[:, :, :], 0.0), writes=[B_init])
        S_.op("pool", lambda e: e.memset(CT_bf, 0.0), writes=[B_init])
        S_.op("pool", lambda e: e.memset(nr[0], 0.0), writes=[B_init])
        S_.op("pool", lambda e: e.memset(nr[1], 0.0), writes=[B_init])

        def emit_load(tt):
            i = tt % 2
            S_.dma("sp", f"xl{i}", lambda e: e.dma_start(out=xt[i][:], in_=xT_v[:, :, tt * TT:(tt + 1) * TT]),
                   writes=B_xt[i])

        def norm_steps(xtile, B_x, hbt, B_h, gcol, misc_bank=MISC, sq_eng="pool"):
            steps = []
            ks = []
            for c in range(KC):
                k = xsq_ctr[0] % NXSQ
                xsq_ctr[0] += 1
                ks.append(k)

            def sq(c):
                k = ks[c]
                if sq_eng == "act":
                    S_.op("act", lambda e: e.activation(out=xsq[k], in_=xtile[:, c, :], func=AF.Square),
                          reads=[B_x[c]], writes=[B_xsq[k]])
                else:
                    S_.op("pool", lambda e: e.tensor_tensor(out=xsq[k], in0=xtile[:, c, :], in1=xtile[:, c, :],
                                                            op=ALU.mult),
                          reads=[B_x[c]], writes=[B_xsq[k]])

            def mm(c):
                k = ks[c]
                S_.op("pe", lambda e: e.matmul(ps[misc_bank][:, :], ones_bf, xsq[k], start=(c == 0),
                                               stop=(c == KC - 1)),
                      reads=[B_xsq[k], B_cst], writes=[B_ps[misc_bank]])

            LAG = 2
            for c in range(KC + LAG):
                def st(c=c):
                    if c < KC:
                        sq(c)
                    if c - LAG >= 0:
                        mm(c - LAG)
                steps.append(st)

            def rs():
                S_.op("act", lambda e: e.activation(out=lnt, in_=ps[misc_bank][:, :], func=AF.Ln, bias=eps_t[:, 0:1],
                                                    scale=1.0 / D),
                      reads=[B_ps[misc_bank], B_cst], writes=[B_h[6], B_h[7]])
                S_.op("act", lambda e: e.activation(out=rstd, in_=lnt, func=AF.Exp, scale=-0.5),
                      reads=[B_h[6], B_h[7]], writes=[B_rstd])
            steps.append(rs)
            for c in range(KC):
                def hbs(c=c):
                    S_.op("dve", lambda e: e.scalar_tensor_tensor(out=hbt[:, c, :], in0=xtile[:, c, :],
                                                                  scalar=gcol[:, c:c + 1], in1=rstd,
                                                                  op0=ALU.mult, op1=ALU.mult),
                          reads=[B_x[c], B_rstd, B_cst], writes=[B_h[c]])
                steps.append(hbs)
            return steps

        evac_ctr = [0]

        def evac_copy(out_ap, in_ap, reads, writes, scale=None):
            k = evac_ctr[0]
            evac_ctr[0] += 1
            if k % 2 == 0:
                if scale is None:
                    S_.op("act", lambda e: e.activation(out=out_ap, in_=in_ap, func=AF.Copy), reads=reads, writes=writes)
                else:
                    S_.op("act", lambda e: e.activation(out=out_ap, in_=in_ap, func=AF.Copy, scale=scale),
                          reads=reads, writes=writes)
            else:
                if scale is None:
                    S_.op("dve", lambda e: e.tensor_copy(out=out_ap, in_=in_ap), reads=reads, writes=writes)
                else:
                    S_.op("dve", lambda e: e.tensor_scalar(out=out_ap, in0=in_ap, scalar1=scale, scalar2=None,
                                                           op0=ALU.mult), reads=reads, writes=writes)

        def emit_vf(tt):
            for blk in range(4):
                b = next_pb()
                j = tt * 4 + blk

                def mmv(e, blk=blk, b=b):
                    ins = None
                    for c in range(KC):
                        lhsT = hb[:, c, blk * 128:(blk + 1) * 128]
                        e.matmul(ps[b][:, :], lhsT, win_bf[:, c, 1024:1536], start=(c == 0), stop=(c == KC - 1))
                        ins = e.matmul(ps[MISC][:, blk * 8:(blk + 1) * 8], lhsT, win_bf[:, c, 1536:1544],
                                       start=(c == 0), stop=(c == KC - 1))
                    return ins

                S_.op("pe", mmv, reads=B_hb + [B_win[0]], writes=[B_ps[b], B_ps[MISC]])
                evac_copy(vst[:, j, :, 0:64], ps[b][:, :].rearrange("p (h d) -> p h d", d=64), [B_ps[b], B_vones],
                          [B_v[j]])

        def emit_qku(tt):
            specs = []
            for a in range(4):
                specs.append(("q", a, a * 128))
            for a in range(4):
                specs.append(("k", a, 512 + a * 128))
            for g in range(4):
                specs.append(("u", g, 1544 + g * 128))
            for kind, idx, col in specs:
                b = next_pb()

                def mm(e, col=col, b=b):
                    ins = None
                    for c in range(KC):
                        ins = e.matmul(ps[b][:, :], win_bf[:, c, col:col + 128], hb[:, c, :], start=(c == 0),
                                       stop=(c == KC - 1))
                    return ins

                wb = {"q": B_win[1], "k": B_win[2], "u": B_win[3]}[kind]
                S_.op("pe", mm, reads=B_hb + [wb], writes=[B_ps[b]])
                if kind == "q":
                    evac_copy(qz[0:64, 2 * idx, :], ps[b][0:64, :], [B_ps[b], B_init], [B_qz[2 * idx]], scale=0.125)
                    evac_copy(qz[64:128, 2 * idx + 1, :], ps[b][64:128, :], [B_ps[b], B_init], [B_qz[2 * idx + 1]],
                              scale=0.125)
                elif kind == "k":
                    evac_copy(kT[:, idx, tt * TT:(tt + 1) * TT], ps[b][:, :], [B_ps[b]], [B_kT[idx][tt]])
                else:
                    evac_copy(uT[:, idx, 16:UW], ps[b][:, :], [B_ps[b]], [B_uT[idx]])

        def emit_fchain_a(tt):
            n0 = tt * 4
            S_.op("dve", lambda e: e.tensor_tensor(out=fz, in0=ps[MISC][:, 0:32], in1=bfor, op=ALU.add),
                  reads=[B_ps[MISC], B_cst], writes=[B_fz])
            S_.op("act", lambda e: e.activation(out=fz, in_=fz, func=AF.Exp, scale=-1.0), reads=[B_fz], writes=[B_fz])
            lview = lst[:, n0:n0 + 4, :].rearrange("p n h -> p (n h)")
            S_.op("act", lambda e: e.activation(out=lview, in_=fz, func=AF.Ln, bias=1.0, scale=1.0),
                  reads=[B_fz], writes=[B_lst[n0 + i] for i in range(4)])
            for i in range(4):
                n = n0 + i
                S_.op("dve", lambda e, n=n: e.tensor_tensor(out=accs[:, n + 1, :], in0=accs[:, n, :], in1=lst[:, n, :],
                                                            op=ALU.add),
                      reads=[B_acc[n], B_lst[n]], writes=[B_acc[n + 1]])

        def emit_fchain_b(tt):
            n0 = tt * 4
            bct = next_pb()

            def mmc(e):
                ins = None
                for i in range(4):
                    n = n0 + i
                    e.matmul(ps[MISC][:, 32 + i * 8:40 + i * 8], tri_f, lst[:, n, :], start=True, stop=False)
                    e.matmul(ps[MISC][:, 32 + i * 8:40 + i * 8], ones_f, accs[:, n, :], start=False, stop=True)
                for i in range(4):
                    n = n0 + i
                    e.matmul(ps[bct][0:8, i * 128:(i + 1) * 128], lst[:, n, :], tri_f, start=True, stop=False)
                    ins = e.matmul(ps[bct][0:8, i * 128:(i + 1) * 128], accs[:, n, :], ones_f, start=False, stop=True)
                return ins

            S_.op("pe", mmc, reads=[B_lst[n0 + i] for i in range(4)] + [B_acc[n0 + i] for i in range(4)] + [B_cst],
                  writes=[B_ps[MISC], B_ps[bct]])
            S_.op("dve", lambda e: e.tensor_copy(out=Cst[:, n0:n0 + 4, :].rearrange("p n h -> p (n h)"),
                                                 in_=ps[MISC][:, 32:64]),
                  reads=[B_ps[MISC]], writes=[B_Cst[tt]])
            S_.op("dve", lambda e: e.tensor_copy(out=CT_bf[0:8, :], in_=ps[bct][0:8, :]),
                  reads=[B_ps[bct], B_init], writes=[B_CT])

        def pool_sched(tt):
            sched = {}
            offs = [0, 1, 6, 12, 19]

            def chain(g):
                w = 2 << g
                src = uT[:, g, :]
                srcB = [B_uT[g], B_uhalo]
                shift, lo, step = 1, 0, 0
                while shift < w:
                    dst = ptmp[step % 2]
                    nlo = lo + shift
                    S_.op("pool", lambda e, dst=dst, src=src, nlo=nlo, shift=shift:
                          e.tensor_tensor(out=dst[:, nlo:UW], in0=src[:, nlo:UW], in1=src[:, nlo - shift:UW - shift],
                                          op=ALU.add),
                          reads=srcB, writes=[B_ptmp[step % 2]])
                    src = dst
                    srcB = [B_ptmp[step % 2]]
                    lo = nlo
                    shift *= 2
                    step += 1
                return src, srcB

            res = {}

            def do_chain(g):
                res[g] = chain(g)

            def do_final(g):
                w = 2 << g
                src, srcB = res[g]
                S_.op("dve", lambda e: e.scalar_tensor_tensor(
                    out=pooledT[:, g, :], in0=src[:, 16:UW], scalar=1.0 / w, in1=uT[:, g, 16:UW],
                    op0=ALU.mult, op1=ALU.subtract),
                    reads=srcB + [B_uT[g]], writes=[B_pooled[g]])
                if tt == 0:
                    S_.op("dve", lambda e: e.tensor_tensor(out=tmp16, in0=src[:, 16:32], in1=invcnt[:, g, :],
                                                           op=ALU.mult),
                          reads=srcB + [B_cst], writes=[B_tmp16])
                    S_.op("dve", lambda e: e.tensor_tensor(out=pooledT[:, g, 0:16], in0=tmp16, in1=uT[:, g, 16:32],
                                                           op=ALU.subtract),
                          reads=[B_tmp16, B_uT[g]], writes=[B_pooled[g]])

            def halo():
                S_.op("pool", lambda e: e.tensor_copy(out=uT[:, :, 0:16], in_=uT[:, :, TT:UW]),
                      reads=B_uT, writes=[B_uhalo])

            sched.setdefault(offs[0], []).append(lambda: do_chain(0))
            for g in range(4):
                sched.setdefault(offs[g + 1], []).append(lambda g=g: do_final(g))
                if g + 1 < 4:
                    sched.setdefault(offs[g + 1], []).append(lambda g=g: do_chain(g + 1))
            sched.setdefault(offs[4], []).append(halo)
            return sched

        def emit_poolmix(tt):
            for g in range(4):
                b = next_pb()
                S_.op("pe", lambda e, g=g, b=b: e.matmul(ps[b][:, :], wpool_bf[:, g, :], pooledT[:, g, :], start=True,
                                                         stop=True),
                      reads=[B_pooled[g], B_w1], writes=[B_ps[b]])
                S_.op("act", lambda e, g=g, b=b: e.activation(out=mixT[:, g, :], in_=ps[b][:, :], func=AF.Copy,
                                                              scale=pscale[:, g:g + 1]),
                      reads=[B_ps[b], B_cst], writes=[B_mix[g]])

        deferred = []

        def flush_deferred(now=None):
            keep = []
            for due, fn in deferred:
                if now is None or due <= now:
                    fn()
                else:
                    keep.append((due, fn))
            deferred[:] = keep

        def emit_attention(tt, steps, psched):
            nb = 4 * tt + 4
            blocks = [(h, j) for h in range(H) for j in range(nb)]
            LOOK = 2
            state = {}
            steps = list(steps)

            def qk(n):
                h, j = blocks[n]
                a = h // 2
                diag = j >= 4 * tt
                c0 = 128 * (j - 4 * tt) if diag else 0
                sb = ST_BANKS[n % 3]
                pi = n % NPT

                def mm(e):
                    e.matmul(ps[sb][:, c0:TT], kT[:, a, j * 128:(j + 1) * 128], qz[:, h, c0:TT],
                             start=True, stop=False)
                    ins = e.matmul(ps[sb][:, c0:TT], selneg[:, h, :], CT_bf[:, c0:TT], start=False,
                                   stop=(not diag))
                    if diag:
                        ins = e.matmul(ps[sb][:, c0:c0 + 128], ident_bf, maskb, start=False, stop=True)
                    return ins

                S_.op("pe", mm, reads=[B_kT[a][j // 4], B_qz[h], B_CT, B_cst], writes=[B_ps[sb]])
                S_.op("act", lambda e: e.activation(out=pT[pi][:, c0:TT], in_=ps[sb][:, c0:TT], func=AF.Exp,
                                                    bias=Cst[:, j, h:h + 1], scale=1.0),
                      reads=[B_ps[sb], B_Cst[j // 4]], writes=[B_pT[pi]])
                state[n] = (c0, pi)

            def pv(n, it):
                h, j = blocks[n]
                a, p0 = h // 2, 64 * (h % 2)
                c0, pi = state.pop(n)
                ab = ACC_BANKS[h % 2]
                vo = (j * H + h) * 65
                S_.op("pe", lambda e: e.matmul(ps[ab][:, c0:TT], vst2d[:, vo:vo + 128], pT[pi][:, c0:TT], start=(j == 0),
                                               stop=(j == nb - 1)),
                      reads=[B_pT[pi], B_v[j], B_vones], writes=[B_ps[ab]])
                if j == nb - 1:
                    nrb = nr[h % 2]
                    Bn = B_nr[h % 2]
                    S_.op("dve", lambda e: e.tensor_copy(out=nrb[0:64, :], in_=ps[ab][0:64, :]),
                          reads=[B_ps[ab], B_init], writes=[Bn])
                    S_.op("dve", lambda e: e.reciprocal(out=nrb[64:65, :], in_=ps[ab][64:65, :]),
                          reads=[B_ps[ab], B_init], writes=[Bn])

                    def fin():
                        b = next_pb()
                        S_.op("pe", lambda e: e.matmul(ps[b][:, :], e64_f, nrb, start=True, stop=True),
                              reads=[Bn, B_cst], writes=[B_ps[b]])
                        S_.op("dve", lambda e: e.tensor_tensor(out=attnT[p0:p0 + 64, a, :], in0=nrb[0:64, :],
                                                               in1=ps[b][0:64, :], op=ALU.mult),
                              reads=[Bn, B_ps[b]], writes=[B_attn[h]])
                    deferred.append((it + 7, fin))

            nblk = len(blocks)
            start_steps = 26 if nblk >= 64 else max(2, nblk // 8)
            for n in range(nblk + LOOK):
                if n < nblk:
                    qk(n)
                flush_deferred(n)
                if n - LOOK >= 0:
                    pv(n - LOOK, n)
                if n >= start_steps and steps:
                    steps.pop(0)()
                for fn in psched.pop(n - 3, []):
                    fn()
            for st in steps:
                st()
            for k in sorted(psched):
                for fn in psched[k]:
                    fn()

        def emit_wout(tt):
            i = tt % 2
            for m in range(KC):
                b = next_pb()

                def mm(e, m=m, b=b):
                    ins = None
                    for ec in range(8):
                        rhs = attnT[:, ec, :] if ec < 4 else mixT[:, ec - 4, :]
                        ins = e.matmul(ps[b][:, :], wout_bf[:, ec, m * 128:(m + 1) * 128], rhs, start=(ec == 0),
                                       stop=(ec == 7))
                    return ins

                S_.op("pe", mm, reads=B_attn + B_mix + [B_w1], writes=[B_ps[b]])
                S_.op("dve", lambda e, m=m, b=b: e.tensor_tensor(out=xt[i][:, m, :], in0=ps[b][:, :], in1=xt[i][:, m, :],
                                                                 op=ALU.add),
                      reads=[B_ps[b], B_xt[i][m]], writes=[B_xt[i][m]])
            if tt + 1 < NT or debug:
                S_.dma("sp", f"xs{i}", lambda e: e.dma_start(out=x1_v[:, :, tt * TT:(tt + 1) * TT], in_=xt[i][:]),
                       reads=B_xt[i])

        for st in norm_steps(xt[0], B_xt[0], hb, B_hb, g1, sq_eng="act"):
            st()
        emit_vf(0)
        emit_fchain_a(0)
        emit_qku(0)
        emit_fchain_b(0)
        for tt in range(NT):
            steps = []
            if tt + 1 < NT:
                emit_load(tt + 1)
                steps = norm_steps(xt[(tt + 1) % 2], B_xt[(tt + 1) % 2], hb, B_hb, g1)
            emit_attention(tt, steps, pool_sched(tt))
            emit_poolmix(tt)
            if tt + 1 < NT:
                emit_vf(tt + 1)
                emit_fchain_a(tt + 1)
            flush_deferred()
            emit_wout(tt)
            if tt + 1 < NT:
                emit_qku(tt + 1)
                emit_fchain_b(tt + 1)
            if tt + 1 == NT - 1 or NT == 1:
                emit_w2_block(0, extra_writes=B_win)

        S_.barrier()

        B_yt = B_xt
        ORDER = [NT - 1] + list(range(NT - 1))
        BUFI = lambda k: (NT - 1 + k) % 2
        B_hb2 = [Buf(f"hb2_{c}") for c in range(KC)]
        B_act = [Buf(f"act{c}") for c in range(FC)]
        B_sil = [Buf("sil0"), Buf("sil1")]
        B_xsq2 = [Buf(f"xsq2_{i}") for i in range(NXSQ2)]
        B_lnt2, B_rstd2 = Buf("lnt2"), Buf("rstd2")
        B_ps2 = [Buf(f"ps2_{i}") for i in range(8)]
        GB, UB, DB, NB = [0, 1], [2, 3], [4, 5], 6

        def emit_load2(k):
            i, tt = BUFI(k), ORDER[k]
            S_.dma("sp", f"yl{i}", lambda e: e.dma_start(out=yt[i][:], in_=x1_v[:, :, tt * TT:(tt + 1) * TT]),
                   writes=B_yt[i])

        for bi in (1, 2, 3):
            emit_w2_block(bi)
        for bi, (f0, f1) in enumerate(FBLK):
            S_.dma("pool", f"d{bi}", lambda e, f0=f0, f1=f1: e.dma_start(out=wd_bf[:, f0:f1, :], in_=wd_v[:, f0:f1, :]),
                   writes=[B_wd[bi]])

        def norm2_steps(xtile, B_x, out_fn, bank, sq_eng="pool"):
            steps = []

            def sq(c):
                k = c % NXSQ2
                if sq_eng == "act":
                    S_.op("act", lambda e: e.activation(out=xsq2[k], in_=xtile[:, c, :], func=AF.Square),
                          reads=[B_x[c]], writes=[B_xsq2[k]])
                else:
                    S_.op("pool", lambda e: e.tensor_tensor(out=xsq2[k], in0=xtile[:, c, :], in1=xtile[:, c, :],
                                                            op=ALU.mult),
                          reads=[B_x[c]], writes=[B_xsq2[k]])

            def mm(c):
                k = c % NXSQ2
                S_.op("pe", lambda e: e.matmul(ps[bank][:, :], ones_bf, xsq2[k], start=(c == 0), stop=(c == KC - 1)),
                      reads=[B_xsq2[k], B_cst], writes=[B_ps2[bank]])

            LAG = 1
            for c in range(KC + LAG):
                def st(c=c):
                    if c < KC:
                        sq(c)
                    if c - LAG >= 0:
                        mm(c - LAG)
                steps.append(st)

            def rs():
                S_.op("act", lambda e: e.activation(out=lnt2, in_=ps[bank][:, :], func=AF.Ln, bias=eps_t[:, 0:1],
                                                    scale=1.0 / D),
                      reads=[B_ps2[bank], B_cst], writes=[B_lnt2] + B_xsq2)
                S_.op("act", lambda e: e.activation(out=rstd2, in_=lnt2, func=AF.Exp, scale=-0.5),
                      reads=[B_lnt2] + B_xsq2, writes=[B_rstd2])
            steps.append(rs)
            for c in range(KC):
                steps.append(lambda c=c: out_fn(c))
            return steps

        def hb_steps(k, sq_eng="pool"):
            i = BUFI(k)
            xtile = yt[i]

            def mk_hb(c):
                S_.op("dve", lambda e: e.scalar_tensor_tensor(out=hb2[:, c, :], in0=xtile[:, c, :],
                                                              scalar=g2[:, c:c + 1], in1=rstd2,
                                                              op0=ALU.mult, op1=ALU.mult),
                      reads=[B_yt[i][c], B_rstd2, B_cst], writes=[B_hb2[c]])
            return norm2_steps(xtile, B_yt[i], mk_hb, NB, sq_eng)

        def final_steps(k, sq_eng="pool"):
            i, tt = BUFI(k), ORDER[k]
            xtile = yt[i]

            def mk_y(c):
                S_.op("dve", lambda e: e.scalar_tensor_tensor(out=xtile[:, c, :], in0=xtile[:, c, :],
                                                              scalar=gf[:, c:c + 1], in1=rstd2,
                                                              op0=ALU.mult, op1=ALU.mult),
                      reads=[B_yt[i][c], B_rstd2, B_cst], writes=[B_yt[i][c]])
                S_.dma("sp", f"ys{i}", lambda e: e.dma_start(out=yT_v[:, c, tt * TT:(tt + 1) * TT], in_=xtile[:, c, :]),
                       reads=[B_yt[i][c]])
            return norm2_steps(xtile, B_yt[i], mk_y, 7, sq_eng)

        def fblk_of(fc):
            for bi, (f0, f1) in enumerate(FBLK):
                if f0 <= fc < f1:
                    return bi, fc - f0
            raise AssertionError

        def emit_gateup(tt, pending):
            for fc in range(FC):
                gb, ub = GB[fc % 2], UB[fc % 2]
                bi, fo = fblk_of(fc)

                def mmg(e, bi=bi, fo=fo, gb=gb):
                    ins = None
                    for c in range(KC):
                        ins = e.matmul(ps[gb][:, :], wg_blk[bi][:, c, fo * 128:(fo + 1) * 128], hb2[:, c, :],
                                       start=(c == 0), stop=(c == KC - 1))
                    return ins

                def mmu(e, bi=bi, fo=fo, ub=ub):
                    ins = None
                    for c in range(KC):
                        ins = e.matmul(ps[ub][:, :], wu_blk[bi][:, c, fo * 128:(fo + 1) * 128], hb2[:, c, :],
                                       start=(c == 0), stop=(c == KC - 1))
                    return ins

                S_.op("pe", mmg, reads=B_hb2 + [B_wg[bi]], writes=[B_ps2[gb]])
                S_.op("pe", mmu, reads=B_hb2 + [B_wu[bi]], writes=[B_ps2[ub]])
                sl = fc % 2
                S_.op("act", lambda e, gb=gb, sl=sl: e.activation(out=sil[sl], in_=ps[gb][:, :], func=AF.Silu),
                      reads=[B_ps2[gb]], writes=[B_sil[sl]])
                S_.op("dve", lambda e, fc=fc, ub=ub, sl=sl: e.tensor_tensor(out=actT[:, fc, :], in0=ps[ub][:, :],
                                                                            in1=sil[sl], op=ALU.mult),
                      reads=[B_ps2[ub], B_sil[sl]], writes=[B_act[fc]])
                if pending and fc >= 1:
                    pending.pop(0)()
            while pending:
                pending.pop(0)()

        def emit_down(k, pending, pops=3):
            i = BUFI(k)
            xtile = yt[i]
            for m in range(KC):
                db = DB[m % 2]

                def mmd(e, m=m, db=db):
                    ins = None
                    for fc in range(FC):
                        ins = e.matmul(ps[db][:, :], wd_bf[:, fc, m * 128:(m + 1) * 128], actT[:, fc, :],
                                       start=(fc == 0), stop=(fc == FC - 1))
                    return ins

                S_.op("pe", mmd, reads=B_act + B_wd, writes=[B_ps2[db]])
                S_.op("dve", lambda e, m=m, db=db: e.tensor_tensor(out=xtile[:, m, :], in0=ps[db][:, :],
                                                                   in1=xtile[:, m, :], op=ALU.add),
                      reads=[B_ps2[db], B_yt[i][m]], writes=[B_yt[i][m]])
                for _ in range(pops):
                    if pending:
                        pending.pop(0)()
            while pending:
                pending.pop(0)()

        for st in hb_steps(0, sq_eng="act"):
            st()
        pend_final = []
        for k in range(NT):
            emit_gateup(k, pend_final)
            if k + 1 < NT:
                emit_load2(k + 1)
            if k + 1 < NT:
                emit_down(k, hb_steps(k + 1))
                pend_final = final_steps(k)
            else:
                emit_down(k, final_steps(k, sq_eng="act"), pops=1)
                pend_final = []
        for st in pend_final:
            st()

        S_.barrier()

        print(f"[kernel] arena words: phase1 {p1_end} phase2 {p2_end} of {AW}; "
              f"sem counts: { {k: v for k, v in S_.cnt.items()} }")

        @block.tensor
        def _(e):
            S_.replay("pe", e, semh)

        @block.scalar
        def _(e):
            S_.replay("act", e, semh)

        @block.vector
        def _(e):
            S_.replay("dve", e, semh)

        @block.gpsimd
        def _(e):
            S_.replay("pool", e, semh)

        @block.sync
        def _(e):
            S_.replay("sp", e, semh)

    return nc


def _consts(norm1_g, norm2_g, final_g, pool_scale, b_forget):
    cf = np.zeros((128, NCF), np.float32)
    cf[:, CF_G1:CF_G1 + 8] = np.asarray(norm1_g, np.float32).reshape(8, 128).T
    cf[:, CF_G2:CF_G2 + 8] = np.asarray(norm2_g, np.float32).reshape(8, 128).T
    cf[:, CF_GF:CF_GF + 8] = np.asarray(final_g, np.float32).reshape(8, 128).T
    cf[:, CF_PS:CF_PS + 4] = np.asarray(pool_scale, np.float32).reshape(4, 128).T
    cf[:, CF_BF:CF_BF + 32] = np.tile(np.asarray(b_forget, np.float32).reshape(1, 8), (128, 4))
    t = np.arange(16)
    ic = np.stack([1.0 / np.minimum(t + 1, 2 << g) for g in range(4)]).astype(np.float32)
    cf[:, CF_IC:CF_IC + 64] = ic.reshape(1, 64)
    kk = np.arange(128)
    cf[:, CF_TRI:CF_TRI + 128] = (kk[:, None] <= kk[None, :]).astype(np.float32)
    cf[64, CF_E64:CF_E64 + 128] = 1.0
    cb = np.zeros((128, NCB), np.float32)
    cb[:, CB_ID:CB_ID + 128] = np.eye(128, dtype=np.float32)
    cb[:, CB_MASK:CB_MASK + 128] = np.where(kk[:, None] <= kk[None, :], 0.0, NEG).astype(np.float32)
    for h in range(8):
        cb[h, CB_SEL + h * 128:CB_SEL + (h + 1) * 128] = -1.0
    return cf, cb


_NC_CACHE = {}


def kernel(x, norm1_g, w_in, b_forget, w_pool, pool_scale, w_out, norm2_g, w_gate, w_up, w_down, final_g):
    x = np.asarray(x, np.float32)
    B = x.shape[0]
    cf, cb = _consts(np.asarray(norm1_g)[0], np.asarray(norm2_g)[0], np.asarray(final_g), np.asarray(pool_scale)[0],
                     np.asarray(b_forget)[0])
    shared = {
        "w_in": np.ascontiguousarray(np.asarray(w_in, np.float32)[0]),
        "w_out": np.ascontiguousarray(np.asarray(w_out, np.float32)[0]),
        "w_pool": np.ascontiguousarray(np.asarray(w_pool, np.float32)[0]),
        "w_gate": np.ascontiguousarray(np.asarray(w_gate, np.float32)[0]),
        "w_up": np.ascontiguousarray(np.asarray(w_up, np.float32)[0]),
        "w_down": np.ascontiguousarray(np.asarray(w_down, np.float32)[0]),
        "cf": cf,
        "cb": cb,
    }
    in_maps = [dict(shared, xT=np.ascontiguousarray(x[b].T)) for b in range(B)]
    if "nc" not in _NC_CACHE:
        _NC_CACHE["nc"] = build_program()
    nc = _NC_CACHE["nc"]
    res = run_bass_kernel_spmd(nc, in_maps, core_ids=list(range(B)))
    out = np.stack([np.asarray(res.results[b]["yT"]).T for b in range(B)])
    return np.ascontiguousarray(out.astype(np.float32))
```
